# Optimizing a Trainium2 kernel written in Bass

```python
import math
import jax
import jax.numpy as jnp
from jax import lax
import numpy as np

D_MODEL = 1024
BATCH = 4
SEQ = 4096
DEPTH = 4
DEC_BATCH = 128
DEC_SEQ = 8
PAST_LEN = 8192
PAGE_SIZE = 128

F32 = jnp.float32
N_EVEN = (DEPTH + 1) // 2
N_ODD = DEPTH // 2
PLE_DIM = 256
NORM_EPS = 1e-6
MASK_NEG = -1e30
F_FLOOR = 1e-30
DEEPNORM_ALPHA = (2 * DEPTH) ** 0.25
DEEPNORM_BETA = (8 * DEPTH) ** -0.25

A_HEADS = 4
A_DK = 128
A_DV = 128
A_CONV = 4
A_CHUNK = 64
A_QK = A_HEADS * A_DK
A_VW = A_HEADS * A_DV
A_CONV_CH = 2 * A_QK + A_VW

B_HEADS = 8
B_KV_HEADS = 2
B_HD = 64
B_GROUP = B_HEADS // B_KV_HEADS
B_QW = B_HEADS * B_HD
B_KVW = B_KV_HEADS * B_HD
WINDOW = 128
ROT_DIM = B_HD // 4
ROPE_THETA = 500000.0

OFF_A_GATE = A_CONV_CH
OFF_A_DECAY = OFF_A_GATE + A_VW
OFF_A_BETA = OFF_A_DECAY + A_HEADS
OFF_B_Q = OFF_A_BETA + A_HEADS
OFF_B_K = OFF_B_Q + B_QW
OFF_B_V = OFF_B_K + B_KVW
OFF_B_GATE = OFF_B_V + B_KVW
EVEN_IN = OFF_B_GATE + B_QW
EVEN_SPLITS = (OFF_A_GATE, OFF_A_DECAY, OFF_A_BETA, OFF_B_Q, OFF_B_K, OFF_B_V, OFF_B_GATE)

C_HEADS = 8
C_DK = D_MODEL // C_HEADS
C_DV = D_MODEL // C_HEADS
C_CHUNK = 32
ODD_IN = 4 * D_MODEL

kernel_name = 'hybrid_deltanet_swa_hgrn2_step'


def layer_norm(x, g, b):
    xf = x.astype(F32)
    mu = jnp.mean(xf, -1, keepdims=True)
    xc = xf - mu
    var = jnp.mean(xc * xc, -1, keepdims=True)
    return (xc * lax.rsqrt(var + NORM_EPS) * g.astype(F32) + b.astype(F32)).astype(x.dtype)


def rms_norm(x, g):
    xf = x.astype(F32)
    return (xf * lax.rsqrt(jnp.mean(xf * xf, -1, keepdims=True) + NORM_EPS) * g.astype(F32)).astype(x.dtype)


def l2_normalize(t):
    tf = t.astype(F32)
    return (tf * lax.rsqrt(jnp.sum(tf * tf, -1, keepdims=True) + NORM_EPS)).astype(t.dtype)


def masked_exp(mask, d):
    return jnp.where(mask, jnp.exp(jnp.where(mask, d, 0.0)), 0.0)


def causal_conv(x, buf, w):
    l = x.shape[1]
    xp = jnp.concatenate([buf.astype(x.dtype), x], axis=1)
    y = sum(xp[:, j:j + l] * w[j] for j in range(A_CONV))
    return jax.nn.silu(y), xp[:, xp.shape[1] - (A_CONV - 1):]


def rotary(x, pos):
    inv = ROPE_THETA ** (-jnp.arange(0, ROT_DIM, 2, dtype=F32) / ROT_DIM)
    ang = pos.astype(F32)[:, None] * inv[None, :]
    cos = jnp.cos(ang)[None, :, None, :]
    sin = jnp.sin(ang)[None, :, None, :]
    xf = x.astype(F32)
    x1 = xf[..., :ROT_DIM // 2]
    x2 = xf[..., ROT_DIM // 2:ROT_DIM]
    rot = jnp.concatenate([x1 * cos - x2 * sin, x2 * cos + x1 * sin], -1).astype(x.dtype)
    return jnp.concatenate([rot, x[..., ROT_DIM:]], -1)


def pad_chunks(t, c, nc, l):
    t = jnp.pad(t, [(0, 0), (0, nc * c - l)] + [(0, 0)] * (t.ndim - 2))
    t = t.reshape((t.shape[0], nc, c) + t.shape[2:])
    return jnp.moveaxis(t, 3, 1)


def unchunk(o, l):
    nc, n, h, c, dv = o.shape
    return o.transpose(1, 0, 3, 2, 4).reshape(n, nc * c, h, dv)[:, :l]


def gated_delta_rule(q, k, v, g, beta, s0):
    n, l, h, dk = q.shape
    out_dtype = v.dtype
    c = min(A_CHUNK, l)
    nc = -(-l // c)
    q, k, v, g, beta = (pad_chunks(t.astype(F32), c, nc, l) for t in (q, k, v, g, beta))
    q = q * dk ** -0.5
    G = jnp.cumsum(g, axis=-1)
    idx = jnp.arange(c)
    causal = idx[:, None] >= idx[None, :]
    strict = idx[:, None] > idx[None, :]
    decay = masked_exp(causal, G[..., :, None] - G[..., None, :])
    kb = k * beta[..., None]
    a_mat = jnp.where(strict, jnp.einsum('nhcid,nhcjd->nhcij', kb, k) * decay, 0.0)
    eye = jnp.eye(c, dtype=F32)
    t_inv = lax.linalg.triangular_solve(a_mat + eye, jnp.broadcast_to(eye, a_mat.shape), left_side=True, lower=True)
    eG = jnp.exp(G)
    u = jnp.einsum('nhcij,nhcje->nhcie', t_inv, v * beta[..., None])
    w = jnp.einsum('nhcij,nhcjd->nhcid', t_inv, kb * eG[..., None])
    qk = jnp.einsum('nhcid,nhcjd->nhcij', q, k) * decay
    q_dec = q * eG[..., None]
    k_dec = k * jnp.exp(G[..., -1:] - G)[..., None]
    g_last = eG[..., -1]

    def step(s, xs):
        qk_c, q_c, k_c, u_c, w_c, gl = xs
        v_new = u_c - jnp.einsum('nhid,nhde->nhie', w_c, s)
        o = jnp.einsum('nhid,nhde->nhie', q_c, s) + jnp.einsum('nhij,nhje->nhie', qk_c, v_new)
        s = s * gl[..., None, None] + jnp.einsum('nhid,nhie->nhde', k_c, v_new)
        return s, o

    xs = tuple(jnp.moveaxis(t, 2, 0) for t in (qk, q_dec, k_dec, u, w, g_last))
    s, o = lax.scan(step, s0.astype(F32), xs)
    return unchunk(o, l).astype(out_dtype), s.astype(s0.dtype)


def hgrn2_recurrence(q, k, v, logf, s0):
    n, l, h, dk = q.shape
    out_dtype = v.dtype
    c = min(C_CHUNK, l)
    nc = -(-l // c)
    q, k, v, logf = (pad_chunks(t.astype(F32), c, nc, l) for t in (q, k, v, logf))
    q = q * dk ** -0.5
    G = jnp.cumsum(logf, axis=-2)
    idx = jnp.arange(c)
    causal = (idx[:, None] >= idx[None, :])[..., None]

    def step(s, xs):
        q_c, k_c, v_c, G_c = xs
        dec = masked_exp(causal, G_c[:, :, :, None, :] - G_c[:, :, None, :, :])
        a = jnp.einsum('nhid,nhijd,nhjd->nhij', q_c, dec, k_c)
        o = jnp.einsum('nhid,nhde->nhie', q_c * jnp.exp(G_c), s) + jnp.einsum('nhij,nhje->nhie', a, v_c)
        gl = G_c[:, :, -1]
        s = s * jnp.exp(gl)[..., None] + jnp.einsum('nhid,nhie->nhde', k_c * jnp.exp(gl[:, :, None] - G_c), v_c)
        return s, o

    xs = tuple(jnp.moveaxis(t, 2, 0) for t in (q, k, v, G))
    s, o = lax.scan(step, s0.astype(F32), xs)
    return unchunk(o, l).astype(out_dtype), s.astype(s0.dtype)


def sink_attention(q, k, v, mask, sinks):
    s = jnp.einsum('...qhgd,...khd->...hgqk', q, k).astype(F32) * B_HD ** -0.5
    s = jnp.where(mask, s, MASK_NEG)
    sink = sinks.astype(F32).reshape(B_KV_HEADS, B_GROUP, 1, 1)
    m = jnp.maximum(jnp.max(s, -1, keepdims=True), sink)
    p = jnp.where(mask, jnp.exp(s - m), 0.0)
    probs = p / (jnp.sum(p, -1, keepdims=True) + jnp.exp(sink - m))
    return jnp.einsum('...hgqk,...khd->...qhgd', probs.astype(v.dtype), v)


def swa_prompt(q, k, v, sinks):
    n, l = q.shape[:2]
    nb = l // WINDOW
    qb = q.reshape(n, nb, WINDOW, B_KV_HEADS, B_GROUP, B_HD)
    kb = k.reshape(n, nb, WINDOW, B_KV_HEADS, B_HD)
    vb = v.reshape(n, nb, WINDOW, B_KV_HEADS, B_HD)
    shift = lambda t: jnp.concatenate([jnp.zeros_like(t[:, :1]), t[:, :-1]], axis=1)
    kx = jnp.concatenate([shift(kb), kb], axis=2)
    vx = jnp.concatenate([shift(vb), vb], axis=2)
    qi = jnp.arange(WINDOW)[:, None] + WINDOW
    kj = jnp.arange(2 * WINDOW)[None, :]
    rel = qi - kj
    band = (rel >= 0) & (rel <= WINDOW)
    valid = (jnp.arange(nb)[:, None, None] > 0) | (kj >= WINDOW)[None]
    mask = (band[None] & valid)[:, None, None]
    return sink_attention(qb, kx, vx, mask, sinks).reshape(n, l, B_QW)


def swa_sample(q, k, v, k_buf, v_buf, sinks):
    n, l = q.shape[:2]
    kx = jnp.concatenate([k_buf.astype(k.dtype), k], axis=1)
    vx = jnp.concatenate([v_buf.astype(v.dtype), v], axis=1)
    qi = jnp.arange(l)[:, None] + WINDOW
    kj = jnp.arange(WINDOW + l)[None, :]
    rel = qi - kj
    mask = (rel >= 0) & (rel <= WINDOW)
    o = sink_attention(q.reshape(n, l, B_KV_HEADS, B_GROUP, B_HD), kx, vx, mask, sinks)
    return o.reshape(n, l, B_QW), kx[:, l:], vx[:, l:]


def even_mixer(x, pos, conv_buf, s0, win_k, win_v, w_in, conv_w, a_log, dt_bias, norm_g, sinks, w_out):
    n, l, _ = x.shape
    qkv_a, gate_a, a_in, b_in, q_b, k_b, v_b, gate_b = jnp.split(x @ w_in, EVEN_SPLITS, axis=-1)
    qkv_a, new_conv = causal_conv(qkv_a, conv_buf, conv_w)
    q_a, k_a, v_a = jnp.split(qkv_a, (A_QK, 2 * A_QK), axis=-1)
    q_a = l2_normalize(q_a.reshape(n, l, A_HEADS, A_DK))
    k_a = l2_normalize(k_a.reshape(n, l, A_HEADS, A_DK))
    v_a = v_a.reshape(n, l, A_HEADS, A_DV)
    log_decay = -jnp.exp(a_log.astype(F32)) * jax.nn.softplus(a_in.astype(F32) + dt_bias.astype(F32))
    beta = jax.nn.sigmoid(b_in.astype(F32))
    o_a, new_s = gated_delta_rule(q_a, k_a, v_a, log_decay, beta, s0)
    o_a = rms_norm(o_a, norm_g).reshape(n, l, A_VW) * jax.nn.silu(gate_a)
    q_b = rotary(q_b.reshape(n, l, B_HEADS, B_HD), pos)
    k_b = rotary(k_b.reshape(n, l, B_KV_HEADS, B_HD), pos)
    v_b = v_b.reshape(n, l, B_KV_HEADS, B_HD)
    if win_k is None:
        o_b = swa_prompt(q_b, k_b, v_b, sinks)
        new_k, new_v = k_b[:, l - WINDOW:], v_b[:, l - WINDOW:]
    else:
        o_b, new_k, new_v = swa_sample(q_b, k_b, v_b, win_k, win_v, sinks)
    o_b = o_b * jax.nn.silu(gate_b)
    y = jnp.concatenate([o_a, o_b], axis=-1) @ w_out
    return y, new_conv, new_s, new_k, new_v


def odd_mixer(x, s0, lb, w_in, norm_g, w_out):
    n, l, _ = x.shape
    q, f, i, gate = jnp.split(x @ w_in, 4, axis=-1)
    q = jax.nn.silu(q).reshape(n, l, C_HEADS, C_DK)
    zf = f.astype(F32)
    fg = lb + (1.0 - lb) * jax.nn.sigmoid(zf)
    logf = jnp.log(jnp.maximum(fg, F_FLOOR)).reshape(n, l, C_HEADS, C_DK)
    k = ((1.0 - lb) * jax.nn.sigmoid(-zf)).reshape(n, l, C_HEADS, C_DK)
    v = i.reshape(n, l, C_HEADS, C_DV)
    o, new_s = hgrn2_recurrence(q, k, v, logf, s0)
    o = rms_norm(o, norm_g).reshape(n, l, D_MODEL) * jax.nn.silu(gate)
    return o @ w_out, new_s


def post_layer(x, y, p, g, b, w_ple_proj, w_ple_gate):
    h = layer_norm(DEEPNORM_ALPHA * x + y, g, b)
    return h + jax.nn.sigmoid(h @ w_ple_gate) * (p @ w_ple_proj)


def setup_inputs(seed: int = 0) -> dict:
    key = jax.random.key(seed)
    ks = jax.random.split(key, 32)
    nrm = lambda k, shape, scale=1.0: scale * jax.random.normal(k, shape, dtype=F32)
    a_log = jnp.log(jax.random.uniform(ks[11], (N_EVEN, A_HEADS), F32, 1.0, 16.0))
    dt = jnp.exp(jax.random.uniform(ks[12], (N_EVEN, A_HEADS), F32, math.log(1e-3), math.log(1e-1)))
    dt_bias = dt + jnp.log(-jnp.expm1(-dt))
    return {
        'x_prompt': nrm(ks[0], (BATCH, SEQ, D_MODEL)),
        'x_sample': nrm(ks[1], (DEC_BATCH, DEC_SEQ, D_MODEL)),
        'state_conv_a': nrm(ks[2], (N_EVEN, DEC_BATCH, A_CONV - 1, A_CONV_CH)),
        'state_delta_a': nrm(ks[3], (N_EVEN, DEC_BATCH, A_HEADS, A_DK, A_DV), 0.3),
        'cache_win_k': nrm(ks[4], (N_EVEN, DEC_BATCH, WINDOW, B_KV_HEADS, B_HD)),
        'cache_win_v': nrm(ks[5], (N_EVEN, DEC_BATCH, WINDOW, B_KV_HEADS, B_HD)),
        'state_hgrn_c': nrm(ks[6], (N_ODD, DEC_BATCH, C_HEADS, C_DK, C_DV), 0.3),
        'p_prompt': nrm(ks[7], (DEPTH, BATCH, SEQ, PLE_DIM)),
        'p_sample': nrm(ks[8], (DEPTH, DEC_BATCH, DEC_SEQ, PLE_DIM)),
        'w_in_even': nrm(ks[9], (N_EVEN, D_MODEL, EVEN_IN), D_MODEL ** -0.5),
        'conv_w_a': nrm(ks[10], (N_EVEN, A_CONV, A_CONV_CH), A_CONV ** -0.5),
        'a_log': a_log,
        'dt_bias': dt_bias,
        'norm_a': 1.0 + nrm(ks[13], (N_EVEN, A_DV), 0.02),
        'sinks_b': nrm(ks[14], (N_EVEN, B_HEADS)),
        'w_out_even': nrm(ks[15], (N_EVEN, D_MODEL, D_MODEL), DEEPNORM_BETA * D_MODEL ** -0.5),
        'w_in_odd': nrm(ks[16], (N_ODD, D_MODEL, ODD_IN), D_MODEL ** -0.5),
        'lb_raw': 1.0 + nrm(ks[17], (N_ODD, D_MODEL), 0.1),
        'norm_c': 1.0 + nrm(ks[18], (N_ODD, C_DV), 0.02),
        'w_out_odd': nrm(ks[19], (N_ODD, D_MODEL, D_MODEL), DEEPNORM_BETA * D_MODEL ** -0.5),
        'ln_g': 1.0 + nrm(ks[20], (DEPTH, D_MODEL), 0.02),
        'ln_b': nrm(ks[21], (DEPTH, D_MODEL), 0.02),
        'w_ple_proj': nrm(ks[22], (DEPTH, PLE_DIM, D_MODEL), PLE_DIM ** -0.5),
        'w_ple_gate': nrm(ks[23], (DEPTH, D_MODEL, D_MODEL), D_MODEL ** -0.5),
    }


def reference(x_prompt, x_sample, state_conv_a, state_delta_a, cache_win_k, cache_win_v, state_hgrn_c,
              p_prompt, p_sample, w_in_even, conv_w_a, a_log, dt_bias, norm_a, sinks_b, w_out_even,
              w_in_odd, lb_raw, norm_c, w_out_odd, ln_g, ln_b, w_ple_proj, w_ple_gate):
    n_p, l_p = x_prompt.shape[:2]
    l_s = x_sample.shape[1]
    pos_p = jnp.arange(l_p)
    pos_s = PAST_LEN + jnp.arange(l_s)
    lb_sm = jax.nn.softmax(lb_raw.astype(F32), axis=0)
    lb_all = jnp.cumsum(lb_sm, axis=0) - lb_sm[0]
    conv_p, conv_s, delta_p, delta_s = [], [], [], []
    wk_p, wk_s, wv_p, wv_s, hg_p, hg_s = [], [], [], [], [], []
    hp, hs = x_prompt, x_sample
    for layer in range(DEPTH):
        j = layer // 2
        if layer % 2 == 0:
            wts = (w_in_even[j], conv_w_a[j], a_log[j], dt_bias[j], norm_a[j], sinks_b[j], w_out_even[j])
            conv0 = jnp.zeros((n_p, A_CONV - 1, A_CONV_CH), hp.dtype)
            s0 = jnp.zeros((n_p, A_HEADS, A_DK, A_DV), hp.dtype)
            yp, c1, s1, k1, v1 = even_mixer(hp, pos_p, conv0, s0, None, None, *wts)
            ys, c2, s2, k2, v2 = even_mixer(hs, pos_s, state_conv_a[j], state_delta_a[j],
                                            cache_win_k[j], cache_win_v[j], *wts)
            conv_p.append(c1); conv_s.append(c2)
            delta_p.append(s1); delta_s.append(s2)
            wk_p.append(k1); wk_s.append(k2)
            wv_p.append(v1); wv_s.append(v2)
        else:
            s0 = jnp.zeros((n_p, C_HEADS, C_DK, C_DV), hp.dtype)
            yp, s1 = odd_mixer(hp, s0, lb_all[j], w_in_odd[j], norm_c[j], w_out_odd[j])
            ys, s2 = odd_mixer(hs, state_hgrn_c[j], lb_all[j], w_in_odd[j], norm_c[j], w_out_odd[j])
            hg_p.append(s1); hg_s.append(s2)
        hp = post_layer(hp, yp, p_prompt[layer], ln_g[layer], ln_b[layer], w_ple_proj[layer], w_ple_gate[layer])
        hs = post_layer(hs, ys, p_sample[layer], ln_g[layer], ln_b[layer], w_ple_proj[layer], w_ple_gate[layer])
    return (hp, hs, jnp.stack(conv_p), jnp.stack(conv_s), jnp.stack(delta_p), jnp.stack(delta_s),
            jnp.stack(wk_p), jnp.stack(wk_s), jnp.stack(wv_p), jnp.stack(wv_s), jnp.stack(hg_p), jnp.stack(hg_s))
```

```python
import numpy as np
from contextlib import ExitStack
import concourse.bass as bass
import concourse.mybir as mybir
from concourse.bass_utils import run_bass_kernel_spmd

F32 = mybir.dt.float32
BF16 = mybir.dt.bfloat16
AF = mybir.ActivationFunctionType
ALU = mybir.AluOpType
AX = mybir.AxisListType

D = 1024
EVEN_IN = 3336
ODD_IN = 4096
PLE = 256
ALPHA = float(8 ** 0.25)
EPS = 1e-6
NEG = -30000.0
PAST_LEN = 8192
ROPE_THETA = 500000.0
NSEQ = 16
SEQUENTIAL = 0
EP_PATTERN_EVEN = "1" + "00" + "1111" + "0" + "11" + "00" + "11" + "0" + "11111"
EP_PATTERN_ODD = "1" + "00" + "111" + "0" + "11" + "00" + "11" + "0" + "11111"


class Buf:
    def __init__(self, name):
        self.name = name
        self.w = None
        self.rs = {}
        self.dsem = None
        self.dcnt = 0
        self.excl = False


class Kern:
    def __init__(self, nc, es):
        self.nc = nc
        self.es = es
        self.eng = {'p': nc.tensor, 'v': nc.vector, 'a': nc.scalar, 'g': nc.gpsimd, 's': nc.sync}
        self.esem = {k: es.enter_context(nc.semaphore("es_" + k)) for k in self.eng}
        self.ecnt = {k: 0 for k in self.eng}
        self.seen = {k: {} for k in self.eng}
        self.alltoks = {}

    def _wait(self, e, tok):
        sem, val = tok
        if e == 'p' and sem is self.esem['p']:
            return
        if self.seen[e].get(sem.name, 0) >= val:
            return
        self.eng[e].wait_ge(sem, val)
        self.seen[e][sem.name] = val

    def _deps(self, e, r, w):
        for b in r:
            if b.w is not None:
                self._wait(e, b.w)
            if b.excl:
                for tok in list(b.rs.values()):
                    if tok[0] is not self.esem[e]:
                        self._wait(e, tok)
        for b in w:
            if b.w is not None:
                self._wait(e, b.w)
            for tok in list(b.rs.values()):
                self._wait(e, tok)

    def _upd(self, tok, r, w):
        for b in r:
            old = b.rs.get(tok[0].name)
            if old is None or old[1] < tok[1]:
                b.rs[tok[0].name] = tok
        for b in w:
            b.w = tok
            b.rs = {}

    def op(self, e, fn, r=(), w=()):
        self._deps(e, r, w)
        inst = fn(self.eng[e])
        self.ecnt[e] += 1
        inst.then_inc(self.esem[e], 1)
        self._upd((self.esem[e], self.ecnt[e]), r, w)

    def dma(self, q, out, in_, dbuf, r=(), w=(), **kw):
        self._deps(q, r, w)
        if dbuf.dsem is None:
            dbuf.dsem = self.es.enter_context(self.nc.semaphore("ds_" + dbuf.name))
        inst = self.eng[q].dma_start(out=out, in_=in_, **kw)
        dbuf.dcnt += 1
        inst.then_inc(dbuf.dsem, 16)
        tok = (dbuf.dsem, 16 * dbuf.dcnt)
        self.alltoks[dbuf.dsem.name] = tok
        self._upd(tok, r, w)

    def finish_all(self, e):
        for tok in self.alltoks.values():
            self._wait(e, tok)

    def barrier(self):
        toks = [(self.esem[e], self.ecnt[e]) for e in self.eng if self.ecnt[e] > 0] + list(self.alltoks.values())
        for e in self.eng:
            for tok in toks:
                self._wait(e, tok)


def make_consts(NT):
    c = {}
    i = np.arange(128)
    blk = i // 32
    same = blk[:, None] == blk[None, :]
    c["ident"] = np.eye(128, dtype=np.float32)
    c["maskC"] = (same & (i[:, None] <= i[None, :])).astype(np.float32)
    c["maskS"] = (same & (i[:, None] < i[None, :])).astype(np.float32)
    c["bones"] = same.astype(np.float32)
    c["mnegC"] = np.where(c["maskC"] > 0, 0.0, NEG).astype(np.float32)
    c["ones"] = np.ones((128, 128), np.float32)
    c["ind"] = (blk[:, None] == np.arange(4)[None, :]).astype(np.float32)
    valid = np.ones((128, 2), np.float32)
    valid[:, 1] = ((i % 32) < 8).astype(np.float32)
    c["valid"] = valid
    M = np.full((3, 128, 256), -1e30, np.float32)
    jj = np.arange(128)
    prev_ok = jj[None, :] >= i[:, None]
    cur_ok = jj[None, :] <= i[:, None]
    M[0, :, :128][prev_ok] = 0.0
    M[0, :, 128:][cur_ok] = 0.0
    M[1, :, 128:][cur_ok] = 0.0
    l = i % 32
    sprev = jj[None, :] >= l[:, None]
    scur = same & ((jj % 32)[None, :] <= l[:, None]) & ((jj % 32)[None, :] < 8)
    M[2, :, :128][sprev] = 0.0
    M[2, :, 128:][scur] = 0.0
    c["swam"] = M
    inv = ROPE_THETA ** (-np.arange(0, 16, 2, dtype=np.float32) / np.float32(16))
    R = np.zeros((NT + 1, 128, 32), np.float32)
    for t in range(NT + 1):
        pos = (128 * t + i) if t < NT else (PAST_LEN + (i % 32))
        ang = pos.astype(np.float32)[:, None] * inv[None, :].astype(np.float32)
        cs = np.cos(ang).astype(np.float32)
        sn = np.sin(ang).astype(np.float32)
        R[t, :, 0:8] = cs
        R[t, :, 8:16] = cs
        R[t, :, 16:24] = -sn
        R[t, :, 24:32] = sn
    c["rope"] = R
    return c


def build(NT, NL=4):
    L = NT * 128
    nc = bass.Bass("TRN2", target_bir_lowering=False)
    es = ExitStack()

    def din(name, shape):
        return nc.dram_tensor(name, list(shape), F32, kind="ExternalInput").ap()

    def dout(name, shape):
        return nc.dram_tensor(name, list(shape), F32, kind="ExternalOutput").ap()

    NE = (NL + 1) // 2
    NO = NL // 2
    xp = din("xp", [L, D]); ppd = din("pp", [NL, L, PLE])
    xs = din("xs", [NSEQ, 8, D]); psd = din("ps", [NL, NSEQ, 8, PLE])
    conv_s = din("conv_s", [NE, NSEQ, 3, 1536]); delta_s = din("delta_s", [NE, NSEQ, 4, 128, 128])
    wk_s = din("wk_s", [NE, NSEQ, 128, 128]); wv_s = din("wv_s", [NE, NSEQ, 128, 128])
    hg_s = din("hg_s", [max(NO, 1), NSEQ, 8, 128, 128])
    w_in_even = din("w_in_even", [NE, D, EVEN_IN]); conv_w = din("conv_w", [NE, 4, 1536])
    a_log = din("a_log", [NE, 4]); dt_bias = din("dt_bias", [NE, 4]); norm_a = din("norm_a", [NE, 128])
    sinks = din("sinks", [NE, 8]); w_out_even = din("w_out_even", [NE, D, D])
    w_in_odd = din("w_in_odd", [max(NO, 1), D, ODD_IN]); lb_raw = din("lb_raw", [2, D])
    norm_c = din("norm_c", [max(NO, 1), 128]); w_out_odd = din("w_out_odd", [max(NO, 1), D, D])
    ln_g = din("ln_g", [NL, D]); ln_b = din("ln_b", [NL, D])
    w_pproj = din("w_pproj", [NL, PLE, D]); w_pgate = din("w_pgate", [NL, D, D])
    c_ident = din("c_ident", [128, 128]); c_maskC = din("c_maskC", [128, 128]); c_maskS = din("c_maskS", [128, 128])
    c_bones = din("c_bones", [128, 128]); c_mnegC = din("c_mnegC", [128, 128]); c_ones = din("c_ones", [128, 128])
    c_ind = din("c_ind", [128, 4]); c_valid = din("c_valid", [128, 2]); c_swam = din("c_swam", [3, 128, 256])
    c_rope = din("c_rope", [NT + 1, 128, 32])

    yp = dout("yp", [L, D]); ys = dout("ys", [NSEQ, 8, D])
    conv_p = dout("conv_p", [NE, 3, 1536]); conv_so = dout("conv_so", [NE, NSEQ, 3, 1536])
    delta_p = dout("delta_p", [NE, 4, 128, 128]); delta_so = dout("delta_so", [NE, NSEQ, 4, 128, 128])
    wk_p = dout("wk_p", [NE, 128, 128]); wk_so = dout("wk_so", [NE, NSEQ, 128, 128])
    wv_p = dout("wv_p", [NE, 128, 128]); wv_so = dout("wv_so", [NE, NSEQ, 128, 128])
    hg_p = dout("hg_p", [max(NO, 1), 8, 128, 128]); hg_so = dout("hg_so", [max(NO, 1), NSEQ, 8, 128, 128])
    hP = [nc.dram_tensor("hP%d" % i, [L, D], F32, kind="Internal").ap() for i in range(2)]
    hS = [nc.dram_tensor("hS%d" % i, [NSEQ, 8, D], F32, kind="Internal").ap() for i in range(2)]
    BhP = [Buf("hP0d"), Buf("hP1d")]
    BhS = [Buf("hS0d"), Buf("hS1d")]

    with es:
        K = Kern(nc, es)
        bufs = {}

        def sb(name, shape, dt=F32):
            t = es.enter_context(nc.sbuf_tensor(name, list(shape), dt))
            bufs[name] = Buf(name)
            return t, bufs[name]

        V = lambda fn, r=(), w=(): K.op('v', fn, r, w)
        A = lambda fn, r=(), w=(): K.op('a', fn, r, w)
        PE = lambda fn, r=(), w=(): K.op('p', fn, r, w)
        G = lambda fn, r=(), w=(): K.op('g', fn, r, w)

        PT = es.enter_context(nc.psum_tensor("PT", [128, 8, 128], BF16)); BPT = Buf("PT")
        PA = es.enter_context(nc.psum_tensor("PA", [128, 7, 512], F32))
        BP = [None] + [Buf("PB%d" % b) for b in range(1, 8)]
        BPT.excl = True
        for b_ in BP[1:]:
            b_.excl = True

        def pb(b):
            return PA[:, b - 1, :]

        def pb4(b):
            return PA[:, b - 1, :].rearrange("p (a c) -> p a c", a=4)

        ident_f, Bidf = sb("ident_f", [128, 128]); ident_b, Bidb = sb("ident_b", [128, 128], BF16)
        maskC, BmC = sb("maskC", [128, 128]); maskS, BmS = sb("maskS", [128, 128])
        bones, Bbo = sb("bones", [128, 128]); mnegC, Bmn = sb("mnegC", [128, 128]); ones_f, Bon = sb("ones_f", [128, 128])
        ind, Bind = sb("ind", [128, 4]); valid, Bval = sb("valid", [128, 2])
        swam, Bswam = sb("swam", [128, 3, 256])
        for t_, b_, d_ in [(ident_f, Bidf, c_ident), (maskC, BmC, c_maskC), (maskS, BmS, c_maskS), (bones, Bbo, c_bones),
                           (mnegC, Bmn, c_mnegC), (ones_f, Bon, c_ones), (ind, Bind, c_ind), (valid, Bval, c_valid)]:
            K.dma('s', t_[:], d_, b_, w=[b_])
        K.dma('s', swam[:], c_swam.rearrange("m p k -> p m k"), Bswam, w=[Bswam])
        V(lambda e: e.tensor_copy(ident_b[:], ident_f[:]), r=[Bidf], w=[Bidb])

        wout, Bwout = sb("wout", [128, 8, D], BF16)
        wgate, Bwgate = sb("wgate", [128, 8, D], BF16)
        wpp, Bwpp = sb("wpp", [128, 2, D], BF16)
        WSTG = 1024
        stg_i = [0]
        gbc, Bgbc = sb("gbc", [128, D]); bbc, Bbbc = sb("bbc", [128, D])

        def load_w(dst, Bdst, src, nrows, ncols):
            for kc in range(nrows // 128):
                for c0 in range(0, ncols, WSTG):
                    cw = min(WSTG, ncols - c0)
                    st, Bst = stg[stg_i[0] % 2]
                    q = 's'
                    K.dma(q, st[:, 0:cw], src[kc * 128:(kc + 1) * 128, c0:c0 + cw], Bst, w=[Bst])
                    if stg_i[0] % 2 == 0:
                        V(lambda e: e.tensor_copy(dst[:, kc, c0:c0 + cw], st[:, 0:cw]), r=[Bst], w=[Bdst])
                    else:
                        A(lambda e: e.copy(dst[:, kc, c0:c0 + cw], st[:, 0:cw]), r=[Bst], w=[Bdst])
                    stg_i[0] += 1

        SL = []
        for i_ in range(2):
            SL.append(dict(xin=sb("xin%d" % i_, [128, D]), pin=sb("pin%d" % i_, [128, PLE]), xb=sb("xb%d" % i_, [128, D], BF16),
                           xT=sb("xT%d" % i_, [128, 8, 128], BF16), pb16=sb("pb16_%d" % i_, [128, PLE], BF16),
                           pT=sb("pT%d" % i_, [128, 2, 128], BF16)))
        stg = [SL[0]["xin"], SL[1]["xin"]]
        zt, Bzt = sb("zt", [128, D])
        st6, Bst6 = sb("st6", [128, 2, 6]); mv, Bmv = sb("mv", [128, 2]); rstd, Brstd = sb("rstd", [128, 1])
        ocb, Bocb = sb("ocb", [128, D], BF16)
        Bd2d = Buf("d2d")
        for i_ in range(2):
            for t_, b_ in [SL[i_]["xin"], SL[i_]["pin"]]:
                V(lambda e: e.memset(t_[:], 0.0), w=[b_])

        def interleave(*gens, seq=False, pattern=None):
            gens = [g for g in gens if g is not None]
            if pattern is not None and not seq:
                alive = [True] * len(gens)
                for ch in pattern:
                    gi = int(ch)
                    if alive[gi]:
                        try:
                            next(gens[gi])
                        except StopIteration:
                            alive[gi] = False
                gens = [g for g, a in zip(gens, alive) if a]
            gens = [g for g in gens if g is not None]
            if seq:
                for g in gens:
                    for _ in g:
                        pass
                return
            while gens:
                for g in list(gens):
                    try:
                        next(g)
                    except StopIteration:
                        gens.remove(g)

        def transposes(n, src_fn, rbufs, dst, Bdst, dst_sl=None):
            for i in range(n):
                PE(lambda e: e.transpose(PT[:, i, :], src_fn(i), ident_b[:]), r=rbufs + [Bidb], w=[BPT])
            if dst_sl is None:
                V(lambda e: e.tensor_copy(dst[:, 0:n, :], PT[:, 0:n, :]), r=[BPT], w=[Bdst])
            else:
                V(lambda e: e.tensor_copy(dst_sl, PT[:, 0:n, :]), r=[BPT], w=[Bdst])

        def load_tile(l, mode, idx, S):
            xin, Bxin = S["xin"]; pin, Bpin = S["pin"]; xb, Bxb = S["xb"]; xT, BxT = S["xT"]; pb16, Bpb16 = S["pb16"]; pT, BpT = S["pT"]
            if mode == 'p':
                src = xp if l == 0 else hP[(l - 1) % 2]
                rb = [] if l == 0 else [BhP[(l - 1) % 2]]
                K.dma('s', xin[:], src[idx * 128:(idx + 1) * 128, :], Bxin, r=rb, w=[Bxin])
                K.dma('s', pin[:], ppd[l, idx * 128:(idx + 1) * 128, :], Bpin, w=[Bpin])
            else:
                src = xs if l == 0 else hS[(l - 1) % 2]
                rb = [] if l == 0 else [BhS[(l - 1) % 2]]
                for q in range(4):
                    K.dma('s', xin[32 * q:32 * q + 8, :], src[4 * idx + q], Bxin, r=rb, w=[Bxin])
                    K.dma('s', pin[32 * q:32 * q + 8, :], psd[l, 4 * idx + q], Bpin, w=[Bpin])
            yield
            A(lambda e: e.copy(xb[:], xin[:]), r=[Bxin], w=[Bxb])
            transposes(8, lambda i: xb[:, i * 128:(i + 1) * 128], [Bxb], xT, BxT)
            yield
            A(lambda e: e.copy(pb16[:], pin[:]), r=[Bpin], w=[Bpb16])
            transposes(2, lambda i: pb16[:, i * 128:(i + 1) * 128], [Bpb16], pT, BpT)
            yield

        def out_proj_post(l, mode, idx, S):
            xin, Bxin = S["xin"]; pT, BpT = S["pT"]
            hnb, Bhnb = S["xb"]; hnT, BhnT = S["xT"]; oT, BoT = S["xT"]
            transposes(8, lambda i: ocb[:, i * 128:(i + 1) * 128], [Bocb], oT, BoT)
            yield
            for n in range(2):
                for kc in range(8):
                    PE(lambda e: e.matmul(pb(1 + n), oT[:, kc, :], wout[:, kc, n * 512:(n + 1) * 512], start=(kc == 0), stop=(kc == 7)),
                       r=[BoT, Bwout], w=[BP[1 + n]])
                V(lambda e: e.scalar_tensor_tensor(zt[:, n * 512:(n + 1) * 512], xin[:, n * 512:(n + 1) * 512], ALPHA, pb(1 + n), ALU.mult, ALU.add),
                  r=[Bxin, BP[1 + n]], w=[Bzt])
            yield
            for n in range(2):
                V(lambda e: e.bn_stats(st6[:, n, :], zt[:, n * 512:(n + 1) * 512]), r=[Bzt], w=[Bst6])
            V(lambda e: e.bn_aggr(mv[:], st6[:]), r=[Bst6], w=[Bmv])
            A(lambda e: e.activation(rstd[:], mv[:, 1:2], AF.Ln, bias=EPS), r=[Bmv], w=[Brstd])
            A(lambda e: e.activation(rstd[:], rstd[:], AF.Exp, scale=-0.5), r=[Brstd], w=[Brstd])
            V(lambda e: e.tensor_scalar(zt[:], zt[:], mv[:, 0:1], rstd[:, 0:1], ALU.subtract, ALU.mult), r=[Bzt, Bmv, Brstd], w=[Bzt])
            V(lambda e: e.tensor_tensor(zt[:], zt[:], gbc[:], ALU.mult), r=[Bzt, Bgbc], w=[Bzt])
            V(lambda e: e.tensor_tensor(zt[:], zt[:], bbc[:], ALU.add), r=[Bzt, Bbbc], w=[Bzt])
            A(lambda e: e.copy(hnb[:], zt[:]), r=[Bzt], w=[Bhnb])
            yield
            transposes(8, lambda i: hnb[:, i * 128:(i + 1) * 128], [Bhnb], hnT, BhnT)
            yield
            for n in range(2):
                for kc in range(8):
                    PE(lambda e: e.matmul(pb(3 + n), hnT[:, kc, :], wgate[:, kc, n * 512:(n + 1) * 512], start=(kc == 0), stop=(kc == 7)),
                       r=[BhnT, Bwgate], w=[BP[3 + n]])
                A(lambda e: e.activation(xin[:, n * 512:(n + 1) * 512], pb(3 + n), AF.Sigmoid), r=[BP[3 + n]], w=[Bxin])
            yield
            for n in range(2):
                for kc in range(2):
                    PE(lambda e: e.matmul(pb(5 + n), pT[:, kc, :], wpp[:, kc, n * 512:(n + 1) * 512], start=(kc == 0), stop=(kc == 1)),
                       r=[BpT, Bwpp], w=[BP[5 + n]])
                V(lambda e: e.tensor_tensor(xin[:, n * 512:(n + 1) * 512], xin[:, n * 512:(n + 1) * 512], pb(5 + n), ALU.mult),
                  r=[Bxin, BP[5 + n]], w=[Bxin])
            V(lambda e: e.tensor_tensor(xin[:], xin[:], zt[:], ALU.add), r=[Bxin, Bzt], w=[Bxin])
            last = (l == NL - 1)
            if mode == 'p':
                dst = yp if last else hP[l % 2]
                wb = [] if last else [BhP[l % 2]]
                K.dma('s', dst[idx * 128:(idx + 1) * 128, :], xin[:], Bxin, r=[Bxin], w=wb)
            else:
                dst = ys if last else hS[l % 2]
                wb = [] if last else [BhS[l % 2]]
                for q in range(4):
                    K.dma('s', dst[4 * idx + q], xin[32 * q:32 * q + 8, :], Bxin, r=[Bxin], w=wb)
            yield

        def make_even(sbl):
          win, Bwin = sbl("win", [128, 8, EVEN_IN], BF16)
          cv, Bcv = sbl("cv", [128, 12, 132], BF16)
          cvo, Bcvo = sbl("cvo", [128, 12, 4, 3]); cst, Bcst = sbl("cst", [128, 12, 4, 3])
          diagw, Bdiagw = sbl("diagw", [128, 12, 4, 128], BF16); cwt, Bcwt = sbl("cwt", [128, 12, 4])
          gna, Bgna = sbl("gna", [128, 512]); sgb, Bsgb = sbl("sgb", [128, 512])
          qkf, Bqkf = sbl("qkf", [128, 10, 64]); vf, Bvf = sbl("vf", [128, 128]); ab, Bab = sbl("ab", [128, 8])
          rt1, Brt1 = sbl("rt1", [128, 10, 16]); rt2, Brt2 = sbl("rt2", [128, 10, 16]); rope, Brope = sbl("rope", [128, 32])
          qkT, BqkT = sbl("qkT", [128, 8, 128])
          rn, Brn = sbl("rn", [128, 8, 128]); sqt, Bsqt = rn, Brn
          qkn, Bqkn = sbl("qkn", [128, 8, 128], BF16); vTb, BvTb = sbl("vTb", [128, 4, 128], BF16)
          vtok, Bvtok = sbl("vtok", [128, 4, 128], BF16); keg, Bkeg = sbl("keg", [128, 4, 128], BF16)
          kdec, Bkdec = sbl("kdec", [128, 4, 128], BF16)
          sc4 = {}
          for nm in ["zz", "gg", "beta", "nbeta", "Gs", "eG", "ekd", "osc", "tmp4", "ss4", "rs4", "mx4", "negm", "rsum", "es4", "den4"]:
              sc4[nm] = sbl("s4_" + nm, [128, 4])
          gq, Bgq = sbl("gq", [128, 4, 4]); eglb, Beglb = sbl("eglb", [128, 4, 4])
          negA, BnegA = sbl("negA", [128, 4]); dtb, Bdtb = sbl("dtb", [128, 4]); nabc, Bnabc = sbl("nabc", [128, 128])
          sinkbc, Bsinkbc = sbl("sinkbc", [128, 8]); rden, Brden = sbl("rden", [128, 8])
          gU, BgU = sbl("gU", [128, 4, 128]); ngU, BngU = sbl("ngU", [128, 4, 128])
          Ef, BEf = sbl("Ef", [128, 4, 128]); Es, BEs = sbl("Es", [128, 4, 128])
          big1, Bbig1 = sbl("big1", [128, 4, 256])
          Xm, BXm = big1[:, :, 0:128], Bbig1
          XTm, BXTm = big1[:, :, 128:256], Bbig1
          sm, Bsm = rn[:].rearrange("p a c -> p (a c)").rearrange("p (a c) -> p a c", a=4), Brn
          Pm, BPm = sbl("Pm", [128, 4, 128])
          Pb, BPb = sbl("Pb", [128, 4, 128], BF16); QKT, BQKT = sbl("QKT", [128, 4, 128], BF16)
          ub, Bub = sbl("ub", [128, 4, 128]); wT, BwT = sbl("wT", [128, 4, 128], BF16)
          vnew, Bvnew = sbl("vnew", [128, 4, 128], BF16)
          tq, Btq = gU, BgU
          o2, Bo2 = ngU, BngU
          oA, BoA = Ef, BEf
          tS, BtS = Pm, BPm
          SfAs = [sbl("SfA%d" % i, [128, 4, 128]) for i in range(2)]; SbAs = [sbl("SbA%d" % i, [128, 4, 128], BF16) for i in range(2)]
          qb16, Bqb16 = sbl("qb16", [128, 4, 2, 64], BF16)
          kcur = [sbl("kcur0", [128, 128], BF16)]
          kTs = [sbl("kTs%d" % i, [128, 128], BF16) for i in range(2)]
          v16 = [sbl("v16_%d" % i, [128, 128], BF16) for i in range(2)]
          qT, BqT = sbl("qT", [128, 4, 128], BF16)
          ztv = zt[:].rearrange("p (a c) -> p a c", a=8)
          kcf, Bkcf = ztv[:, 0:4, :], Bzt
          vcf, Bvcf = ztv[:, 4:8, :], Bzt
          kc16, Bkc16 = sbl("kc16", [128, 4, 128], BF16); vc16, Bvc16 = sbl("vc16", [128, 4, 128], BF16)
          kTc, BkTc = sbl("kTc", [128, 4, 128], BF16)
          pr, Bpr = sbl("pr", [128, 4, 256], BF16)
          prT, BprT = sbl("prT", [128, 8, 128], BF16); ob, Bob = sbl("ob", [128, 8, 64])
          for t_, b_ in [(v16[0][0], v16[0][1]), (v16[1][0], v16[1][1]), (kTs[0][0], kTs[0][1]), (kTs[1][0], kTs[1][1])]:
              V(lambda e: e.memset(t_[:], 0.0), w=[b_])

          def setup(j):
                load_w(win, Bwin, w_in_even[j], D, EVEN_IN)
                load_w(wout, Bwout, w_out_even[j], D, D)
                K.dma('s', negA[:], a_log[j:j + 1, :].partition_broadcast(128), BnegA, w=[BnegA])
                A(lambda e: e.activation(negA[:], negA[:], AF.Exp), r=[BnegA], w=[BnegA])
                V(lambda e: e.tensor_scalar_mul(negA[:], negA[:], -1.0), r=[BnegA], w=[BnegA])
                K.dma('s', dtb[:], dt_bias[j:j + 1, :].partition_broadcast(128), Bdtb, w=[Bdtb])
                K.dma('s', nabc[:], norm_a[j:j + 1, :].partition_broadcast(128), Bnabc, w=[Bnabc])
                K.dma('s', sinkbc[:], sinks[j:j + 1, :].partition_broadcast(128), Bsinkbc, w=[Bsinkbc])
                for jt in range(4):
                    K.dma('s', cwt[:, :, jt], conv_w[j, jt].rearrange("(m c) -> c m", c=128), Bcwt, w=[Bcwt], allow_slow_non_contiguous=True)
                for m in range(12):
                    for jt in range(4):
                        V(lambda e: e.tensor_scalar_mul(diagw[:, m, jt, :], ident_f[:], cwt[:, m, jt:jt + 1]), r=[Bidf, Bcwt], w=[Bdiagw])

          def even_tile(l, mode, idx, S, S2):
            j = l // 2
            vcol = 0 if mode == 'p' else 1
            lastp = (mode == 'p' and idx == NT - 1)
            xT, BxT = S["xT"]
            crow = gU[:].rearrange("p a c -> p (a c)")
            crow_o = ngU[:].rearrange("p a c -> p (a c)")

            def Pre():
                yield from load_tile(l, mode, idx, S)

            def Pg():
                K.dma('s', rope[:], c_rope[idx if mode == 'p' else NT], Brope, w=[Brope])
                if mode == 'p':
                    if idx == 0:
                        V(lambda e: e.memset(cv[:, :, 0:3], 0.0), w=[Bcv])
                    else:
                        V(lambda e: e.tensor_copy(cv[:, :, 0:3], cv[:, :, 128:131]), r=[Bcv], w=[Bcv])
                else:
                    for bk in range(3):
                        K.dma('s', crow[0:12, :], conv_s[j, 4 * idx:4 * idx + 4, :, bk * 512:(bk + 1) * 512].rearrange("q r c -> (q r) c"), BgU, w=[BgU])
                        for mm in range(4):
                            m = 4 * bk + mm
                            PE(lambda e: e.matmul(pb(7)[:, m * 12:(m + 1) * 12], crow[0:12, mm * 128:(mm + 1) * 128], ident_f[0:12, 0:12],
                                                  start=True, stop=True), r=[BgU, Bidf], w=[BP[7]])
                    V(lambda e: e.tensor_copy(cst[:].rearrange("p m q r -> p (m q r)"), pb(7)[:, 0:144]), r=[BP[7]], w=[Bcst])
                yield
                qlist = [0] if mode == 'p' else [0, 1, 2, 3]
                for bk in range(3):
                    for mm in range(4):
                        m = 4 * bk + mm
                        for kc in range(8):
                            PE(lambda e: e.matmul(pb4(1 + bk)[:, mm, :], win[:, kc, m * 128:(m + 1) * 128], xT[:, kc, :], start=(kc == 0), stop=(kc == 7)),
                               r=[Bwin, BxT], w=[BP[1 + bk]])
                    A(lambda e: e.copy(cv[:, 4 * bk:4 * bk + 4, 3:131], pb4(1 + bk)), r=[BP[1 + bk]], w=[Bcv])
                    if lastp or mode == 's':
                        V(lambda e: e.tensor_copy(ub[:], pb4(1 + bk)), r=[BP[1 + bk]], w=[Bub])
                        for mm in range(4):
                            for q in qlist:
                                t0_ = 125 if mode == 'p' else 32 * q + 5
                                PE(lambda e: e.matmul(pb(7)[32 * q:32 * q + 3, mm * 128:(mm + 1) * 128], ub[:, mm, t0_:t0_ + 3], ident_f[:],
                                                      start=True, stop=True, tile_position=(0, 32 * q)), r=[Bub, Bidf], w=[BP[7]])
                        V(lambda e: e.tensor_copy(crow_o, pb(7)), r=[BP[7]], w=[BngU])
                        for q in qlist:
                            dst_ = conv_p[j, :, bk * 512:(bk + 1) * 512] if mode == 'p' else conv_so[j, 4 * idx + q, :, bk * 512:(bk + 1) * 512]
                            K.dma('s', dst_, crow_o[32 * q:32 * q + 3, :], BngU, r=[BngU])
                    yield
                if mode == 's':
                    for q in range(4):
                        V(lambda e: e.tensor_copy(cv[:, :, 32 * q:32 * q + 3], cst[:, :, q, :]), r=[Bcst, Bcv], w=[Bcv])
                def tm(b, o0, c0, n):
                    for kc in range(8):
                        PE(lambda e: e.matmul(pb(b)[:, o0:o0 + n], xT[:, kc, :], win[:, kc, c0:c0 + n], start=(kc == 0), stop=(kc == 7)),
                           r=[Bwin, BxT], w=[BP[b]])
                tm(4, 0, 1536, 512)
                A(lambda e: e.activation(gna[:], pb(4), AF.Silu), r=[BP[4]], w=[Bgna])
                V(lambda e: e.tensor_tensor(gna[:].rearrange("p (a c) -> p a c", a=4), gna[:].rearrange("p (a c) -> p a c", a=4),
                                            nabc[:].unsqueeze(1).to_broadcast([128, 4, 128]), ALU.mult), r=[Bgna, Bnabc], w=[Bgna])
                yield
                tm(5, 0, 2056, 512)
                V(lambda e: e.tensor_copy(qkf[:, 0:8, :], pb(5).rearrange("p (a c) -> p a c", a=8)), r=[BP[5]], w=[Bqkf])
                yield
                tm(6, 0, 2568, 256)
                tm(6, 256, 2048, 8)
                V(lambda e: e.tensor_copy(qkf[:, 8:10, :], pb(6)[:, 0:128].rearrange("p (a c) -> p a c", a=2)), r=[BP[6]], w=[Bqkf])
                V(lambda e: e.tensor_copy(vf[:], pb(6)[:, 128:256]), r=[BP[6]], w=[Bvf])
                V(lambda e: e.tensor_copy(ab[:], pb(6)[:, 256:264]), r=[BP[6]], w=[Bab])
                yield
                tm(7, 0, 2824, 512)
                A(lambda e: e.activation(sgb[:], pb(7), AF.Silu), r=[BP[7]], w=[Bsgb])
                yield
                for bk in range(3):
                    for mm in range(4):
                        m = 4 * bk + mm
                        for jt in range(4):
                            PE(lambda e: e.matmul(pb4(1 + bk)[:, mm, :], diagw[:, m, jt, :], cv[:, m, jt:jt + 128], start=(jt == 0), stop=(jt == 3)),
                               r=[Bdiagw, Bcv], w=[BP[1 + bk]])
                    if bk < 2:
                        A(lambda e: e.activation(qkT[:, 4 * bk:4 * bk + 4, :], pb4(1 + bk), AF.Silu), r=[BP[1 + bk]], w=[BqkT])
                    else:
                        A(lambda e: e.activation(vTb[:], pb4(3), AF.Silu), r=[BP[3]], w=[BvTb])
                    yield

            def Dg():
                zz, Bzz = sc4["zz"]; gg, Bgg = sc4["gg"]; beta, Bbeta = sc4["beta"]; nbeta, Bnbeta = sc4["nbeta"]
                Gs, BGs = sc4["Gs"]; eG, BeG = sc4["eG"]; ekd, Bekd = sc4["ekd"]; osc, Bosc = sc4["osc"]; tmp4, Btmp4 = sc4["tmp4"]
                def Da():
                    V(lambda e: e.tensor_tensor(sqt[:], qkT[:], qkT[:], ALU.mult), r=[BqkT], w=[Bsqt])
                    for b in range(2):
                        PE(lambda e: e.matmul(pb(1 + b), ones_f[:], sqt[:, 4 * b:4 * b + 4, :].rearrange("p a c -> p (a c)"), start=True, stop=True),
                           r=[Bon, Bsqt], w=[BP[1 + b]])
                    for b in range(2):
                        A(lambda e: e.activation(rn[:, 4 * b:4 * b + 4, :], pb4(1 + b), AF.Ln, bias=EPS), r=[BP[1 + b]], w=[Brn])
                    A(lambda e: e.activation(rn[:], rn[:], AF.Exp, scale=-0.5), r=[Brn], w=[Brn])
                    V(lambda e: e.tensor_tensor(qkn[:], qkT[:], rn[:], ALU.mult), r=[BqkT, Brn], w=[Bqkn])
                    yield
                    transposes(4, lambda i: vTb[:, i, :], [BvTb], vtok, Bvtok)
                    yield

                def Db():
                    V(lambda e: e.tensor_tensor(zz[:], ab[:, 0:4], dtb[:], ALU.add), r=[Bab, Bdtb], w=[Bzz])
                    A(lambda e: e.activation(zz[:], zz[:], AF.Exp), r=[Bzz], w=[Bzz])
                    A(lambda e: e.activation(zz[:], zz[:], AF.Ln, bias=1.0), r=[Bzz], w=[Bzz])
                    V(lambda e: e.scalar_tensor_tensor(gg[:], zz[:], valid[:, vcol:vcol + 1], negA[:], ALU.mult, ALU.mult), r=[Bzz, Bval, BnegA], w=[Bgg])
                    A(lambda e: e.activation(beta[:], ab[:, 4:8], AF.Sigmoid), r=[Bab], w=[Bbeta])
                    V(lambda e: e.tensor_scalar_mul(beta[:], beta[:], valid[:, vcol:vcol + 1]), r=[Bbeta, Bval], w=[Bbeta])
                    V(lambda e: e.tensor_scalar_mul(nbeta[:], beta[:], -1.0), r=[Bbeta], w=[Bnbeta])
                    V(lambda e: e.tensor_tensor(gq[:], gg[:].unsqueeze(1).to_broadcast([128, 4, 4]), ind[:].unsqueeze(2).to_broadcast([128, 4, 4]), ALU.mult),
                      r=[Bgg, Bind], w=[Bgq])
                    PE(lambda e: e.matmul(pb(6)[:, 0:4], maskC[:], gg[:], start=True, stop=True), r=[BmC, Bgg], w=[BP[6]])
                    PE(lambda e: e.matmul(pb(6)[:, 4:8], bones[:], gg[:], start=True, stop=True), r=[Bbo, Bgg], w=[BP[6]])
                    PE(lambda e: e.matmul(pb(6)[:, 8:24], ones_f[:], gq[:].rearrange("p a c -> p (a c)"), start=True, stop=True), r=[Bon, Bgq], w=[BP[6]])
                    V(lambda e: e.tensor_copy(Gs[:], pb(6)[:, 0:4]), r=[BP[6]], w=[BGs])
                    A(lambda e: e.activation(eG[:], pb(6)[:, 0:4], AF.Exp), r=[BP[6]], w=[BeG])
                    V(lambda e: e.tensor_tensor(tmp4[:], pb(6)[:, 4:8], Gs[:], ALU.subtract), r=[BP[6], BGs], w=[Btmp4])
                    A(lambda e: e.activation(ekd[:], tmp4[:], AF.Exp), r=[Btmp4], w=[Bekd])
                    A(lambda e: e.activation(eglb[:].rearrange("p a c -> p (a c)"), pb(6)[:, 8:24], AF.Exp), r=[BP[6]], w=[Beglb])
                    V(lambda e: e.tensor_scalar_mul(osc[:], eG[:], float(128 ** -0.5)), r=[BeG], w=[Bosc])
                    yield
                    V(lambda e: e.tensor_tensor(gU[:], maskC[:].unsqueeze(1).to_broadcast([128, 4, 128]), gg[:].unsqueeze(2).to_broadcast([128, 4, 128]), ALU.mult),
                      r=[BmC, Bgg], w=[BgU])
                    A(lambda e: e.mul(ngU[:], gU[:], -1.0), r=[BgU], w=[BngU])
                    for h in range(4):
                        PE(lambda e: e.matmul(pb4(7)[:, h, :], ones_f[:], gU[:, h, :], start=True, stop=False), r=[Bon, BgU], w=[BP[7]])
                        PE(lambda e: e.matmul(pb4(7)[:, h, :], ngU[:, h, :], ones_f[:], start=False, stop=False), r=[Bon, BngU], w=[BP[7]])
                        PE(lambda e: e.matmul(pb4(7)[:, h, :], ident_f[:], mnegC[:], start=False, stop=True), r=[Bidf, Bmn], w=[BP[7]])
                    A(lambda e: e.activation(Ef[:], pb4(7), AF.Exp), r=[BP[7]], w=[BEf])
                    V(lambda e: e.tensor_tensor(Es[:], Ef[:], maskS[:].unsqueeze(1).to_broadcast([128, 4, 128]), ALU.mult), r=[BEf, BmS], w=[BEs])
                    yield

                yield
                subs = [Da(), Db()]
                while subs:
                    for g_ in list(subs):
                        try:
                            next(g_)
                        except StopIteration:
                            subs.remove(g_)
                    yield
                for h in range(4):
                    PE(lambda e: e.transpose(PT[:, h, :], qkn[:, 4 + h, :], ident_b[:]), r=[Bqkn, Bidb], w=[BPT])
                V(lambda e: e.tensor_tensor(keg[:], PT[:, 0:4, :], eG[:].unsqueeze(2).to_broadcast([128, 4, 128]), ALU.mult), r=[BPT, BeG], w=[Bkeg])
                V(lambda e: e.tensor_tensor(kdec[:], PT[:, 0:4, :], ekd[:].unsqueeze(2).to_broadcast([128, 4, 128]), ALU.mult), r=[BPT, Bekd], w=[Bkdec])
                yield
                for h in range(4):
                    PE(lambda e: e.matmul(pb4(4)[:, h, :], qkn[:, 4 + h, :], qkn[:, 4 + h, :], start=True, stop=True), r=[Bqkn], w=[BP[4]])
                    PE(lambda e: e.matmul(pb4(5)[:, h, :], qkn[:, 4 + h, :], qkn[:, h, :], start=True, stop=True), r=[Bqkn], w=[BP[5]])
                V(lambda e: e.tensor_tensor(Xm[:], pb4(4), Es[:], ALU.mult), r=[BP[4], BEs], w=[BXm])
                V(lambda e: e.tensor_tensor(Xm[:], Xm[:], beta[:].unsqueeze(2).to_broadcast([128, 4, 128]), ALU.mult), r=[BXm, Bbeta], w=[BXm])
                V(lambda e: e.scalar_tensor_tensor(QKT[:], pb4(5), float(128 ** -0.5), Ef[:], ALU.mult, ALU.mult), r=[BP[5], BEf], w=[BQKT])
                yield
                for h in range(4):
                    PE(lambda e: e.transpose(pb4(4)[:, h, :], Xm[:, h, :], ident_f[:]), r=[BXm, Bidf], w=[BP[4]])
                V(lambda e: e.tensor_copy(XTm[:], pb4(4)), r=[BP[4]], w=[BXTm])
                V(lambda e: e.tensor_tensor(Pm[:], ident_f[:].unsqueeze(1).to_broadcast([128, 4, 128]), Xm[:], ALU.subtract), r=[Bidf, BXm], w=[BPm])
                for lev in range(4):
                    for h in range(4):
                        PE(lambda e: e.matmul(pb4(5)[:, h, :], XTm[:, h, :], Xm[:, h, :], start=True, stop=True), r=[BXm, BXTm], w=[BP[5]])
                        PE(lambda e: e.matmul(pb4(6)[:, h, :], Xm[:, h, :], XTm[:, h, :], start=True, stop=True), r=[BXm, BXTm], w=[BP[6]])
                    yield
                    V(lambda e: e.tensor_copy(Xm[:], pb4(5)), r=[BP[5]], w=[BXm])
                    A(lambda e: e.copy(XTm[:], pb4(6)), r=[BP[6]], w=[BXTm])
                    for h in range(4):
                        PE(lambda e: e.matmul(pb4(4)[:, h, :], XTm[:, h, :], Pm[:, h, :], start=True, stop=True), r=[BXTm, BPm], w=[BP[4]])
                    yield
                    V(lambda e: e.tensor_tensor(Pm[:], Pm[:], pb4(4), ALU.add), r=[BPm, BP[4]], w=[BPm])
                    yield
                A(lambda e: e.copy(Pb[:], Pm[:]), r=[BPm], w=[BPb])
                yield
                for h in range(4):
                    PE(lambda e: e.matmul(pb4(5)[:, h, :], Pb[:, h, :], vtok[:, h, :], start=True, stop=True), r=[BPb, Bvtok], w=[BP[5]])
                    PE(lambda e: e.matmul(pb4(6)[:, h, :], keg[:, h, :], Pb[:, h, :], start=True, stop=True), r=[BPb, Bkeg], w=[BP[6]])
                V(lambda e: e.tensor_tensor(ub[:], pb4(5), beta[:].unsqueeze(2).to_broadcast([128, 4, 128]), ALU.mult), r=[BP[5], Bbeta], w=[Bub])
                A(lambda e: e.copy(wT[:], pb4(6)), r=[BP[6]], w=[BwT])
                yield
                SfA, BSfA = SfAs[0]; SbA, BSbA = SbAs[0]
                if mode == 'p' and idx == 0:
                    V(lambda e: e.memset(SfA[:], 0.0), w=[BSfA])
                    V(lambda e: e.memset(SbA[:], 0.0), w=[BSbA])
                if mode == 's':
                    K.dma('s', SfAs[0][0][:], delta_s[j, 4 * idx].rearrange("h d e -> d h e"), SfAs[0][1], w=[SfAs[0][1]])
                for q in range(4):
                    qs = slice(32 * q, 32 * q + 32)
                    if mode == 's':
                        SfA, BSfA = SfAs[q % 2]; SbA, BSbA = SbAs[q % 2]
                        if q < 3:
                            K.dma('s', SfAs[(q + 1) % 2][0][:], delta_s[j, 4 * idx + q + 1].rearrange("h d e -> d h e"), SfAs[(q + 1) % 2][1],
                                  w=[SfAs[(q + 1) % 2][1]])
                        A(lambda e: e.copy(SbA[:], SfA[:]), r=[BSfA], w=[BSbA])
                    for h in range(4):
                        PE(lambda e: e.matmul(pb4(4)[qs, h, :], wT[:, h, qs], SbA[:, h, :], start=True, stop=True, tile_position=(0, 32 * q)),
                           r=[BwT, BSbA], w=[BP[4]])
                        PE(lambda e: e.matmul(pb4(7)[qs, h, :], qkn[:, h, qs], SbA[:, h, :], start=True, stop=True, tile_position=(0, 32 * q)),
                           r=[Bqkn, BSbA], w=[BP[7]])
                    V(lambda e: e.tensor_tensor(tS[:], SfA[:], eglb[:, q, :].unsqueeze(2).to_broadcast([128, 4, 128]), ALU.mult), r=[BSfA, Beglb], w=[BtS])
                    yield
                    V(lambda e: e.tensor_tensor(tq[qs], pb4(4)[qs], nbeta[qs].unsqueeze(2).to_broadcast([32, 4, 128]), ALU.mult), r=[BP[4], Bnbeta], w=[Btq])
                    V(lambda e: e.tensor_tensor(vnew[qs], tq[qs], ub[qs], ALU.add), r=[Btq, Bub], w=[Bvnew])
                    for h in range(4):
                        PE(lambda e: e.matmul(pb4(5)[:, h, :], kdec[qs, h, :], vnew[qs, h, :], start=True, stop=True, tile_position=(32 * q, 0)),
                           r=[Bkdec, Bvnew], w=[BP[5]])
                    yield
                    if mode == 'p':
                        V(lambda e: e.tensor_tensor(SbA[:], tS[:], pb4(5), ALU.add), r=[BtS, BP[5]], w=[BSbA])
                    V(lambda e: e.tensor_tensor(SfA[:], tS[:], pb4(5), ALU.add), r=[BtS, BP[5]], w=[BSfA])
                    if mode == 's':
                        K.dma('s', delta_so[j, 4 * idx + q].rearrange("h d e -> d h e"), SfA[:], BSfA, r=[BSfA])
                if lastp:
                    K.dma('s', delta_p[j].rearrange("h d e -> d h e"), SfA[:], BSfA, r=[BSfA])
                yield
                yield
                for h in range(4):
                    PE(lambda e: e.matmul(pb4(6)[:, h, :], QKT[:, h, :], vnew[:, h, :], start=True, stop=True), r=[BQKT, Bvnew], w=[BP[6]])
                V(lambda e: e.tensor_tensor(o2[:], pb4(7), osc[:].unsqueeze(2).to_broadcast([128, 4, 128]), ALU.mult), r=[BP[7], Bosc], w=[Bo2])
                V(lambda e: e.tensor_tensor(oA[:], o2[:], pb4(6), ALU.add), r=[Bo2, BP[6]], w=[BoA])
                yield
                ss4, Bss4 = sc4["ss4"]; rs4, Brs4 = sc4["rs4"]
                V(lambda e: e.tensor_tensor(o2[:], oA[:], oA[:], ALU.mult), r=[BoA], w=[Bo2])
                V(lambda e: e.tensor_reduce(ss4[:], o2[:], AX.X, ALU.add), r=[Bo2], w=[Bss4])
                A(lambda e: e.activation(rs4[:], ss4[:], AF.Ln, bias=EPS, scale=1.0 / 128), r=[Bss4], w=[Brs4])
                A(lambda e: e.activation(rs4[:], rs4[:], AF.Exp, scale=-0.5), r=[Brs4], w=[Brs4])
                V(lambda e: e.tensor_tensor(oA[:], oA[:], rs4[:].unsqueeze(2).to_broadcast([128, 4, 128]), ALU.mult), r=[BoA, Brs4], w=[BoA])
                V(lambda e: e.tensor_tensor(ocb[:, 0:512].rearrange("p (a c) -> p a c", a=4), oA[:], gna[:].rearrange("p (a c) -> p a c", a=4), ALU.mult),
                  r=[BoA, Bgna], w=[Bocb])


            def Sg():
                ridx = idx if mode == 'p' else NT
                V(lambda e: e.tensor_tensor(rt1[:], qkf[:, :, 0:16], rope[:, 0:16].unsqueeze(1).to_broadcast([128, 10, 16]), ALU.mult),
                  r=[Bqkf, Brope], w=[Brt1])
                V(lambda e: e.tensor_tensor(rt2[:, :, 0:8], qkf[:, :, 8:16], rope[:, 16:24].unsqueeze(1).to_broadcast([128, 10, 8]), ALU.mult),
                  r=[Bqkf, Brope], w=[Brt2])
                V(lambda e: e.tensor_tensor(rt2[:, :, 8:16], qkf[:, :, 0:8], rope[:, 24:32].unsqueeze(1).to_broadcast([128, 10, 8]), ALU.mult),
                  r=[Bqkf, Brope], w=[Brt2])
                V(lambda e: e.tensor_tensor(qkf[:, :, 0:16], rt1[:], rt2[:], ALU.add), r=[Brt1, Brt2], w=[Bqkf])
                yield
                cur = idx % 2 if mode == 'p' else 0
                prv = 1 - cur
                kc_t, Bkc_t = kcur[0]
                kT_cur, BkT_cur = kTs[cur]; kT_prv, BkT_prv = kTs[prv]
                v_cur, Bv_cur = v16[cur]; v_prv, Bv_prv = v16[prv]
                A(lambda e: e.copy(qb16[:], qkf[:, 0:8, :].rearrange("p (a c) d -> p c a d", a=2)), r=[Bqkf], w=[Bqb16])
                A(lambda e: e.copy(kc_t[:], qkf[:, 8:10, :].rearrange("p a c -> p (a c)")), r=[Bqkf], w=[Bkc_t])
                A(lambda e: e.copy(v_cur[:], vf[:]), r=[Bvf], w=[Bv_cur])
                if lastp:
                    K.dma('s', wk_p[j], qkf[:, 8:10, :].rearrange("p a c -> p (a c)"), Bqkf, r=[Bqkf])
                    K.dma('s', wv_p[j], vf[:], Bvf, r=[Bvf])
                if mode == 's':
                    for q in range(4):
                        sq_ = 4 * idx + q
                        K.dma('s', wk_so[j, sq_, 120:128, :], qkf[32 * q:32 * q + 8, 8:10, :].rearrange("p a c -> p (a c)"), Bqkf, r=[Bqkf])
                        K.dma('s', wv_so[j, sq_, 120:128, :], vf[32 * q:32 * q + 8, :], Bvf, r=[Bvf])
                        K.dma('s', wk_so[j, sq_, 0:120, :], wk_s[j, sq_, 8:128, :], Bd2d)
                        K.dma('s', wv_so[j, sq_, 0:120, :], wv_s[j, sq_, 8:128, :], Bd2d)
                        K.dma('s', kcf[:, q, :], wk_s[j, sq_], Bkcf, w=[Bkcf])
                        K.dma('s', vcf[:, q, :], wv_s[j, sq_], Bvcf, w=[Bvcf])
                    A(lambda e: e.copy(kc16[:], kcf[:]), r=[Bkcf], w=[Bkc16])
                    A(lambda e: e.copy(vc16[:], vcf[:]), r=[Bvcf], w=[Bvc16])
                    transposes(4, lambda i: kc16[:, i, :], [Bkc16], kTc, BkTc)
                yield
                transposes(4, lambda i: qb16[:, i, :, :].rearrange("p a d -> p (a d)"), [Bqb16], qT, BqT)
                PE(lambda e: e.transpose(PT[:, 0, :], kc_t[:], ident_b[:]), r=[Bkc_t, Bidb], w=[BPT])
                V(lambda e: e.tensor_copy(kT_cur[:], PT[:, 0, :]), r=[BPT], w=[BkT_cur])
                mi = 2 if mode == 's' else (1 if idx == 0 else 0)
                mx4, Bmx4 = sc4["mx4"]; negm, Bnegm = sc4["negm"]; rsum, Brsum = sc4["rsum"]; es4, Bes4 = sc4["es4"]; den4, Bden4 = sc4["den4"]
                for gi in range(2):
                    yield
                    ks = slice(64 * gi, 64 * gi + 64)
                    for c in range(4):
                        b = 1 + c // 2
                        o0 = (c % 2) * 256
                        if mode == 'p':
                            PE(lambda e: e.matmul(pb(b)[:, o0:o0 + 128], qT[ks, c, :], kT_prv[ks, :], start=True, stop=True, tile_position=(64 * gi, 0)),
                               r=[BqT, BkT_prv], w=[BP[b]])
                        else:
                            for q in range(4):
                                qs = slice(32 * q, 32 * q + 32)
                                PE(lambda e: e.matmul(pb(b)[qs, o0:o0 + 128], qT[ks, c, qs], kTc[ks, q, :], start=True, stop=True,
                                                      tile_position=(64 * gi, 32 * q)), r=[BqT, BkTc], w=[BP[b]])
                        PE(lambda e: e.matmul(pb(b)[:, o0 + 128:o0 + 256], qT[ks, c, :], kT_cur[ks, :], start=True, stop=True, tile_position=(64 * gi, 0)),
                           r=[BqT, BkT_cur], w=[BP[b]])
                    yield
                    for b2 in range(2):
                        V(lambda e: e.scalar_tensor_tensor(sm[:, 2 * b2:2 * b2 + 2, :], pb(1 + b2).rearrange("p (a c) -> p a c", a=2), 0.125,
                                                           swam[:, mi, :].unsqueeze(1).to_broadcast([128, 2, 256]), ALU.mult, ALU.add),
                          r=[BP[1 + b2], Bswam], w=[Bsm])
                    V(lambda e: e.tensor_reduce(mx4[:], sm[:], AX.X, ALU.max), r=[Bsm], w=[Bmx4])
                    V(lambda e: e.tensor_tensor(mx4[:], mx4[:], sinkbc[:, 4 * gi:4 * gi + 4], ALU.max), r=[Bmx4, Bsinkbc], w=[Bmx4])
                    V(lambda e: e.tensor_scalar_mul(negm[:], mx4[:], -1.0), r=[Bmx4], w=[Bnegm])
                    yield
                    for c in range(4):
                        A(lambda e: e.activation(pr[:, c, :], sm[:, c, :], AF.Exp, bias=negm[:, c:c + 1], accum_out=rsum[:, c:c + 1]),
                          r=[Bsm, Bnegm], w=[Bpr, Brsum])
                    V(lambda e: e.tensor_tensor(es4[:], sinkbc[:, 4 * gi:4 * gi + 4], mx4[:], ALU.subtract), r=[Bsinkbc, Bmx4], w=[Bes4])
                    A(lambda e: e.activation(es4[:], es4[:], AF.Exp), r=[Bes4], w=[Bes4])
                    V(lambda e: e.tensor_tensor(den4[:], rsum[:], es4[:], ALU.add), r=[Brsum, Bes4], w=[Bden4])
                    V(lambda e: e.reciprocal(rden[:, 4 * gi:4 * gi + 4], den4[:]), r=[Bden4], w=[Brden])
                    yield
                    for c in range(4):
                        PE(lambda e: e.transpose(PT[:, 2 * c, :], pr[:, c, 0:128], ident_b[:]), r=[Bpr, Bidb], w=[BPT])
                        PE(lambda e: e.transpose(PT[:, 2 * c + 1, :], pr[:, c, 128:256], ident_b[:]), r=[Bpr, Bidb], w=[BPT])
                    V(lambda e: e.tensor_copy(prT[:], PT[:]), r=[BPT], w=[BprT])
                    yield
                    for c in range(4):
                        h = 4 * gi + c
                        hs = slice(64 * h, 64 * h + 64)
                        if mode == 'p':
                            PE(lambda e: e.matmul(pb(3)[:, hs], prT[:, 2 * c, :], v_prv[:, ks], start=True, stop=False), r=[BprT, Bv_prv], w=[BP[3]])
                        else:
                            for q in range(4):
                                qs = slice(32 * q, 32 * q + 32)
                                PE(lambda e: e.matmul(pb(3)[qs, hs], prT[:, 2 * c, qs], vc16[:, q, ks], start=True, stop=False, tile_position=(0, 32 * q)),
                                   r=[BprT, Bvc16], w=[BP[3]])
                        PE(lambda e: e.matmul(pb(3)[:, hs], prT[:, 2 * c + 1, :], v_cur[:, ks], start=False, stop=True), r=[BprT, Bv_cur], w=[BP[3]])
                yield
                V(lambda e: e.tensor_tensor(ob[:], pb(3).rearrange("p (a c) -> p a c", a=8), rden[:].unsqueeze(2).to_broadcast([128, 8, 64]), ALU.mult),
                  r=[BP[3], Brden], w=[Bob])
                V(lambda e: e.tensor_tensor(ocb[:, 512:1024], ob[:].rearrange("p a c -> p (a c)"), sgb[:], ALU.mult), r=[Bob, Bsgb], w=[Bocb])

            def Eg():
                yield from out_proj_post(l, mode, idx, S)

            return Pre(), Pg(), [Dg(), Sg()], Eg()

          return setup, even_tile

        def make_odd(sbl):
          win, Bwin = sbl("win", [128, 8, ODD_IN], BF16)
          omlbc, Bomlbc = sbl("omlbc", [128, D]); ncbc, Bncbc = sbl("ncbc", [128, 128])
          kk, Bkk = sbl("kk", [128, D]); logf, Blogf = sbl("logf", [128, D])
          Gsb, BGsb = sbl("Gsb", [128, D])
          tmpD, BtmpD = sbl("tmpD", [128, D]); eGd, BeGd = sbl("eGd", [128, D])
          ktl, Bktl = sbl("ktl", [128, D], BF16); khat, Bkhat = sbl("khat", [128, D], BF16); qtl, Bqtl = sbl("qtl", [128, D], BF16)
          vtls = [sbl("vtl%d" % i, [128, D], BF16) for i in range(2)]; gnc, Bgnc = sbl("gnc", [128, D])
          qTo, BqTo = sbl("qTo", [128, 8, 128], BF16); kTo, BkTo = sbl("kTo", [128, 8, 128], BF16)
          ATo, BATo = sbl("ATo", [128, 8, 128], BF16)
          eglo, Beglo = sbl("eglo", [128, 8, 4]); tSC, BtSC = sbl("tSC", [128, 8, 128])
          SfCs = [sbl("SfC%d" % i, [128, 8, 128]) for i in range(2)]; SbCs = [sbl("SbC%d" % i, [128, 8, 128], BF16) for i in range(2)]
          oC, BoC = sbl("oC", [128, 8, 128]); ss8, Bss8 = sbl("ss8", [128, 8]); rs8, Brs8 = sbl("rs8", [128, 8])

          def setup(j):
                load_w(win, Bwin, w_in_odd[j], D, ODD_IN)
                load_w(wout, Bwout, w_out_odd[j], D, D)
                K.dma('s', ncbc[:], norm_c[j:j + 1, :].partition_broadcast(128), Bncbc, w=[Bncbc])
                if j == 0:
                    V(lambda e: e.memset(omlbc[:], 1.0), w=[Bomlbc])
                else:
                    K.dma('s', tmpD[:], lb_raw[0:1, :].partition_broadcast(128), BtmpD, w=[BtmpD])
                    K.dma('s', eGd[:], lb_raw[1:2, :].partition_broadcast(128), BeGd, w=[BeGd])
                    V(lambda e: e.tensor_tensor(omlbc[:], tmpD[:], eGd[:], ALU.subtract), r=[BtmpD, BeGd], w=[Bomlbc])
                    A(lambda e: e.activation(omlbc[:], omlbc[:], AF.Sigmoid), r=[Bomlbc], w=[Bomlbc])

          def odd_tile(l, mode, idx, S, S2):
            vtl, Bvtl = vtls[0] if S is SL[0] else vtls[1]
            j = l // 2
            vcol = 0 if mode == 'p' else 1
            lastp = (mode == 'p' and idx == NT - 1)
            xT, BxT = S["xT"]

            def Pre():
                yield from load_tile(l, mode, idx, S)

            def Pg():

                def proj(c0, b0):
                    for n in range(2):
                        for kc in range(8):
                            PE(lambda e: e.matmul(pb(b0 + n), xT[:, kc, :], win[:, kc, c0 + n * 512:c0 + (n + 1) * 512], start=(kc == 0), stop=(kc == 7)),
                               r=[Bwin, BxT], w=[BP[b0 + n]])

                H = lambda t_, n: t_[:, n * 512:(n + 1) * 512]
                yield
                proj(1024, 5)
                for n in range(2):
                    A(lambda e: e.activation(H(kk, n), pb(5 + n), AF.Sigmoid, scale=-1.0), r=[BP[5 + n]], w=[Bkk])
                V(lambda e: e.tensor_tensor(kk[:], kk[:], omlbc[:], ALU.mult), r=[Bkk, Bomlbc], w=[Bkk])
                A(lambda e: e.activation(logf[:], kk[:], AF.Ln, bias=1.0, scale=-1.0), r=[Bkk], w=[Blogf])
                if mode == 's':
                    V(lambda e: e.tensor_scalar_mul(logf[:], logf[:], valid[:, 1:2]), r=[Blogf, Bval], w=[Blogf])
                yield
                proj(2048, 5)
                for n in range(2):
                    if mode == 's':
                        V(lambda e: e.tensor_scalar_mul(H(vtl, n), pb(5 + n), valid[:, 1:2]), r=[BP[5 + n], Bval], w=[Bvtl])
                    else:
                        V(lambda e: e.tensor_copy(H(vtl, n), pb(5 + n)), r=[BP[5 + n]], w=[Bvtl])
                yield
                for n in range(2):
                    PE(lambda e: e.matmul(pb(1 + n), maskC[:], H(logf, n), start=True, stop=True), r=[BmC, Blogf], w=[BP[1 + n]])
                    PE(lambda e: e.matmul(pb(3 + n), bones[:], H(logf, n), start=True, stop=True), r=[Bbo, Blogf], w=[BP[3 + n]])
                for h in range(8):
                    PE(lambda e: e.matmul(pb(7)[:, 4 * h:4 * h + 4], logf[:, h * 128:(h + 1) * 128], ind[:], start=True, stop=True), r=[Blogf, Bind], w=[BP[7]])
                A(lambda e: e.activation(eglo[:].rearrange("p a c -> p (a c)"), pb(7)[:, 0:32], AF.Exp), r=[BP[7]], w=[Beglo])
                for n in range(2):
                    V(lambda e: e.tensor_copy(H(Gsb, n), pb(1 + n)), r=[BP[1 + n]], w=[BGsb])
                    A(lambda e: e.activation(H(tmpD, n), pb(1 + n), AF.Exp, scale=-1.0), r=[BP[1 + n]], w=[BtmpD])
                    A(lambda e: e.activation(H(eGd, n), pb(1 + n), AF.Exp), r=[BP[1 + n]], w=[BeGd])
                V(lambda e: e.tensor_tensor(ktl[:], kk[:], tmpD[:], ALU.mult), r=[Bkk, BtmpD], w=[Bktl])
                for n in range(2):
                    V(lambda e: e.tensor_tensor(H(tmpD, n), pb(3 + n), H(Gsb, n), ALU.subtract), r=[BP[3 + n], BGsb], w=[BtmpD])
                A(lambda e: e.activation(tmpD[:], tmpD[:], AF.Exp), r=[BtmpD], w=[BtmpD])
                V(lambda e: e.tensor_tensor(khat[:], kk[:], tmpD[:], ALU.mult), r=[Bkk, BtmpD], w=[Bkhat])
                yield
                proj(0, 1)
                for n in range(2):
                    A(lambda e: e.activation(H(tmpD, n), pb(1 + n), AF.Silu), r=[BP[1 + n]], w=[BtmpD])
                V(lambda e: e.scalar_tensor_tensor(qtl[:], tmpD[:], float(128 ** -0.5), eGd[:], ALU.mult, ALU.mult), r=[BtmpD, BeGd], w=[Bqtl])
                yield
                proj(3072, 3)
                for n in range(2):
                    A(lambda e: e.activation(H(gnc, n), pb(3 + n), AF.Silu), r=[BP[3 + n]], w=[Bgnc])
                V(lambda e: e.tensor_tensor(gnc[:].rearrange("p (a c) -> p a c", a=8), gnc[:].rearrange("p (a c) -> p a c", a=8),
                                            ncbc[:].unsqueeze(1).to_broadcast([128, 8, 128]), ALU.mult), r=[Bgnc, Bncbc], w=[Bgnc])
                yield
                transposes(8, lambda i: qtl[:, i * 128:(i + 1) * 128], [Bqtl], qTo, BqTo)
                yield
                transposes(8, lambda i: ktl[:, i * 128:(i + 1) * 128], [Bktl], kTo, BkTo)
                yield
                for h in range(8):
                    b = 5 + h // 4
                    PE(lambda e: e.matmul(pb4(b)[:, h % 4, :], kTo[:, h, :], qTo[:, h, :], start=True, stop=True), r=[BkTo, BqTo], w=[BP[b]])
                for n in range(2):
                    V(lambda e: e.tensor_tensor(ATo[:, 4 * n:4 * n + 4, :], pb4(5 + n), maskC[:].unsqueeze(1).to_broadcast([128, 4, 128]), ALU.mult),
                      r=[BP[5 + n], BmC], w=[BATo])

            def Cg():
                SfC, BSfC = SfCs[0]; SbC, BSbC = SbCs[0]
                if mode == 'p' and idx == 0:
                    V(lambda e: e.memset(SfC[:], 0.0), w=[BSfC])
                    V(lambda e: e.memset(SbC[:], 0.0), w=[BSbC])
                if mode == 's':
                    K.dma('s', SfCs[0][0][:], hg_s[j, 4 * idx].rearrange("h d e -> d h e"), SfCs[0][1], w=[SfCs[0][1]])
                for q in range(4):
                    qs = slice(32 * q, 32 * q + 32)
                    if mode == 's':
                        SfC, BSfC = SfCs[q % 2]; SbC, BSbC = SbCs[q % 2]
                        if q < 3:
                            K.dma('s', SfCs[(q + 1) % 2][0][:], hg_s[j, 4 * idx + q + 1].rearrange("h d e -> d h e"), SfCs[(q + 1) % 2][1],
                                  w=[SfCs[(q + 1) % 2][1]])
                        A(lambda e: e.copy(SbC[:], SfC[:]), r=[BSfC], w=[BSbC])
                    for h in range(8):
                        b = 1 + h // 4
                        PE(lambda e: e.matmul(pb4(b)[qs, h % 4, :], qTo[:, h, qs], SbC[:, h, :], start=True, stop=True, tile_position=(0, 32 * q)),
                           r=[BqTo, BSbC], w=[BP[b]])
                    for h in range(8):
                        b = 3 + h // 4
                        PE(lambda e: e.matmul(pb4(b)[:, h % 4, :], khat[qs, h * 128:(h + 1) * 128], vtl[qs, h * 128:(h + 1) * 128], start=True, stop=True,
                                              tile_position=(32 * q, 0)), r=[Bkhat, Bvtl], w=[BP[b]])
                    V(lambda e: e.tensor_tensor(tSC[:], SfC[:], eglo[:, :, q:q + 1].to_broadcast([128, 8, 128]), ALU.mult), r=[BSfC, Beglo], w=[BtSC])
                    yield
                    if mode == 'p':
                        for n in range(2):
                            V(lambda e: e.tensor_tensor(SbC[:, 4 * n:4 * n + 4, :], tSC[:, 4 * n:4 * n + 4, :], pb4(3 + n), ALU.add), r=[BtSC, BP[3 + n]], w=[BSbC])
                    for n in range(2):
                        V(lambda e: e.tensor_tensor(SfC[:, 4 * n:4 * n + 4, :], tSC[:, 4 * n:4 * n + 4, :], pb4(3 + n), ALU.add), r=[BtSC, BP[3 + n]], w=[BSfC])
                    if mode == 's':
                        K.dma('s', hg_so[j, 4 * idx + q].rearrange("h d e -> d h e"), SfC[:], BSfC, r=[BSfC])
                if lastp:
                    K.dma('s', hg_p[j].rearrange("h d e -> d h e"), SfC[:], BSfC, r=[BSfC])
                yield
                for n in range(2):
                    A(lambda e: e.copy(oC[:, 4 * n:4 * n + 4, :], pb4(1 + n)), r=[BP[1 + n]], w=[BoC])
                for h in range(8):
                    b = 5 + h // 4
                    PE(lambda e: e.matmul(pb4(b)[:, h % 4, :], ATo[:, h, :], vtl[:, h * 128:(h + 1) * 128], start=True, stop=True), r=[BATo, Bvtl], w=[BP[b]])
                for n in range(2):
                    V(lambda e: e.tensor_tensor(oC[:, 4 * n:4 * n + 4, :], oC[:, 4 * n:4 * n + 4, :], pb4(5 + n), ALU.add), r=[BoC, BP[5 + n]], w=[BoC])
                yield
                t3 = tmpD[:].rearrange("p (a c) -> p a c", a=8)
                V(lambda e: e.tensor_tensor(t3, oC[:], oC[:], ALU.mult), r=[BoC], w=[BtmpD])
                V(lambda e: e.tensor_reduce(ss8[:], t3, AX.X, ALU.add), r=[BtmpD], w=[Bss8])
                A(lambda e: e.activation(rs8[:], ss8[:], AF.Ln, bias=EPS, scale=1.0 / 128), r=[Bss8], w=[Brs8])
                A(lambda e: e.activation(rs8[:], rs8[:], AF.Exp, scale=-0.5), r=[Brs8], w=[Brs8])
                V(lambda e: e.tensor_tensor(oC[:], oC[:], rs8[:].unsqueeze(2).to_broadcast([128, 8, 128]), ALU.mult), r=[BoC, Brs8], w=[BoC])
                V(lambda e: e.tensor_tensor(ocb[:].rearrange("p (a c) -> p a c", a=8), oC[:], gnc[:].rearrange("p (a c) -> p a c", a=8), ALU.mult),
                  r=[BoC, Bgnc], w=[Bocb])

            def Eg():
                yield from out_proj_post(l, mode, idx, S)

            return Pre(), Pg(), [Cg()], Eg()

          return setup, odd_tile

        for l in range(NL):
            j = l // 2
            with ExitStack() as les:
                def sbl(name, shape, dt=F32, _les=les, _l=l):
                    t = _les.enter_context(nc.sbuf_tensor("%s_L%d" % (name, _l), list(shape), dt))
                    return t, Buf("%s_L%d" % (name, _l))
                K.dma('s', gbc[:], ln_g[l:l + 1, :].partition_broadcast(128), Bgbc, w=[Bgbc])
                K.dma('s', bbc[:], ln_b[l:l + 1, :].partition_broadcast(128), Bbbc, w=[Bbbc])
                load_w(wgate, Bwgate, w_pgate[l], D, D)
                load_w(wpp, Bwpp, w_pproj[l], PLE, D)
                setup, tile_fn = (make_even if l % 2 == 0 else make_odd)(sbl)
                setup(j)
                tiles = [('p', t) for t in range(NT)] + [('s', st) for st in range(NSEQ // 4)]
                parts = [tile_fn(l, mode_, idx_, SL[ti % 2], SL[(ti + 1) % 2]) for ti, (mode_, idx_) in enumerate(tiles)]
                prefetch = (l % 2 == 1)
                if prefetch:
                    interleave(parts[0][0])
                prevE = None

                def chain2(a_, b_):
                    yield from a_
                    yield from b_

                def take(gen_, n_):
                    for _ in range(n_):
                        try:
                            next(gen_)
                        except StopIteration:
                            return
                        yield
                early = [False] * len(tiles)
                for ti in range(len(tiles)):
                    Pre_, Pg, chains, Eg = parts[ti]
                    if prefetch or early[ti]:
                        interleave(prevE, Pg, seq=(SEQUENTIAL in (1, 2)))
                    else:
                        interleave(prevE, chain2(Pre_, Pg), seq=(SEQUENTIAL in (1, 2)))
                    if len(chains) == 2:
                        D_, S_ = chains
                        aliveD = aliveS = True
                        while aliveS:
                            for _ in range(2):
                                if aliveD:
                                    try:
                                        next(D_)
                                    except StopIteration:
                                        aliveD = False
                            try:
                                next(S_)
                            except StopIteration:
                                aliveS = False
                        X_ = None
                        if ti + 1 < len(tiles) and tiles[ti + 1][0] == 'p' and tiles[ti + 1][1] < NT - 1 and not SEQUENTIAL:
                            X_ = chain2(parts[ti + 1][0], take(parts[ti + 1][1], 4))
                            early[ti + 1] = True
                        interleave(D_ if aliveD else None, X_)
                    else:
                        nxt = None
                        if prefetch and ti + 1 < len(tiles):
                            nxt = chain2(parts[ti + 1][0], take(parts[ti + 1][1], 3)) if not SEQUENTIAL else parts[ti + 1][0]
                        interleave(chains[0], nxt, seq=(SEQUENTIAL in (1, 3)))
                    prevE = Eg
                interleave(prevE)
                K.barrier()
        K.finish_all('s')
    return nc


_CACHE = {}


def run_cores(per_core, NT, NL=4):
    key = (NT, NL)
    if key not in _CACHE:
        _CACHE[key] = build(NT, NL)
    nc = _CACHE[key]
    res = run_bass_kernel_spmd(nc, per_core, core_ids=list(range(len(per_core))))
    return res.results


def kernel(x_prompt, x_sample, state_conv_a, state_delta_a, cache_win_k, cache_win_v, state_hgrn_c,
           p_prompt, p_sample, w_in_even, conv_w_a, a_log, dt_bias, norm_a, sinks_b, w_out_even,
           w_in_odd, lb_raw, norm_c, w_out_odd, ln_g, ln_b, w_ple_proj, w_ple_gate):
    f = lambda a: np.ascontiguousarray(np.asarray(a, dtype=np.float32))
    x_prompt = f(x_prompt); x_sample = f(x_sample)
    NB, L = x_prompt.shape[:2]
    NT = L // 128
    NL = ln_g.shape[0]
    NE = (NL + 1) // 2
    consts = make_consts(NT)
    shared = {
        "w_in_even": f(w_in_even), "conv_w": f(conv_w_a), "a_log": f(a_log), "dt_bias": f(dt_bias), "norm_a": f(norm_a),
        "sinks": f(sinks_b), "w_out_even": f(w_out_even), "w_in_odd": f(w_in_odd), "lb_raw": f(lb_raw), "norm_c": f(norm_c),
        "w_out_odd": f(w_out_odd), "ln_g": f(ln_g), "ln_b": f(ln_b), "w_pproj": f(w_ple_proj), "w_pgate": f(w_ple_gate),
        "c_ident": consts["ident"], "c_maskC": consts["maskC"], "c_maskS": consts["maskS"], "c_bones": consts["bones"],
        "c_mnegC": consts["mnegC"], "c_ones": consts["ones"], "c_ind": consts["ind"], "c_valid": consts["valid"],
        "c_swam": consts["swam"], "c_rope": consts["rope"],
    }
    p_prompt = f(p_prompt); p_sample = f(p_sample)
    state_conv_a = f(state_conv_a); state_delta_a = f(state_delta_a)
    cache_win_k = f(cache_win_k); cache_win_v = f(cache_win_v); state_hgrn_c = f(state_hgrn_c)
    ncores = 8
    per_core = []
    for c in range(ncores):
        sq = c % NB
        s0, s1 = c * NSEQ, (c + 1) * NSEQ
        m = dict(shared)
        m["xp"] = f(x_prompt[sq]); m["pp"] = f(p_prompt[:, sq])
        m["xs"] = f(x_sample[s0:s1]); m["ps"] = f(p_sample[:, s0:s1])
        m["conv_s"] = f(state_conv_a[:, s0:s1]); m["delta_s"] = f(state_delta_a[:, s0:s1])
        m["wk_s"] = f(cache_win_k[:, s0:s1].reshape(NE, NSEQ, 128, 128)); m["wv_s"] = f(cache_win_v[:, s0:s1].reshape(NE, NSEQ, 128, 128))
        m["hg_s"] = f(state_hgrn_c[:, s0:s1])
        per_core.append(m)
    res = run_cores(per_core, NT, NL)
    cat = lambda k, ax: np.concatenate([res[c][k] for c in range(ncores)], axis=ax)
    stk = lambda k: np.stack([res[c][k] for c in range(NB)], axis=1)
    y_p = np.stack([res[c]["yp"] for c in range(NB)], axis=0)
    y_s = cat("ys", 0)
    NS = ncores * NSEQ
    outs = (y_p, y_s, stk("conv_p"), cat("conv_so", 1), stk("delta_p"), cat("delta_so", 1),
            stk("wk_p").reshape(NE, NB, 128, 2, 64), cat("wk_so", 1).reshape(NE, NS, 128, 2, 64),
            stk("wv_p").reshape(NE, NB, 128, 2, 64), cat("wv_so", 1).reshape(NE, NS, 128, 2, 64),
            stk("hg_p"), cat("hg_so", 1))
    return tuple(np.ascontiguousarray(o.astype(np.float32)) for o in outs)
```

```python
import numpy as np
from contextlib import ExitStack
import concourse.bass as bass
import concourse.mybir as mybir
from concourse.bass_utils import run_bass_kernel_spmd

F32 = mybir.dt.float32
BF16 = mybir.dt.bfloat16
AF = mybir.ActivationFunctionType
ALU = mybir.AluOpType
AX = mybir.AxisListType

D = 1024
EVEN_IN = 3336
ODD_IN = 4096
PLE = 256
ALPHA = float(8 ** 0.25)
EPS = 1e-6
NEG = -30000.0
PAST_LEN = 8192
ROPE_THETA = 500000.0
NSEQ = 16
SEQUENTIAL = 0
EP_PATTERN_EVEN = "1" + "00" + "1111" + "0" + "11" + "00" + "11" + "0" + "11111"
EP_PATTERN_ODD = "1" + "00" + "111" + "0" + "11" + "00" + "11" + "0" + "11111"


class Buf:
    def __init__(self, name):
        self.name = name
        self.w = None
        self.rs = {}
        self.dsem = None
        self.dcnt = 0
        self.excl = False


class Kern:
    def __init__(self, nc, es):
        self.nc = nc
        self.es = es
        self.eng = {'p': nc.tensor, 'v': nc.vector, 'a': nc.scalar, 'g': nc.gpsimd, 's': nc.sync}
        self.esem = {k: es.enter_context(nc.semaphore("es_" + k)) for k in self.eng}
        self.ecnt = {k: 0 for k in self.eng}
        self.seen = {k: {} for k in self.eng}
        self.alltoks = {}

    def _wait(self, e, tok):
        sem, val = tok
        if e == 'p' and sem is self.esem['p']:
            return
        if self.seen[e].get(sem.name, 0) >= val:
            return
        self.eng[e].wait_ge(sem, val)
        self.seen[e][sem.name] = val

    def _deps(self, e, r, w):
        for b in r:
            if b.w is not None:
                self._wait(e, b.w)
            if b.excl:
                for tok in list(b.rs.values()):
                    if tok[0] is not self.esem[e]:
                        self._wait(e, tok)
        for b in w:
            if b.w is not None:
                self._wait(e, b.w)
            for tok in list(b.rs.values()):
                self._wait(e, tok)

    def _upd(self, tok, r, w):
        for b in r:
            old = b.rs.get(tok[0].name)
            if old is None or old[1] < tok[1]:
                b.rs[tok[0].name] = tok
        for b in w:
            b.w = tok
            b.rs = {}

    def op(self, e, fn, r=(), w=()):
        self._deps(e, r, w)
        inst = fn(self.eng[e])
        self.ecnt[e] += 1
        inst.then_inc(self.esem[e], 1)
        self._upd((self.esem[e], self.ecnt[e]), r, w)

    def dma(self, q, out, in_, dbuf, r=(), w=(), **kw):
        self._deps(q, r, w)
        if dbuf.dsem is None:
            dbuf.dsem = self.es.enter_context(self.nc.semaphore("ds_" + dbuf.name))
        inst = self.eng[q].dma_start(out=out, in_=in_, **kw)
        dbuf.dcnt += 1
        inst.then_inc(dbuf.dsem, 16)
        tok = (dbuf.dsem, 16 * dbuf.dcnt)
        self.alltoks[dbuf.dsem.name] = tok
        self._upd(tok, r, w)

    def finish_all(self, e):
        for tok in self.alltoks.values():
            self._wait(e, tok)

    def barrier(self):
        toks = [(self.esem[e], self.ecnt[e]) for e in self.eng if self.ecnt[e] > 0] + list(self.alltoks.values())
        for e in self.eng:
            for tok in toks:
                self._wait(e, tok)


def make_consts(NT):
    c = {}
    i = np.arange(128)
    blk = i // 32
    same = blk[:, None] == blk[None, :]
    c["ident"] = np.eye(128, dtype=np.float32)
    c["maskC"] = (same & (i[:, None] <= i[None, :])).astype(np.float32)
    c["maskS"] = (same & (i[:, None] < i[None, :])).astype(np.float32)
    c["bones"] = same.astype(np.float32)
    c["mnegC"] = np.where(c["maskC"] > 0, 0.0, NEG).astype(np.float32)
    c["ones"] = np.ones((128, 128), np.float32)
    c["ind"] = (blk[:, None] == np.arange(4)[None, :]).astype(np.float32)
    valid = np.ones((128, 2), np.float32)
    valid[:, 1] = ((i % 32) < 8).astype(np.float32)
    c["valid"] = valid
    M = np.full((3, 128, 256), -1e30, np.float32)
    jj = np.arange(128)
    prev_ok = jj[None, :] >= i[:, None]
    cur_ok = jj[None, :] <= i[:, None]
    M[0, :, :128][prev_ok] = 0.0
    M[0, :, 128:][cur_ok] = 0.0
    M[1, :, 128:][cur_ok] = 0.0
    l = i % 32
    sprev = jj[None, :] >= l[:, None]
    scur = same & ((jj % 32)[None, :] <= l[:, None]) & ((jj % 32)[None, :] < 8)
    M[2, :, :128][sprev] = 0.0
    M[2, :, 128:][scur] = 0.0
    c["swam"] = M
    inv = ROPE_THETA ** (-np.arange(0, 16, 2, dtype=np.float32) / np.float32(16))
    R = np.zeros((NT + 1, 128, 32), np.float32)
    for t in range(NT + 1):
        pos = (128 * t + i) if t < NT else (PAST_LEN + (i % 32))
        ang = pos.astype(np.float32)[:, None] * inv[None, :].astype(np.float32)
        cs = np.cos(ang).astype(np.float32)
        sn = np.sin(ang).astype(np.float32)
        R[t, :, 0:8] = cs
        R[t, :, 8:16] = cs
        R[t, :, 16:24] = -sn
        R[t, :, 24:32] = sn
    c["rope"] = R
    return c


def build(NT, NL=4):
    L = NT * 128
    nc = bass.Bass("TRN2", target_bir_lowering=False)
    es = ExitStack()

    def din(name, shape):
        return nc.dram_tensor(name, list(shape), F32, kind="ExternalInput").ap()

    def dout(name, shape):
        return nc.dram_tensor(name, list(shape), F32, kind="ExternalOutput").ap()

    NE = (NL + 1) // 2
    NO = NL // 2
    xp = din("xp", [L, D]); ppd = din("pp", [NL, L, PLE])
    xs = din("xs", [NSEQ, 8, D]); psd = din("ps", [NL, NSEQ, 8, PLE])
    conv_s = din("conv_s", [NE, NSEQ, 3, 1536]); delta_s = din("delta_s", [NE, NSEQ, 4, 128, 128])
    wk_s = din("wk_s", [NE, NSEQ, 128, 128]); wv_s = din("wv_s", [NE, NSEQ, 128, 128])
    hg_s = din("hg_s", [max(NO, 1), NSEQ, 8, 128, 128])
    w_in_even = din("w_in_even", [NE, D, EVEN_IN]); conv_w = din("conv_w", [NE, 4, 1536])
    a_log = din("a_log", [NE, 4]); dt_bias = din("dt_bias", [NE, 4]); norm_a = din("norm_a", [NE, 128])
    sinks = din("sinks", [NE, 8]); w_out_even = din("w_out_even", [NE, D, D])
    w_in_odd = din("w_in_odd", [max(NO, 1), D, ODD_IN]); lb_raw = din("lb_raw", [2, D])
    norm_c = din("norm_c", [max(NO, 1), 128]); w_out_odd = din("w_out_odd", [max(NO, 1), D, D])
    ln_g = din("ln_g", [NL, D]); ln_b = din("ln_b", [NL, D])
    w_pproj = din("w_pproj", [NL, PLE, D]); w_pgate = din("w_pgate", [NL, D, D])
    c_ident = din("c_ident", [128, 128]); c_maskC = din("c_maskC", [128, 128]); c_maskS = din("c_maskS", [128, 128])
    c_bones = din("c_bones", [128, 128]); c_mnegC = din("c_mnegC", [128, 128]); c_ones = din("c_ones", [128, 128])
    c_ind = din("c_ind", [128, 4]); c_valid = din("c_valid", [128, 2]); c_swam = din("c_swam", [3, 128, 256])
    c_rope = din("c_rope", [NT + 1, 128, 32])

    yp = dout("yp", [L, D]); ys = dout("ys", [NSEQ, 8, D])
    conv_p = dout("conv_p", [NE, 3, 1536]); conv_so = dout("conv_so", [NE, NSEQ, 3, 1536])
    delta_p = dout("delta_p", [NE, 4, 128, 128]); delta_so = dout("delta_so", [NE, NSEQ, 4, 128, 128])
    wk_p = dout("wk_p", [NE, 128, 128]); wk_so = dout("wk_so", [NE, NSEQ, 128, 128])
    wv_p = dout("wv_p", [NE, 128, 128]); wv_so = dout("wv_so", [NE, NSEQ, 128, 128])
    hg_p = dout("hg_p", [max(NO, 1), 8, 128, 128]); hg_so = dout("hg_so", [max(NO, 1), NSEQ, 8, 128, 128])
    hP = [nc.dram_tensor("hP%d" % i, [L, D], F32, kind="Internal").ap() for i in range(2)]
    hS = [nc.dram_tensor("hS%d" % i, [NSEQ, 8, D], F32, kind="Internal").ap() for i in range(2)]
    BhP = [Buf("hP0d"), Buf("hP1d")]
    BhS = [Buf("hS0d"), Buf("hS1d")]

    with es:
        K = Kern(nc, es)
        bufs = {}

        def sb(name, shape, dt=F32):
            t = es.enter_context(nc.sbuf_tensor(name, list(shape), dt))
            bufs[name] = Buf(name)
            return t, bufs[name]

        V = lambda fn, r=(), w=(): K.op('v', fn, r, w)
        A = lambda fn, r=(), w=(): K.op('a', fn, r, w)
        PE = lambda fn, r=(), w=(): K.op('p', fn, r, w)
        G = lambda fn, r=(), w=(): K.op('g', fn, r, w)

        PT = es.enter_context(nc.psum_tensor("PT", [128, 8, 128], BF16)); BPT = Buf("PT")
        PA = es.enter_context(nc.psum_tensor("PA", [128, 7, 512], F32))
        BP = [None] + [Buf("PB%d" % b) for b in range(1, 8)]
        BPT.excl = True
        for b_ in BP[1:]:
            b_.excl = True

        def pb(b):
            return PA[:, b - 1, :]

        def pb4(b):
            return PA[:, b - 1, :].rearrange("p (a c) -> p a c", a=4)

        ident_f, Bidf = sb("ident_f", [128, 128]); ident_b, Bidb = sb("ident_b", [128, 128], BF16)
        maskC, BmC = sb("maskC", [128, 128]); maskS, BmS = sb("maskS", [128, 128])
        bones, Bbo = sb("bones", [128, 128]); mnegC, Bmn = sb("mnegC", [128, 128]); ones_f, Bon = sb("ones_f", [128, 128])
        ind, Bind = sb("ind", [128, 4]); valid, Bval = sb("valid", [128, 2])
        swam, Bswam = sb("swam", [128, 3, 256])
        for t_, b_, d_ in [(ident_f, Bidf, c_ident), (maskC, BmC, c_maskC), (maskS, BmS, c_maskS), (bones, Bbo, c_bones),
                           (mnegC, Bmn, c_mnegC), (ones_f, Bon, c_ones), (ind, Bind, c_ind), (valid, Bval, c_valid)]:
            K.dma('s', t_[:], d_, b_, w=[b_])
        K.dma('s', swam[:], c_swam.rearrange("m p k -> p m k"), Bswam, w=[Bswam])
        V(lambda e: e.tensor_copy(ident_b[:], ident_f[:]), r=[Bidf], w=[Bidb])

        wout, Bwout = sb("wout", [128, 8, D], BF16)
        wgate, Bwgate = sb("wgate", [128, 8, D], BF16)
        wpp, Bwpp = sb("wpp", [128, 2, D], BF16)
        WSTG = 1024
        stg_i = [0]
        gbc, Bgbc = sb("gbc", [128, D]); bbc, Bbbc = sb("bbc", [128, D])

        def load_w(dst, Bdst, src, nrows, ncols):
            for kc in range(nrows // 128):
                for c0 in range(0, ncols, WSTG):
                    cw = min(WSTG, ncols - c0)
                    st, Bst = stg[stg_i[0] % 3]
                    q = 's'
                    K.dma(q, st[:, 0:cw], src[kc * 128:(kc + 1) * 128, c0:c0 + cw], Bst, w=[Bst])
                    if stg_i[0] % 2 == 0:
                        V(lambda e: e.tensor_copy(dst[:, kc, c0:c0 + cw], st[:, 0:cw]), r=[Bst], w=[Bdst])
                    else:
                        A(lambda e: e.copy(dst[:, kc, c0:c0 + cw], st[:, 0:cw]), r=[Bst], w=[Bdst])
                    stg_i[0] += 1

        SL = []
        for i_ in range(2):
            SL.append(dict(xin=sb("xin%d" % i_, [128, D]), pin=sb("pin%d" % i_, [128, PLE]), xb=sb("xb%d" % i_, [128, D], BF16),
                           xT=sb("xT%d" % i_, [128, 8, 128], BF16), pb16=sb("pb16_%d" % i_, [128, PLE], BF16),
                           pT=sb("pT%d" % i_, [128, 2, 128], BF16)))
        zt, Bzt = sb("zt", [128, D])
        stg = [SL[0]["xin"], SL[1]["xin"], (zt, Bzt)]
        st6, Bst6 = sb("st6", [128, 2, 6]); mv, Bmv = sb("mv", [128, 2]); rstd, Brstd = sb("rstd", [128, 1])
        ocb, Bocb = sb("ocb", [128, D], BF16)
        Bd2d = Buf("d2d")
        for i_ in range(2):
            for t_, b_ in [SL[i_]["xin"], SL[i_]["pin"]]:
                V(lambda e: e.memset(t_[:], 0.0), w=[b_])

        def interleave(*gens, seq=False, pattern=None):
            gens = [g for g in gens if g is not None]
            if pattern is not None and not seq:
                alive = [True] * len(gens)
                for ch in pattern:
                    gi = int(ch)
                    if alive[gi]:
                        try:
                            next(gens[gi])
                        except StopIteration:
                            alive[gi] = False
                gens = [g for g, a in zip(gens, alive) if a]
            gens = [g for g in gens if g is not None]
            if seq:
                for g in gens:
                    for _ in g:
                        pass
                return
            while gens:
                for g in list(gens):
                    try:
                        next(g)
                    except StopIteration:
                        gens.remove(g)

        def transposes(n, src_fn, rbufs, dst, Bdst, dst_sl=None):
            for i in range(n):
                PE(lambda e: e.transpose(PT[:, i, :], src_fn(i), ident_b[:]), r=rbufs + [Bidb], w=[BPT])
            if dst_sl is None:
                V(lambda e: e.tensor_copy(dst[:, 0:n, :], PT[:, 0:n, :]), r=[BPT], w=[Bdst])
            else:
                V(lambda e: e.tensor_copy(dst_sl, PT[:, 0:n, :]), r=[BPT], w=[Bdst])

        def load_tile(l, mode, idx, S):
            xin, Bxin = S["xin"]; pin, Bpin = S["pin"]; xb, Bxb = S["xb"]; xT, BxT = S["xT"]; pb16, Bpb16 = S["pb16"]; pT, BpT = S["pT"]
            if mode == 'p':
                src = xp if l == 0 else hP[(l - 1) % 2]
                rb = [] if l == 0 else [BhP[(l - 1) % 2]]
                K.dma('s', xin[:], src[idx * 128:(idx + 1) * 128, :], Bxin, r=rb, w=[Bxin])
                K.dma('s', pin[:], ppd[l, idx * 128:(idx + 1) * 128, :], Bpin, w=[Bpin])
            else:
                src = xs if l == 0 else hS[(l - 1) % 2]
                rb = [] if l == 0 else [BhS[(l - 1) % 2]]
                for q in range(4):
                    K.dma('s', xin[32 * q:32 * q + 8, :], src[4 * idx + q], Bxin, r=rb, w=[Bxin])
                    K.dma('s', pin[32 * q:32 * q + 8, :], psd[l, 4 * idx + q], Bpin, w=[Bpin])
            yield
            A(lambda e: e.copy(xb[:], xin[:]), r=[Bxin], w=[Bxb])
            transposes(8, lambda i: xb[:, i * 128:(i + 1) * 128], [Bxb], xT, BxT)
            yield
            A(lambda e: e.copy(pb16[:], pin[:]), r=[Bpin], w=[Bpb16])
            transposes(2, lambda i: pb16[:, i * 128:(i + 1) * 128], [Bpb16], pT, BpT)
            yield

        def out_proj_post(l, mode, idx, S):
            xin, Bxin = S["xin"]; pT, BpT = S["pT"]
            hnb, Bhnb = S["xb"]; hnT, BhnT = S["xT"]; oT, BoT = S["xT"]
            transposes(8, lambda i: ocb[:, i * 128:(i + 1) * 128], [Bocb], oT, BoT)
            yield
            for n in range(2):
                for kc in range(8):
                    PE(lambda e: e.matmul(pb(1 + n), oT[:, kc, :], wout[:, kc, n * 512:(n + 1) * 512], start=(kc == 0), stop=(kc == 7)),
                       r=[BoT, Bwout], w=[BP[1 + n]])
                V(lambda e: e.scalar_tensor_tensor(zt[:, n * 512:(n + 1) * 512], xin[:, n * 512:(n + 1) * 512], ALPHA, pb(1 + n), ALU.mult, ALU.add),
                  r=[Bxin, BP[1 + n]], w=[Bzt])
            yield
            for n in range(2):
                V(lambda e: e.bn_stats(st6[:, n, :], zt[:, n * 512:(n + 1) * 512]), r=[Bzt], w=[Bst6])
            V(lambda e: e.bn_aggr(mv[:], st6[:]), r=[Bst6], w=[Bmv])
            A(lambda e: e.activation(rstd[:], mv[:, 1:2], AF.Ln, bias=EPS), r=[Bmv], w=[Brstd])
            A(lambda e: e.activation(rstd[:], rstd[:], AF.Exp, scale=-0.5), r=[Brstd], w=[Brstd])
            V(lambda e: e.tensor_scalar(zt[:], zt[:], mv[:, 0:1], rstd[:, 0:1], ALU.subtract, ALU.mult), r=[Bzt, Bmv, Brstd], w=[Bzt])
            V(lambda e: e.tensor_tensor(zt[:], zt[:], gbc[:], ALU.mult), r=[Bzt, Bgbc], w=[Bzt])
            V(lambda e: e.tensor_tensor(zt[:], zt[:], bbc[:], ALU.add), r=[Bzt, Bbbc], w=[Bzt])
            A(lambda e: e.copy(hnb[:], zt[:]), r=[Bzt], w=[Bhnb])
            yield
            transposes(8, lambda i: hnb[:, i * 128:(i + 1) * 128], [Bhnb], hnT, BhnT)
            yield
            for n in range(2):
                for kc in range(8):
                    PE(lambda e: e.matmul(pb(3 + n), hnT[:, kc, :], wgate[:, kc, n * 512:(n + 1) * 512], start=(kc == 0), stop=(kc == 7)),
                       r=[BhnT, Bwgate], w=[BP[3 + n]])
                A(lambda e: e.activation(xin[:, n * 512:(n + 1) * 512], pb(3 + n), AF.Sigmoid), r=[BP[3 + n]], w=[Bxin])
            yield
            for n in range(2):
                for kc in range(2):
                    PE(lambda e: e.matmul(pb(5 + n), pT[:, kc, :], wpp[:, kc, n * 512:(n + 1) * 512], start=(kc == 0), stop=(kc == 1)),
                       r=[BpT, Bwpp], w=[BP[5 + n]])
                V(lambda e: e.tensor_tensor(xin[:, n * 512:(n + 1) * 512], xin[:, n * 512:(n + 1) * 512], pb(5 + n), ALU.mult),
                  r=[Bxin, BP[5 + n]], w=[Bxin])
            V(lambda e: e.tensor_tensor(xin[:], xin[:], zt[:], ALU.add), r=[Bxin, Bzt], w=[Bxin])
            last = (l == NL - 1)
            if mode == 'p':
                dst = yp if last else hP[l % 2]
                wb = [] if last else [BhP[l % 2]]
                K.dma('s', dst[idx * 128:(idx + 1) * 128, :], xin[:], Bxin, r=[Bxin], w=wb)
            else:
                dst = ys if last else hS[l % 2]
                wb = [] if last else [BhS[l % 2]]
                for q in range(4):
                    K.dma('s', dst[4 * idx + q], xin[32 * q:32 * q + 8, :], Bxin, r=[Bxin], w=wb)
            yield

        def make_even(sbl):
          win, Bwin = sbl("win", [128, 8, EVEN_IN], BF16)
          cv, Bcv = sbl("cv", [128, 12, 132], BF16)
          cvo, Bcvo = sbl("cvo", [128, 12, 4, 3]); cst, Bcst = sbl("cst", [128, 12, 4, 3])
          diagw, Bdiagw = sbl("diagw", [128, 12, 4, 128], BF16); cwt, Bcwt = sbl("cwt", [128, 12, 4])
          gna, Bgna = sbl("gna", [128, 512]); sgb, Bsgb = sbl("sgb", [128, 512])
          qkf, Bqkf = sbl("qkf", [128, 10, 64]); vf, Bvf = sbl("vf", [128, 128]); ab, Bab = sbl("ab", [128, 8])
          rt1, Brt1 = sbl("rt1", [128, 10, 16]); rt2, Brt2 = sbl("rt2", [128, 10, 16]); rope, Brope = sbl("rope", [128, 32])
          qkT, BqkT = sbl("qkT", [128, 8, 128])
          rn, Brn = sbl("rn", [128, 8, 128]); sqt, Bsqt = rn, Brn
          qkn, Bqkn = sbl("qkn", [128, 8, 128], BF16); vTb, BvTb = sbl("vTb", [128, 4, 128], BF16)
          vtok, Bvtok = sbl("vtok", [128, 4, 128], BF16); keg, Bkeg = sbl("keg", [128, 4, 128], BF16)
          kdec, Bkdec = sbl("kdec", [128, 4, 128], BF16)
          sc4 = {}
          for nm in ["zz", "gg", "beta", "nbeta", "Gs", "eG", "ekd", "osc", "tmp4", "ss4", "rs4", "mx4", "negm", "rsum", "es4", "den4"]:
              sc4[nm] = sbl("s4_" + nm, [128, 4])
          gq, Bgq = sbl("gq", [128, 4, 4]); eglb, Beglb = sbl("eglb", [128, 4, 4])
          negA, BnegA = sbl("negA", [128, 4]); dtb, Bdtb = sbl("dtb", [128, 4]); nabc, Bnabc = sbl("nabc", [128, 128])
          sinkbc, Bsinkbc = sbl("sinkbc", [128, 8]); rden, Brden = sbl("rden", [128, 8])
          gU, BgU = sbl("gU", [128, 4, 128]); ngU, BngU = sbl("ngU", [128, 4, 128])
          Ef, BEf = sbl("Ef", [128, 4, 128]); Es, BEs = sbl("Es", [128, 4, 128])
          big1, Bbig1 = sbl("big1", [128, 4, 256])
          Xm, BXm = big1[:, :, 0:128], Bbig1
          XTm, BXTm = big1[:, :, 128:256], Bbig1
          sm, Bsm = rn[:].rearrange("p a c -> p (a c)").rearrange("p (a c) -> p a c", a=4), Brn
          Pm, BPm = sbl("Pm", [128, 4, 128])
          Pb, BPb = sbl("Pb", [128, 4, 128], BF16); QKT, BQKT = sbl("QKT", [128, 4, 128], BF16)
          ub, Bub = sbl("ub", [128, 4, 128]); wT, BwT = sbl("wT", [128, 4, 128], BF16)
          vnew, Bvnew = sbl("vnew", [128, 4, 128], BF16)
          tq, Btq = gU, BgU
          o2, Bo2 = ngU, BngU
          oA, BoA = Ef, BEf
          tS, BtS = Pm, BPm
          SfAs = [sbl("SfA%d" % i, [128, 4, 128]) for i in range(2)]; SbAs = [sbl("SbA%d" % i, [128, 4, 128], BF16) for i in range(2)]
          qb16, Bqb16 = sbl("qb16", [128, 4, 2, 64], BF16)
          kcur = [sbl("kcur0", [128, 128], BF16)]
          kTs = [sbl("kTs%d" % i, [128, 128], BF16) for i in range(2)]
          v16 = [sbl("v16_%d" % i, [128, 128], BF16) for i in range(2)]
          qT, BqT = sbl("qT", [128, 4, 128], BF16)
          ztv = zt[:].rearrange("p (a c) -> p a c", a=8)
          kcf, Bkcf = ztv[:, 0:4, :], Bzt
          vcf, Bvcf = ztv[:, 4:8, :], Bzt
          kc16, Bkc16 = sbl("kc16", [128, 4, 128], BF16); vc16, Bvc16 = sbl("vc16", [128, 4, 128], BF16)
          kTc, BkTc = sbl("kTc", [128, 4, 128], BF16)
          pr, Bpr = sbl("pr", [128, 4, 256], BF16)
          prT, BprT = sbl("prT", [128, 8, 128], BF16); ob, Bob = sbl("ob", [128, 8, 64])
          for t_, b_ in [(v16[0][0], v16[0][1]), (v16[1][0], v16[1][1]), (kTs[0][0], kTs[0][1]), (kTs[1][0], kTs[1][1])]:
              V(lambda e: e.memset(t_[:], 0.0), w=[b_])

          def setup(j):
                load_w(win, Bwin, w_in_even[j], D, EVEN_IN)
                load_w(wout, Bwout, w_out_even[j], D, D)
                K.dma('s', negA[:], a_log[j:j + 1, :].partition_broadcast(128), BnegA, w=[BnegA])
                A(lambda e: e.activation(negA[:], negA[:], AF.Exp), r=[BnegA], w=[BnegA])
                V(lambda e: e.tensor_scalar_mul(negA[:], negA[:], -1.0), r=[BnegA], w=[BnegA])
                K.dma('s', dtb[:], dt_bias[j:j + 1, :].partition_broadcast(128), Bdtb, w=[Bdtb])
                K.dma('s', nabc[:], norm_a[j:j + 1, :].partition_broadcast(128), Bnabc, w=[Bnabc])
                K.dma('s', sinkbc[:], sinks[j:j + 1, :].partition_broadcast(128), Bsinkbc, w=[Bsinkbc])
                for jt in range(4):
                    K.dma('s', cwt[:, :, jt], conv_w[j, jt].rearrange("(m c) -> c m", c=128), Bcwt, w=[Bcwt], allow_slow_non_contiguous=True)
                for m in range(12):
                    for jt in range(4):
                        V(lambda e: e.tensor_scalar_mul(diagw[:, m, jt, :], ident_f[:], cwt[:, m, jt:jt + 1]), r=[Bidf, Bcwt], w=[Bdiagw])

          def even_tile(l, mode, idx, S, S2):
            j = l // 2
            vcol = 0 if mode == 'p' else 1
            lastp = (mode == 'p' and idx == NT - 1)
            xT, BxT = S["xT"]
            crow = gU[:].rearrange("p a c -> p (a c)")
            crow_o = ngU[:].rearrange("p a c -> p (a c)")

            def Pre():
                yield from load_tile(l, mode, idx, S)

            def Pg():
                K.dma('s', rope[:], c_rope[idx if mode == 'p' else NT], Brope, w=[Brope])
                if mode == 'p':
                    if idx == 0:
                        V(lambda e: e.memset(cv[:, :, 0:3], 0.0), w=[Bcv])
                    else:
                        V(lambda e: e.tensor_copy(cv[:, :, 0:3], cv[:, :, 128:131]), r=[Bcv], w=[Bcv])
                else:
                    for bk in range(3):
                        K.dma('s', crow[0:12, :], conv_s[j, 4 * idx:4 * idx + 4, :, bk * 512:(bk + 1) * 512].rearrange("q r c -> (q r) c"), BgU, w=[BgU])
                        for mm in range(4):
                            m = 4 * bk + mm
                            PE(lambda e: e.matmul(pb(7)[:, m * 12:(m + 1) * 12], crow[0:12, mm * 128:(mm + 1) * 128], ident_f[0:12, 0:12],
                                                  start=True, stop=True), r=[BgU, Bidf], w=[BP[7]])
                    V(lambda e: e.tensor_copy(cst[:].rearrange("p m q r -> p (m q r)"), pb(7)[:, 0:144]), r=[BP[7]], w=[Bcst])
                yield
                qlist = [0] if mode == 'p' else [0, 1, 2, 3]
                for bk in range(3):
                    for mm in range(4):
                        m = 4 * bk + mm
                        for kc in range(8):
                            PE(lambda e: e.matmul(pb4(1 + bk)[:, mm, :], win[:, kc, m * 128:(m + 1) * 128], xT[:, kc, :], start=(kc == 0), stop=(kc == 7)),
                               r=[Bwin, BxT], w=[BP[1 + bk]])
                    A(lambda e: e.copy(cv[:, 4 * bk:4 * bk + 4, 3:131], pb4(1 + bk)), r=[BP[1 + bk]], w=[Bcv])
                    if lastp or mode == 's':
                        V(lambda e: e.tensor_copy(ub[:], pb4(1 + bk)), r=[BP[1 + bk]], w=[Bub])
                        for mm in range(4):
                            for q in qlist:
                                t0_ = 125 if mode == 'p' else 32 * q + 5
                                PE(lambda e: e.matmul(pb(7)[32 * q:32 * q + 3, mm * 128:(mm + 1) * 128], ub[:, mm, t0_:t0_ + 3], ident_f[:],
                                                      start=True, stop=True, tile_position=(0, 32 * q)), r=[Bub, Bidf], w=[BP[7]])
                        V(lambda e: e.tensor_copy(crow_o, pb(7)), r=[BP[7]], w=[BngU])
                        for q in qlist:
                            dst_ = conv_p[j, :, bk * 512:(bk + 1) * 512] if mode == 'p' else conv_so[j, 4 * idx + q, :, bk * 512:(bk + 1) * 512]
                            K.dma('s', dst_, crow_o[32 * q:32 * q + 3, :], BngU, r=[BngU])
                    yield
                if mode == 's':
                    for q in range(4):
                        V(lambda e: e.tensor_copy(cv[:, :, 32 * q:32 * q + 3], cst[:, :, q, :]), r=[Bcst, Bcv], w=[Bcv])
                def tm(b, o0, c0, n):
                    for kc in range(8):
                        PE(lambda e: e.matmul(pb(b)[:, o0:o0 + n], xT[:, kc, :], win[:, kc, c0:c0 + n], start=(kc == 0), stop=(kc == 7)),
                           r=[Bwin, BxT], w=[BP[b]])
                tm(4, 0, 1536, 512)
                A(lambda e: e.activation(gna[:], pb(4), AF.Silu), r=[BP[4]], w=[Bgna])
                V(lambda e: e.tensor_tensor(gna[:].rearrange("p (a c) -> p a c", a=4), gna[:].rearrange("p (a c) -> p a c", a=4),
                                            nabc[:].unsqueeze(1).to_broadcast([128, 4, 128]), ALU.mult), r=[Bgna, Bnabc], w=[Bgna])
                yield
                tm(5, 0, 2056, 512)
                V(lambda e: e.tensor_copy(qkf[:, 0:8, :], pb(5).rearrange("p (a c) -> p a c", a=8)), r=[BP[5]], w=[Bqkf])
                yield
                tm(6, 0, 2568, 256)
                tm(6, 256, 2048, 8)
                V(lambda e: e.tensor_copy(qkf[:, 8:10, :], pb(6)[:, 0:128].rearrange("p (a c) -> p a c", a=2)), r=[BP[6]], w=[Bqkf])
                V(lambda e: e.tensor_copy(vf[:], pb(6)[:, 128:256]), r=[BP[6]], w=[Bvf])
                V(lambda e: e.tensor_copy(ab[:], pb(6)[:, 256:264]), r=[BP[6]], w=[Bab])
                yield
                tm(7, 0, 2824, 512)
                A(lambda e: e.activation(sgb[:], pb(7), AF.Silu), r=[BP[7]], w=[Bsgb])
                yield
                for bk in range(3):
                    for mm in range(4):
                        m = 4 * bk + mm
                        for jt in range(4):
                            PE(lambda e: e.matmul(pb4(1 + bk)[:, mm, :], diagw[:, m, jt, :], cv[:, m, jt:jt + 128], start=(jt == 0), stop=(jt == 3)),
                               r=[Bdiagw, Bcv], w=[BP[1 + bk]])
                    if bk < 2:
                        A(lambda e: e.activation(qkT[:, 4 * bk:4 * bk + 4, :], pb4(1 + bk), AF.Silu), r=[BP[1 + bk]], w=[BqkT])
                    else:
                        A(lambda e: e.activation(vTb[:], pb4(3), AF.Silu), r=[BP[3]], w=[BvTb])
                    yield

            def Dg():
                zz, Bzz = sc4["zz"]; gg, Bgg = sc4["gg"]; beta, Bbeta = sc4["beta"]; nbeta, Bnbeta = sc4["nbeta"]
                Gs, BGs = sc4["Gs"]; eG, BeG = sc4["eG"]; ekd, Bekd = sc4["ekd"]; osc, Bosc = sc4["osc"]; tmp4, Btmp4 = sc4["tmp4"]
                def Da():
                    V(lambda e: e.tensor_tensor(sqt[:], qkT[:], qkT[:], ALU.mult), r=[BqkT], w=[Bsqt])
                    for b in range(2):
                        PE(lambda e: e.matmul(pb(1 + b), ones_f[:], sqt[:, 4 * b:4 * b + 4, :].rearrange("p a c -> p (a c)"), start=True, stop=True),
                           r=[Bon, Bsqt], w=[BP[1 + b]])
                    for b in range(2):
                        A(lambda e: e.activation(rn[:, 4 * b:4 * b + 4, :], pb4(1 + b), AF.Ln, bias=EPS), r=[BP[1 + b]], w=[Brn])
                    A(lambda e: e.activation(rn[:], rn[:], AF.Exp, scale=-0.5), r=[Brn], w=[Brn])
                    V(lambda e: e.tensor_tensor(qkn[:], qkT[:], rn[:], ALU.mult), r=[BqkT, Brn], w=[Bqkn])
                    yield
                    transposes(4, lambda i: vTb[:, i, :], [BvTb], vtok, Bvtok)
                    yield

                def Db():
                    V(lambda e: e.tensor_tensor(zz[:], ab[:, 0:4], dtb[:], ALU.add), r=[Bab, Bdtb], w=[Bzz])
                    A(lambda e: e.activation(zz[:], zz[:], AF.Exp), r=[Bzz], w=[Bzz])
                    A(lambda e: e.activation(zz[:], zz[:], AF.Ln, bias=1.0), r=[Bzz], w=[Bzz])
                    V(lambda e: e.scalar_tensor_tensor(gg[:], zz[:], valid[:, vcol:vcol + 1], negA[:], ALU.mult, ALU.mult), r=[Bzz, Bval, BnegA], w=[Bgg])
                    A(lambda e: e.activation(beta[:], ab[:, 4:8], AF.Exp, scale=-1.0), r=[Bab], w=[Bbeta])
                    V(lambda e: e.tensor_scalar_add(beta[:], beta[:], 1.0), r=[Bbeta], w=[Bbeta])
                    V(lambda e: e.reciprocal(beta[:], beta[:]), r=[Bbeta], w=[Bbeta])
                    V(lambda e: e.tensor_scalar_mul(beta[:], beta[:], valid[:, vcol:vcol + 1]), r=[Bbeta, Bval], w=[Bbeta])
                    V(lambda e: e.tensor_scalar_mul(nbeta[:], beta[:], -1.0), r=[Bbeta], w=[Bnbeta])
                    V(lambda e: e.tensor_tensor(gq[:], gg[:].unsqueeze(1).to_broadcast([128, 4, 4]), ind[:].unsqueeze(2).to_broadcast([128, 4, 4]), ALU.mult),
                      r=[Bgg, Bind], w=[Bgq])
                    PE(lambda e: e.matmul(pb(6)[:, 0:4], maskC[:], gg[:], start=True, stop=True), r=[BmC, Bgg], w=[BP[6]])
                    PE(lambda e: e.matmul(pb(6)[:, 4:8], bones[:], gg[:], start=True, stop=True), r=[Bbo, Bgg], w=[BP[6]])
                    PE(lambda e: e.matmul(pb(6)[:, 8:24], ones_f[:], gq[:].rearrange("p a c -> p (a c)"), start=True, stop=True), r=[Bon, Bgq], w=[BP[6]])
                    V(lambda e: e.tensor_copy(Gs[:], pb(6)[:, 0:4]), r=[BP[6]], w=[BGs])
                    A(lambda e: e.activation(eG[:], pb(6)[:, 0:4], AF.Exp), r=[BP[6]], w=[BeG])
                    V(lambda e: e.tensor_tensor(tmp4[:], pb(6)[:, 4:8], Gs[:], ALU.subtract), r=[BP[6], BGs], w=[Btmp4])
                    A(lambda e: e.activation(ekd[:], tmp4[:], AF.Exp), r=[Btmp4], w=[Bekd])
                    A(lambda e: e.activation(eglb[:].rearrange("p a c -> p (a c)"), pb(6)[:, 8:24], AF.Exp), r=[BP[6]], w=[Beglb])
                    V(lambda e: e.tensor_scalar_mul(osc[:], eG[:], float(128 ** -0.5)), r=[BeG], w=[Bosc])
                    yield
                    V(lambda e: e.tensor_tensor(gU[:], maskC[:].unsqueeze(1).to_broadcast([128, 4, 128]), gg[:].unsqueeze(2).to_broadcast([128, 4, 128]), ALU.mult),
                      r=[BmC, Bgg], w=[BgU])
                    A(lambda e: e.mul(ngU[:], gU[:], -1.0), r=[BgU], w=[BngU])
                    for h in range(4):
                        PE(lambda e: e.matmul(pb4(7)[:, h, :], ones_f[:], gU[:, h, :], start=True, stop=False), r=[Bon, BgU], w=[BP[7]])
                        PE(lambda e: e.matmul(pb4(7)[:, h, :], ngU[:, h, :], ones_f[:], start=False, stop=False), r=[Bon, BngU], w=[BP[7]])
                        PE(lambda e: e.matmul(pb4(7)[:, h, :], ident_f[:], mnegC[:], start=False, stop=True), r=[Bidf, Bmn], w=[BP[7]])
                    A(lambda e: e.activation(Ef[:], pb4(7), AF.Exp), r=[BP[7]], w=[BEf])
                    V(lambda e: e.tensor_tensor(Es[:], Ef[:], maskS[:].unsqueeze(1).to_broadcast([128, 4, 128]), ALU.mult), r=[BEf, BmS], w=[BEs])
                    yield

                yield
                subs = [Da(), Db()]
                while subs:
                    for g_ in list(subs):
                        try:
                            next(g_)
                        except StopIteration:
                            subs.remove(g_)
                    yield
                for h in range(4):
                    PE(lambda e: e.transpose(PT[:, h, :], qkn[:, 4 + h, :], ident_b[:]), r=[Bqkn, Bidb], w=[BPT])
                V(lambda e: e.tensor_tensor(keg[:], PT[:, 0:4, :], eG[:].unsqueeze(2).to_broadcast([128, 4, 128]), ALU.mult), r=[BPT, BeG], w=[Bkeg])
                V(lambda e: e.tensor_tensor(kdec[:], PT[:, 0:4, :], ekd[:].unsqueeze(2).to_broadcast([128, 4, 128]), ALU.mult), r=[BPT, Bekd], w=[Bkdec])
                yield
                for h in range(4):
                    PE(lambda e: e.matmul(pb4(4)[:, h, :], qkn[:, 4 + h, :], qkn[:, 4 + h, :], start=True, stop=True), r=[Bqkn], w=[BP[4]])
                    PE(lambda e: e.matmul(pb4(5)[:, h, :], qkn[:, 4 + h, :], qkn[:, h, :], start=True, stop=True), r=[Bqkn], w=[BP[5]])
                V(lambda e: e.tensor_tensor(Xm[:], pb4(4), Es[:], ALU.mult), r=[BP[4], BEs], w=[BXm])
                V(lambda e: e.tensor_tensor(Xm[:], Xm[:], beta[:].unsqueeze(2).to_broadcast([128, 4, 128]), ALU.mult), r=[BXm, Bbeta], w=[BXm])
                V(lambda e: e.scalar_tensor_tensor(QKT[:], pb4(5), float(128 ** -0.5), Ef[:], ALU.mult, ALU.mult), r=[BP[5], BEf], w=[BQKT])
                yield
                for h in range(4):
                    PE(lambda e: e.transpose(pb4(4)[:, h, :], Xm[:, h, :], ident_f[:]), r=[BXm, Bidf], w=[BP[4]])
                V(lambda e: e.tensor_copy(XTm[:], pb4(4)), r=[BP[4]], w=[BXTm])
                V(lambda e: e.tensor_tensor(Pm[:], ident_f[:].unsqueeze(1).to_broadcast([128, 4, 128]), Xm[:], ALU.subtract), r=[Bidf, BXm], w=[BPm])
                for lev in range(4):
                    for h in range(4):
                        PE(lambda e: e.matmul(pb4(5)[:, h, :], XTm[:, h, :], Xm[:, h, :], start=True, stop=True), r=[BXm, BXTm], w=[BP[5]])
                        PE(lambda e: e.matmul(pb4(6)[:, h, :], Xm[:, h, :], XTm[:, h, :], start=True, stop=True), r=[BXm, BXTm], w=[BP[6]])
                    yield
                    V(lambda e: e.tensor_copy(Xm[:], pb4(5)), r=[BP[5]], w=[BXm])
                    A(lambda e: e.copy(XTm[:], pb4(6)), r=[BP[6]], w=[BXTm])
                    for h in range(4):
                        PE(lambda e: e.matmul(pb4(4)[:, h, :], XTm[:, h, :], Pm[:, h, :], start=True, stop=True), r=[BXTm, BPm], w=[BP[4]])
                    yield
                    V(lambda e: e.tensor_tensor(Pm[:], Pm[:], pb4(4), ALU.add), r=[BPm, BP[4]], w=[BPm])
                    yield
                A(lambda e: e.copy(Pb[:], Pm[:]), r=[BPm], w=[BPb])
                yield
                for h in range(4):
                    PE(lambda e: e.matmul(pb4(5)[:, h, :], Pb[:, h, :], vtok[:, h, :], start=True, stop=True), r=[BPb, Bvtok], w=[BP[5]])
                    PE(lambda e: e.matmul(pb4(6)[:, h, :], keg[:, h, :], Pb[:, h, :], start=True, stop=True), r=[BPb, Bkeg], w=[BP[6]])
                V(lambda e: e.tensor_tensor(ub[:], pb4(5), beta[:].unsqueeze(2).to_broadcast([128, 4, 128]), ALU.mult), r=[BP[5], Bbeta], w=[Bub])
                A(lambda e: e.copy(wT[:], pb4(6)), r=[BP[6]], w=[BwT])
                yield
                SfA, BSfA = SfAs[0]; SbA, BSbA = SbAs[0]
                if mode == 'p' and idx == 0:
                    V(lambda e: e.memset(SfA[:], 0.0), w=[BSfA])
                    V(lambda e: e.memset(SbA[:], 0.0), w=[BSbA])
                if mode == 's':
                    K.dma('s', SfAs[0][0][:], delta_s[j, 4 * idx].rearrange("h d e -> d h e"), SfAs[0][1], w=[SfAs[0][1]])
                for q in range(4):
                    qs = slice(32 * q, 32 * q + 32)
                    if mode == 's':
                        SfA, BSfA = SfAs[q % 2]; SbA, BSbA = SbAs[q % 2]
                        if q < 3:
                            K.dma('s', SfAs[(q + 1) % 2][0][:], delta_s[j, 4 * idx + q + 1].rearrange("h d e -> d h e"), SfAs[(q + 1) % 2][1],
                                  w=[SfAs[(q + 1) % 2][1]])
                        A(lambda e: e.copy(SbA[:], SfA[:]), r=[BSfA], w=[BSbA])
                    for h in range(4):
                        PE(lambda e: e.matmul(pb4(4)[qs, h, :], wT[:, h, qs], SbA[:, h, :], start=True, stop=True, tile_position=(0, 32 * q)),
                           r=[BwT, BSbA], w=[BP[4]])
                        PE(lambda e: e.matmul(pb4(7)[qs, h, :], qkn[:, h, qs], SbA[:, h, :], start=True, stop=True, tile_position=(0, 32 * q)),
                           r=[Bqkn, BSbA], w=[BP[7]])
                    V(lambda e: e.tensor_tensor(tS[:], SfA[:], eglb[:, q, :].unsqueeze(2).to_broadcast([128, 4, 128]), ALU.mult), r=[BSfA, Beglb], w=[BtS])
                    yield
                    V(lambda e: e.tensor_tensor(tq[qs], pb4(4)[qs], nbeta[qs].unsqueeze(2).to_broadcast([32, 4, 128]), ALU.mult), r=[BP[4], Bnbeta], w=[Btq])
                    V(lambda e: e.tensor_tensor(vnew[qs], tq[qs], ub[qs], ALU.add), r=[Btq, Bub], w=[Bvnew])
                    for h in range(4):
                        PE(lambda e: e.matmul(pb4(5)[:, h, :], kdec[qs, h, :], vnew[qs, h, :], start=True, stop=True, tile_position=(32 * q, 0)),
                           r=[Bkdec, Bvnew], w=[BP[5]])
                    yield
                    if mode == 'p':
                        V(lambda e: e.tensor_tensor(SbA[:], tS[:], pb4(5), ALU.add), r=[BtS, BP[5]], w=[BSbA])
                    V(lambda e: e.tensor_tensor(SfA[:], tS[:], pb4(5), ALU.add), r=[BtS, BP[5]], w=[BSfA])
                    if mode == 's':
                        K.dma('s', delta_so[j, 4 * idx + q].rearrange("h d e -> d h e"), SfA[:], BSfA, r=[BSfA])
                if lastp:
                    K.dma('s', delta_p[j].rearrange("h d e -> d h e"), SfA[:], BSfA, r=[BSfA])
                yield
                yield
                for h in range(4):
                    PE(lambda e: e.matmul(pb4(6)[:, h, :], QKT[:, h, :], vnew[:, h, :], start=True, stop=True), r=[BQKT, Bvnew], w=[BP[6]])
                V(lambda e: e.tensor_tensor(o2[:], pb4(7), osc[:].unsqueeze(2).to_broadcast([128, 4, 128]), ALU.mult), r=[BP[7], Bosc], w=[Bo2])
                V(lambda e: e.tensor_tensor(oA[:], o2[:], pb4(6), ALU.add), r=[Bo2, BP[6]], w=[BoA])
                yield
                ss4, Bss4 = sc4["ss4"]; rs4, Brs4 = sc4["rs4"]
                V(lambda e: e.tensor_tensor(o2[:], oA[:], oA[:], ALU.mult), r=[BoA], w=[Bo2])
                V(lambda e: e.tensor_reduce(ss4[:], o2[:], AX.X, ALU.add), r=[Bo2], w=[Bss4])
                A(lambda e: e.activation(rs4[:], ss4[:], AF.Ln, bias=EPS, scale=1.0 / 128), r=[Bss4], w=[Brs4])
                A(lambda e: e.activation(rs4[:], rs4[:], AF.Exp, scale=-0.5), r=[Brs4], w=[Brs4])
                V(lambda e: e.tensor_tensor(oA[:], oA[:], rs4[:].unsqueeze(2).to_broadcast([128, 4, 128]), ALU.mult), r=[BoA, Brs4], w=[BoA])
                V(lambda e: e.tensor_tensor(ocb[:, 0:512].rearrange("p (a c) -> p a c", a=4), oA[:], gna[:].rearrange("p (a c) -> p a c", a=4), ALU.mult),
                  r=[BoA, Bgna], w=[Bocb])


            def Sg():
                ridx = idx if mode == 'p' else NT
                V(lambda e: e.tensor_tensor(rt1[:], qkf[:, :, 0:16], rope[:, 0:16].unsqueeze(1).to_broadcast([128, 10, 16]), ALU.mult),
                  r=[Bqkf, Brope], w=[Brt1])
                V(lambda e: e.tensor_tensor(rt2[:, :, 0:8], qkf[:, :, 8:16], rope[:, 16:24].unsqueeze(1).to_broadcast([128, 10, 8]), ALU.mult),
                  r=[Bqkf, Brope], w=[Brt2])
                V(lambda e: e.tensor_tensor(rt2[:, :, 8:16], qkf[:, :, 0:8], rope[:, 24:32].unsqueeze(1).to_broadcast([128, 10, 8]), ALU.mult),
                  r=[Bqkf, Brope], w=[Brt2])
                V(lambda e: e.tensor_tensor(qkf[:, :, 0:16], rt1[:], rt2[:], ALU.add), r=[Brt1, Brt2], w=[Bqkf])
                yield
                cur = idx % 2 if mode == 'p' else 0
                prv = 1 - cur
                kc_t, Bkc_t = kcur[0]
                kT_cur, BkT_cur = kTs[cur]; kT_prv, BkT_prv = kTs[prv]
                v_cur, Bv_cur = v16[cur]; v_prv, Bv_prv = v16[prv]
                A(lambda e: e.copy(qb16[:], qkf[:, 0:8, :].rearrange("p (a c) d -> p c a d", a=2)), r=[Bqkf], w=[Bqb16])
                A(lambda e: e.copy(kc_t[:], qkf[:, 8:10, :].rearrange("p a c -> p (a c)")), r=[Bqkf], w=[Bkc_t])
                A(lambda e: e.copy(v_cur[:], vf[:]), r=[Bvf], w=[Bv_cur])
                if lastp:
                    K.dma('s', wk_p[j], qkf[:, 8:10, :].rearrange("p a c -> p (a c)"), Bqkf, r=[Bqkf])
                    K.dma('s', wv_p[j], vf[:], Bvf, r=[Bvf])
                if mode == 's':
                    for q in range(4):
                        sq_ = 4 * idx + q
                        K.dma('s', wk_so[j, sq_, 120:128, :], qkf[32 * q:32 * q + 8, 8:10, :].rearrange("p a c -> p (a c)"), Bqkf, r=[Bqkf])
                        K.dma('s', wv_so[j, sq_, 120:128, :], vf[32 * q:32 * q + 8, :], Bvf, r=[Bvf])
                        K.dma('s', wk_so[j, sq_, 0:120, :], wk_s[j, sq_, 8:128, :], Bd2d)
                        K.dma('s', wv_so[j, sq_, 0:120, :], wv_s[j, sq_, 8:128, :], Bd2d)
                        K.dma('s', kcf[:, q, :], wk_s[j, sq_], Bkcf, w=[Bkcf])
                        K.dma('s', vcf[:, q, :], wv_s[j, sq_], Bvcf, w=[Bvcf])
                    A(lambda e: e.copy(kc16[:], kcf[:]), r=[Bkcf], w=[Bkc16])
                    A(lambda e: e.copy(vc16[:], vcf[:]), r=[Bvcf], w=[Bvc16])
                    transposes(4, lambda i: kc16[:, i, :], [Bkc16], kTc, BkTc)
                yield
                transposes(4, lambda i: qb16[:, i, :, :].rearrange("p a d -> p (a d)"), [Bqb16], qT, BqT)
                PE(lambda e: e.transpose(PT[:, 0, :], kc_t[:], ident_b[:]), r=[Bkc_t, Bidb], w=[BPT])
                V(lambda e: e.tensor_copy(kT_cur[:], PT[:, 0, :]), r=[BPT], w=[BkT_cur])
                mi = 2 if mode == 's' else (1 if idx == 0 else 0)
                mx4, Bmx4 = sc4["mx4"]; negm, Bnegm = sc4["negm"]; rsum, Brsum = sc4["rsum"]; es4, Bes4 = sc4["es4"]; den4, Bden4 = sc4["den4"]
                for gi in range(2):
                    yield
                    ks = slice(64 * gi, 64 * gi + 64)
                    for c in range(4):
                        b = 1 + c // 2
                        o0 = (c % 2) * 256
                        if mode == 'p':
                            PE(lambda e: e.matmul(pb(b)[:, o0:o0 + 128], qT[ks, c, :], kT_prv[ks, :], start=True, stop=True, tile_position=(64 * gi, 0)),
                               r=[BqT, BkT_prv], w=[BP[b]])
                        else:
                            for q in range(4):
                                qs = slice(32 * q, 32 * q + 32)
                                PE(lambda e: e.matmul(pb(b)[qs, o0:o0 + 128], qT[ks, c, qs], kTc[ks, q, :], start=True, stop=True,
                                                      tile_position=(64 * gi, 32 * q)), r=[BqT, BkTc], w=[BP[b]])
                        PE(lambda e: e.matmul(pb(b)[:, o0 + 128:o0 + 256], qT[ks, c, :], kT_cur[ks, :], start=True, stop=True, tile_position=(64 * gi, 0)),
                           r=[BqT, BkT_cur], w=[BP[b]])
                    yield
                    for b2 in range(2):
                        V(lambda e: e.scalar_tensor_tensor(sm[:, 2 * b2:2 * b2 + 2, :], pb(1 + b2).rearrange("p (a c) -> p a c", a=2), 0.125,
                                                           swam[:, mi, :].unsqueeze(1).to_broadcast([128, 2, 256]), ALU.mult, ALU.add),
                          r=[BP[1 + b2], Bswam], w=[Bsm])
                    V(lambda e: e.tensor_reduce(mx4[:], sm[:], AX.X, ALU.max), r=[Bsm], w=[Bmx4])
                    V(lambda e: e.tensor_tensor(mx4[:], mx4[:], sinkbc[:, 4 * gi:4 * gi + 4], ALU.max), r=[Bmx4, Bsinkbc], w=[Bmx4])
                    V(lambda e: e.tensor_scalar_mul(negm[:], mx4[:], -1.0), r=[Bmx4], w=[Bnegm])
                    yield
                    for c in range(4):
                        A(lambda e: e.activation(pr[:, c, :], sm[:, c, :], AF.Exp, bias=negm[:, c:c + 1], accum_out=rsum[:, c:c + 1]),
                          r=[Bsm, Bnegm], w=[Bpr, Brsum])
                    V(lambda e: e.tensor_tensor(es4[:], sinkbc[:, 4 * gi:4 * gi + 4], mx4[:], ALU.subtract), r=[Bsinkbc, Bmx4], w=[Bes4])
                    A(lambda e: e.activation(es4[:], es4[:], AF.Exp), r=[Bes4], w=[Bes4])
                    V(lambda e: e.tensor_tensor(den4[:], rsum[:], es4[:], ALU.add), r=[Brsum, Bes4], w=[Bden4])
                    V(lambda e: e.reciprocal(rden[:, 4 * gi:4 * gi + 4], den4[:]), r=[Bden4], w=[Brden])
                    yield
                    for c in range(4):
                        PE(lambda e: e.transpose(PT[:, 2 * c, :], pr[:, c, 0:128], ident_b[:]), r=[Bpr, Bidb], w=[BPT])
                        PE(lambda e: e.transpose(PT[:, 2 * c + 1, :], pr[:, c, 128:256], ident_b[:]), r=[Bpr, Bidb], w=[BPT])
                    V(lambda e: e.tensor_copy(prT[:], PT[:]), r=[BPT], w=[BprT])
                    yield
                    for c in range(4):
                        h = 4 * gi + c
                        hs = slice(64 * h, 64 * h + 64)
                        if mode == 'p':
                            PE(lambda e: e.matmul(pb(3)[:, hs], prT[:, 2 * c, :], v_prv[:, ks], start=True, stop=False), r=[BprT, Bv_prv], w=[BP[3]])
                        else:
                            for q in range(4):
                                qs = slice(32 * q, 32 * q + 32)
                                PE(lambda e: e.matmul(pb(3)[qs, hs], prT[:, 2 * c, qs], vc16[:, q, ks], start=True, stop=False, tile_position=(0, 32 * q)),
                                   r=[BprT, Bvc16], w=[BP[3]])
                        PE(lambda e: e.matmul(pb(3)[:, hs], prT[:, 2 * c + 1, :], v_cur[:, ks], start=False, stop=True), r=[BprT, Bv_cur], w=[BP[3]])
                yield
                V(lambda e: e.tensor_tensor(ob[:], pb(3).rearrange("p (a c) -> p a c", a=8), rden[:].unsqueeze(2).to_broadcast([128, 8, 64]), ALU.mult),
                  r=[BP[3], Brden], w=[Bob])
                V(lambda e: e.tensor_tensor(ocb[:, 512:1024], ob[:].rearrange("p a c -> p (a c)"), sgb[:], ALU.mult), r=[Bob, Bsgb], w=[Bocb])

            def Eg():
                yield from out_proj_post(l, mode, idx, S)

            return Pre(), Pg(), [Dg(), Sg()], Eg()

          return setup, even_tile

        def make_odd(sbl):
          win, Bwin = sbl("win", [128, 8, ODD_IN], BF16)
          omlbc, Bomlbc = sbl("omlbc", [128, D]); ncbc, Bncbc = sbl("ncbc", [128, 128])
          kk, Bkk = sbl("kk", [128, D]); logf, Blogf = sbl("logf", [128, D])
          Gsb, BGsb = sbl("Gsb", [128, D])
          tmpD, BtmpD = sbl("tmpD", [128, D]); eGd, BeGd = sbl("eGd", [128, D])
          ktl, Bktl = sbl("ktl", [128, D], BF16); khat, Bkhat = sbl("khat", [128, D], BF16); qtl, Bqtl = sbl("qtl", [128, D], BF16)
          vtl, Bvtl = sbl("vtl", [128, D], BF16); gnc, Bgnc = sbl("gnc", [128, D])
          qTo, BqTo = sbl("qTo", [128, 8, 128], BF16); kTo, BkTo = sbl("kTo", [128, 8, 128], BF16)
          ATo, BATo = sbl("ATo", [128, 8, 128], BF16)
          eglo, Beglo = sbl("eglo", [128, 8, 4]); tSC, BtSC = sbl("tSC", [128, 8, 128])
          SfCs = [sbl("SfC%d" % i, [128, 8, 128]) for i in range(2)]; SbCs = [sbl("SbC%d" % i, [128, 8, 128], BF16) for i in range(2)]
          oC, BoC = sbl("oC", [128, 8, 128]); ss8, Bss8 = sbl("ss8", [128, 8]); rs8, Brs8 = sbl("rs8", [128, 8])

          def setup(j):
                load_w(win, Bwin, w_in_odd[j], D, ODD_IN)
                load_w(wout, Bwout, w_out_odd[j], D, D)
                K.dma('s', ncbc[:], norm_c[j:j + 1, :].partition_broadcast(128), Bncbc, w=[Bncbc])
                if j == 0:
                    V(lambda e: e.memset(omlbc[:], 1.0), w=[Bomlbc])
                else:
                    K.dma('s', tmpD[:], lb_raw[0:1, :].partition_broadcast(128), BtmpD, w=[BtmpD])
                    K.dma('s', eGd[:], lb_raw[1:2, :].partition_broadcast(128), BeGd, w=[BeGd])
                    V(lambda e: e.tensor_tensor(omlbc[:], tmpD[:], eGd[:], ALU.subtract), r=[BtmpD, BeGd], w=[Bomlbc])
                    A(lambda e: e.activation(omlbc[:], omlbc[:], AF.Sigmoid), r=[Bomlbc], w=[Bomlbc])

          def odd_tile(l, mode, idx, S, S2):
            j = l // 2
            vcol = 0 if mode == 'p' else 1
            lastp = (mode == 'p' and idx == NT - 1)
            xT, BxT = S["xT"]

            def Pre():
                yield from load_tile(l, mode, idx, S)

            def Pg():

                def proj(c0, b0):
                    for n in range(2):
                        for kc in range(8):
                            PE(lambda e: e.matmul(pb(b0 + n), xT[:, kc, :], win[:, kc, c0 + n * 512:c0 + (n + 1) * 512], start=(kc == 0), stop=(kc == 7)),
                               r=[Bwin, BxT], w=[BP[b0 + n]])

                H = lambda t_, n: t_[:, n * 512:(n + 1) * 512]
                yield
                proj(1024, 5)
                for n in range(2):
                    A(lambda e: e.activation(H(kk, n), pb(5 + n), AF.Sigmoid, scale=-1.0), r=[BP[5 + n]], w=[Bkk])
                V(lambda e: e.tensor_tensor(kk[:], kk[:], omlbc[:], ALU.mult), r=[Bkk, Bomlbc], w=[Bkk])
                A(lambda e: e.activation(logf[:], kk[:], AF.Ln, bias=1.0, scale=-1.0), r=[Bkk], w=[Blogf])
                if mode == 's':
                    V(lambda e: e.tensor_scalar_mul(logf[:], logf[:], valid[:, 1:2]), r=[Blogf, Bval], w=[Blogf])
                yield
                for n in range(2):
                    PE(lambda e: e.matmul(pb(1 + n), maskC[:], H(logf, n), start=True, stop=True), r=[BmC, Blogf], w=[BP[1 + n]])
                    PE(lambda e: e.matmul(pb(3 + n), bones[:], H(logf, n), start=True, stop=True), r=[Bbo, Blogf], w=[BP[3 + n]])
                for h in range(8):
                    PE(lambda e: e.matmul(pb(7)[:, 4 * h:4 * h + 4], logf[:, h * 128:(h + 1) * 128], ind[:], start=True, stop=True), r=[Blogf, Bind], w=[BP[7]])
                A(lambda e: e.activation(eglo[:].rearrange("p a c -> p (a c)"), pb(7)[:, 0:32], AF.Exp), r=[BP[7]], w=[Beglo])
                for n in range(2):
                    V(lambda e: e.tensor_copy(H(Gsb, n), pb(1 + n)), r=[BP[1 + n]], w=[BGsb])
                    A(lambda e: e.activation(H(tmpD, n), pb(1 + n), AF.Exp, scale=-1.0), r=[BP[1 + n]], w=[BtmpD])
                    A(lambda e: e.activation(H(eGd, n), pb(1 + n), AF.Exp), r=[BP[1 + n]], w=[BeGd])
                V(lambda e: e.tensor_tensor(ktl[:], kk[:], tmpD[:], ALU.mult), r=[Bkk, BtmpD], w=[Bktl])
                for n in range(2):
                    V(lambda e: e.tensor_tensor(H(tmpD, n), pb(3 + n), H(Gsb, n), ALU.subtract), r=[BP[3 + n], BGsb], w=[BtmpD])
                A(lambda e: e.activation(tmpD[:], tmpD[:], AF.Exp), r=[BtmpD], w=[BtmpD])
                V(lambda e: e.tensor_tensor(khat[:], kk[:], tmpD[:], ALU.mult), r=[Bkk, BtmpD], w=[Bkhat])
                yield
                proj(2048, 5)
                for n in range(2):
                    if mode == 's':
                        V(lambda e: e.tensor_scalar_mul(H(vtl, n), pb(5 + n), valid[:, 1:2]), r=[BP[5 + n], Bval], w=[Bvtl])
                    else:
                        V(lambda e: e.tensor_copy(H(vtl, n), pb(5 + n)), r=[BP[5 + n]], w=[Bvtl])
                yield
                proj(0, 1)
                for n in range(2):
                    A(lambda e: e.activation(H(tmpD, n), pb(1 + n), AF.Silu), r=[BP[1 + n]], w=[BtmpD])
                V(lambda e: e.scalar_tensor_tensor(qtl[:], tmpD[:], float(128 ** -0.5), eGd[:], ALU.mult, ALU.mult), r=[BtmpD, BeGd], w=[Bqtl])
                yield
                proj(3072, 3)
                for n in range(2):
                    A(lambda e: e.activation(H(gnc, n), pb(3 + n), AF.Silu), r=[BP[3 + n]], w=[Bgnc])
                V(lambda e: e.tensor_tensor(gnc[:].rearrange("p (a c) -> p a c", a=8), gnc[:].rearrange("p (a c) -> p a c", a=8),
                                            ncbc[:].unsqueeze(1).to_broadcast([128, 8, 128]), ALU.mult), r=[Bgnc, Bncbc], w=[Bgnc])
                yield
                transposes(8, lambda i: qtl[:, i * 128:(i + 1) * 128], [Bqtl], qTo, BqTo)
                yield
                transposes(8, lambda i: ktl[:, i * 128:(i + 1) * 128], [Bktl], kTo, BkTo)
                yield
                for h in range(8):
                    b = 5 + h // 4
                    PE(lambda e: e.matmul(pb4(b)[:, h % 4, :], kTo[:, h, :], qTo[:, h, :], start=True, stop=True), r=[BkTo, BqTo], w=[BP[b]])
                for n in range(2):
                    V(lambda e: e.tensor_tensor(ATo[:, 4 * n:4 * n + 4, :], pb4(5 + n), maskC[:].unsqueeze(1).to_broadcast([128, 4, 128]), ALU.mult),
                      r=[BP[5 + n], BmC], w=[BATo])

            def Cg():
                SfC, BSfC = SfCs[0]; SbC, BSbC = SbCs[0]
                if mode == 'p' and idx == 0:
                    V(lambda e: e.memset(SfC[:], 0.0), w=[BSfC])
                    V(lambda e: e.memset(SbC[:], 0.0), w=[BSbC])
                if mode == 's':
                    K.dma('s', SfCs[0][0][:], hg_s[j, 4 * idx].rearrange("h d e -> d h e"), SfCs[0][1], w=[SfCs[0][1]])
                for q in range(4):
                    qs = slice(32 * q, 32 * q + 32)
                    if mode == 's':
                        SfC, BSfC = SfCs[q % 2]; SbC, BSbC = SbCs[q % 2]
                        if q < 3:
                            K.dma('s', SfCs[(q + 1) % 2][0][:], hg_s[j, 4 * idx + q + 1].rearrange("h d e -> d h e"), SfCs[(q + 1) % 2][1],
                                  w=[SfCs[(q + 1) % 2][1]])
                        A(lambda e: e.copy(SbC[:], SfC[:]), r=[BSfC], w=[BSbC])
                    for h in range(8):
                        b = 1 + h // 4
                        PE(lambda e: e.matmul(pb4(b)[qs, h % 4, :], qTo[:, h, qs], SbC[:, h, :], start=True, stop=True, tile_position=(0, 32 * q)),
                           r=[BqTo, BSbC], w=[BP[b]])
                    for h in range(8):
                        b = 3 + h // 4
                        PE(lambda e: e.matmul(pb4(b)[:, h % 4, :], khat[qs, h * 128:(h + 1) * 128], vtl[qs, h * 128:(h + 1) * 128], start=True, stop=True,
                                              tile_position=(32 * q, 0)), r=[Bkhat, Bvtl], w=[BP[b]])
                    V(lambda e: e.tensor_tensor(tSC[:], SfC[:], eglo[:, :, q:q + 1].to_broadcast([128, 8, 128]), ALU.mult), r=[BSfC, Beglo], w=[BtSC])
                    yield
                    if mode == 'p':
                        for n in range(2):
                            V(lambda e: e.tensor_tensor(SbC[:, 4 * n:4 * n + 4, :], tSC[:, 4 * n:4 * n + 4, :], pb4(3 + n), ALU.add), r=[BtSC, BP[3 + n]], w=[BSbC])
                    for n in range(2):
                        V(lambda e: e.tensor_tensor(SfC[:, 4 * n:4 * n + 4, :], tSC[:, 4 * n:4 * n + 4, :], pb4(3 + n), ALU.add), r=[BtSC, BP[3 + n]], w=[BSfC])
                    if mode == 's':
                        K.dma('s', hg_so[j, 4 * idx + q].rearrange("h d e -> d h e"), SfC[:], BSfC, r=[BSfC])
                if lastp:
                    K.dma('s', hg_p[j].rearrange("h d e -> d h e"), SfC[:], BSfC, r=[BSfC])
                yield
                for n in range(2):
                    A(lambda e: e.copy(oC[:, 4 * n:4 * n + 4, :], pb4(1 + n)), r=[BP[1 + n]], w=[BoC])
                for h in range(8):
                    b = 5 + h // 4
                    PE(lambda e: e.matmul(pb4(b)[:, h % 4, :], ATo[:, h, :], vtl[:, h * 128:(h + 1) * 128], start=True, stop=True), r=[BATo, Bvtl], w=[BP[b]])
                for n in range(2):
                    V(lambda e: e.tensor_tensor(oC[:, 4 * n:4 * n + 4, :], oC[:, 4 * n:4 * n + 4, :], pb4(5 + n), ALU.add), r=[BoC, BP[5 + n]], w=[BoC])
                yield
                t3 = tmpD[:].rearrange("p (a c) -> p a c", a=8)
                V(lambda e: e.tensor_tensor(t3, oC[:], oC[:], ALU.mult), r=[BoC], w=[BtmpD])
                V(lambda e: e.tensor_reduce(ss8[:], t3, AX.X, ALU.add), r=[BtmpD], w=[Bss8])
                A(lambda e: e.activation(rs8[:], ss8[:], AF.Ln, bias=EPS, scale=1.0 / 128), r=[Bss8], w=[Brs8])
                A(lambda e: e.activation(rs8[:], rs8[:], AF.Exp, scale=-0.5), r=[Brs8], w=[Brs8])
                V(lambda e: e.tensor_tensor(oC[:], oC[:], rs8[:].unsqueeze(2).to_broadcast([128, 8, 128]), ALU.mult), r=[BoC, Brs8], w=[BoC])
                V(lambda e: e.tensor_tensor(ocb[:].rearrange("p (a c) -> p a c", a=8), oC[:], gnc[:].rearrange("p (a c) -> p a c", a=8), ALU.mult),
                  r=[BoC, Bgnc], w=[Bocb])

            def Eg():
                yield from out_proj_post(l, mode, idx, S)

            return Pre(), Pg(), [Cg()], Eg()

          return setup, odd_tile

        for l in range(NL):
            j = l // 2
            with ExitStack() as les:
                def sbl(name, shape, dt=F32, _les=les, _l=l):
                    t = _les.enter_context(nc.sbuf_tensor("%s_L%d" % (name, _l), list(shape), dt))
                    return t, Buf("%s_L%d" % (name, _l))
                K.dma('s', gbc[:], ln_g[l:l + 1, :].partition_broadcast(128), Bgbc, w=[Bgbc])
                K.dma('s', bbc[:], ln_b[l:l + 1, :].partition_broadcast(128), Bbbc, w=[Bbbc])
                load_w(wgate, Bwgate, w_pgate[l], D, D)
                load_w(wpp, Bwpp, w_pproj[l], PLE, D)
                setup, tile_fn = (make_even if l % 2 == 0 else make_odd)(sbl)
                setup(j)
                tiles = [('p', t) for t in range(NT)] + [('s', st) for st in range(NSEQ // 4)]
                parts = [tile_fn(l, mode_, idx_, SL[ti % 2], SL[(ti + 1) % 2]) for ti, (mode_, idx_) in enumerate(tiles)]
                prefetch = (l % 2 == 1)
                if prefetch:
                    interleave(parts[0][0])
                prevE = None

                def chain2(a_, b_):
                    yield from a_
                    yield from b_

                def take(gen_, n_):
                    for _ in range(n_):
                        try:
                            next(gen_)
                        except StopIteration:
                            return
                        yield
                early = [False] * len(tiles)
                for ti in range(len(tiles)):
                    Pre_, Pg, chains, Eg = parts[ti]
                    if prefetch or early[ti]:
                        interleave(prevE, Pg, seq=(SEQUENTIAL in (1, 2)))
                    else:
                        interleave(prevE, chain2(Pre_, Pg), seq=(SEQUENTIAL in (1, 2)))
                    if len(chains) == 2:
                        D_, S_ = chains
                        aliveD = aliveS = True
                        while aliveS:
                            for _ in range(2):
                                if aliveD:
                                    try:
                                        next(D_)
                                    except StopIteration:
                                        aliveD = False
                            try:
                                next(S_)
                            except StopIteration:
                                aliveS = False
                        X_ = None
                        if ti + 1 < len(tiles) and tiles[ti + 1][0] == 'p' and tiles[ti + 1][1] < NT - 1 and not SEQUENTIAL:
                            X_ = chain2(parts[ti + 1][0], take(parts[ti + 1][1], 4))
                            early[ti + 1] = True
                        interleave(D_ if aliveD else None, X_)
                    else:
                        nxt = None
                        if prefetch and ti + 1 < len(tiles):
                            nxt = chain2(parts[ti + 1][0], take(parts[ti + 1][1], 2)) if not SEQUENTIAL else parts[ti + 1][0]
                        interleave(chains[0], nxt, seq=(SEQUENTIAL in (1, 3)))
                    prevE = Eg
                interleave(prevE)
                K.barrier()
        K.finish_all('s')
    return nc


_CACHE = {}


def run_cores(per_core, NT, NL=4):
    key = (NT, NL)
    if key not in _CACHE:
        _CACHE[key] = build(NT, NL)
    nc = _CACHE[key]
    res = run_bass_kernel_spmd(nc, per_core, core_ids=list(range(len(per_core))))
    return res.results


def kernel(x_prompt, x_sample, state_conv_a, state_delta_a, cache_win_k, cache_win_v, state_hgrn_c,
           p_prompt, p_sample, w_in_even, conv_w_a, a_log, dt_bias, norm_a, sinks_b, w_out_even,
           w_in_odd, lb_raw, norm_c, w_out_odd, ln_g, ln_b, w_ple_proj, w_ple_gate):
    f = lambda a: np.ascontiguousarray(np.asarray(a, dtype=np.float32))
    x_prompt = f(x_prompt); x_sample = f(x_sample)
    NB, L = x_prompt.shape[:2]
    NT = L // 128
    NL = ln_g.shape[0]
    NE = (NL + 1) // 2
    consts = make_consts(NT)
    shared = {
        "w_in_even": f(w_in_even), "conv_w": f(conv_w_a), "a_log": f(a_log), "dt_bias": f(dt_bias), "norm_a": f(norm_a),
        "sinks": f(sinks_b), "w_out_even": f(w_out_even), "w_in_odd": f(w_in_odd), "lb_raw": f(lb_raw), "norm_c": f(norm_c),
        "w_out_odd": f(w_out_odd), "ln_g": f(ln_g), "ln_b": f(ln_b), "w_pproj": f(w_ple_proj), "w_pgate": f(w_ple_gate),
        "c_ident": consts["ident"], "c_maskC": consts["maskC"], "c_maskS": consts["maskS"], "c_bones": consts["bones"],
        "c_mnegC": consts["mnegC"], "c_ones": consts["ones"], "c_ind": consts["ind"], "c_valid": consts["valid"],
        "c_swam": consts["swam"], "c_rope": consts["rope"],
    }
    p_prompt = f(p_prompt); p_sample = f(p_sample)
    state_conv_a = f(state_conv_a); state_delta_a = f(state_delta_a)
    cache_win_k = f(cache_win_k); cache_win_v = f(cache_win_v); state_hgrn_c = f(state_hgrn_c)
    ncores = 8
    per_core = []
    for c in range(ncores):
        sq = c % NB
        s0, s1 = c * NSEQ, (c + 1) * NSEQ
        m = dict(shared)
        m["xp"] = f(x_prompt[sq]); m["pp"] = f(p_prompt[:, sq])
        m["xs"] = f(x_sample[s0:s1]); m["ps"] = f(p_sample[:, s0:s1])
        m["conv_s"] = f(state_conv_a[:, s0:s1]); m["delta_s"] = f(state_delta_a[:, s0:s1])
        m["wk_s"] = f(cache_win_k[:, s0:s1].reshape(NE, NSEQ, 128, 128)); m["wv_s"] = f(cache_win_v[:, s0:s1].reshape(NE, NSEQ, 128, 128))
        m["hg_s"] = f(state_hgrn_c[:, s0:s1])
        per_core.append(m)
    res = run_cores(per_core, NT, NL)
    cat = lambda k, ax: np.concatenate([res[c][k] for c in range(ncores)], axis=ax)
    stk = lambda k: np.stack([res[c][k] for c in range(NB)], axis=1)
    y_p = np.stack([res[c]["yp"] for c in range(NB)], axis=0)
    y_s = cat("ys", 0)
    NS = ncores * NSEQ
    outs = (y_p, y_s, stk("conv_p"), cat("conv_so", 1), stk("delta_p"), cat("delta_so", 1),
            stk("wk_p").reshape(NE, NB, 128, 2, 64), cat("wk_so", 1).reshape(NE, NS, 128, 2, 64),
            stk("wv_p").reshape(NE, NB, 128, 2, 64), cat("wv_so", 1).reshape(NE, NS, 128, 2, 64),
            stk("hg_p"), cat("hg_so", 1))
    return tuple(np.ascontiguousarray(o.astype(np.float32)) for o in outs)
```

```python
import numpy as np
from contextlib import ExitStack
import concourse.bass as bass
import concourse.mybir as mybir
from concourse.bass_utils import run_bass_kernel_spmd

F32 = mybir.dt.float32
BF16 = mybir.dt.bfloat16
AF = mybir.ActivationFunctionType
ALU = mybir.AluOpType
AX = mybir.AxisListType

D = 1024
EVEN_IN = 3336
ODD_IN = 4096
PLE = 256
ALPHA = float(8 ** 0.25)
EPS = 1e-6
NEG = -30000.0
PAST_LEN = 8192
ROPE_THETA = 500000.0
NSEQ = 16
SEQUENTIAL = 0
EP_PATTERN_EVEN = "1" + "00" + "1111" + "0" + "11" + "00" + "11" + "0" + "11111"
EP_PATTERN_ODD = "1" + "00" + "111" + "0" + "11" + "00" + "11" + "0" + "11111"


class Buf:
    def __init__(self, name):
        self.name = name
        self.w = None
        self.rs = {}
        self.dsem = None
        self.dcnt = 0
        self.excl = False


class Kern:
    def __init__(self, nc, es):
        self.nc = nc
        self.es = es
        self.eng = {'p': nc.tensor, 'v': nc.vector, 'a': nc.scalar, 'g': nc.gpsimd, 's': nc.sync}
        self.esem = {k: es.enter_context(nc.semaphore("es_" + k)) for k in self.eng}
        self.ecnt = {k: 0 for k in self.eng}
        self.seen = {k: {} for k in self.eng}
        self.alltoks = {}

    def _wait(self, e, tok):
        sem, val = tok
        if e == 'p' and sem is self.esem['p']:
            return
        if self.seen[e].get(sem.name, 0) >= val:
            return
        self.eng[e].wait_ge(sem, val)
        self.seen[e][sem.name] = val

    def _deps(self, e, r, w):
        for b in r:
            if b.w is not None:
                self._wait(e, b.w)
            if b.excl:
                for tok in list(b.rs.values()):
                    if tok[0] is not self.esem[e]:
                        self._wait(e, tok)
        for b in w:
            if b.w is not None:
                self._wait(e, b.w)
            for tok in list(b.rs.values()):
                self._wait(e, tok)

    def _upd(self, tok, r, w):
        for b in r:
            old = b.rs.get(tok[0].name)
            if old is None or old[1] < tok[1]:
                b.rs[tok[0].name] = tok
        for b in w:
            b.w = tok
            b.rs = {}

    def op(self, e, fn, r=(), w=()):
        self._deps(e, r, w)
        inst = fn(self.eng[e])
        self.ecnt[e] += 1
        inst.then_inc(self.esem[e], 1)
        self._upd((self.esem[e], self.ecnt[e]), r, w)

    def dma(self, q, out, in_, dbuf, r=(), w=(), **kw):
        self._deps(q, r, w)
        if dbuf.dsem is None:
            dbuf.dsem = self.es.enter_context(self.nc.semaphore("ds_" + dbuf.name))
        inst = self.eng[q].dma_start(out=out, in_=in_, **kw)
        dbuf.dcnt += 1
        inst.then_inc(dbuf.dsem, 16)
        tok = (dbuf.dsem, 16 * dbuf.dcnt)
        self.alltoks[dbuf.dsem.name] = tok
        self._upd(tok, r, w)

    def finish_all(self, e):
        for tok in self.alltoks.values():
            self._wait(e, tok)

    def barrier(self):
        toks = [(self.esem[e], self.ecnt[e]) for e in self.eng if self.ecnt[e] > 0] + list(self.alltoks.values())
        for e in self.eng:
            for tok in toks:
                self._wait(e, tok)


def make_consts(NT):
    c = {}
    i = np.arange(128)
    blk = i // 32
    same = blk[:, None] == blk[None, :]
    c["ident"] = np.eye(128, dtype=np.float32)
    c["maskC"] = (same & (i[:, None] <= i[None, :])).astype(np.float32)
    c["maskS"] = (same & (i[:, None] < i[None, :])).astype(np.float32)
    c["bones"] = same.astype(np.float32)
    c["mnegC"] = np.where(c["maskC"] > 0, 0.0, NEG).astype(np.float32)
    c["ones"] = np.ones((128, 128), np.float32)
    c["ind"] = (blk[:, None] == np.arange(4)[None, :]).astype(np.float32)
    valid = np.ones((128, 2), np.float32)
    valid[:, 1] = ((i % 32) < 8).astype(np.float32)
    c["valid"] = valid
    M = np.full((3, 128, 256), -1e30, np.float32)
    jj = np.arange(128)
    prev_ok = jj[None, :] >= i[:, None]
    cur_ok = jj[None, :] <= i[:, None]
    M[0, :, :128][prev_ok] = 0.0
    M[0, :, 128:][cur_ok] = 0.0
    M[1, :, 128:][cur_ok] = 0.0
    l = i % 32
    sprev = jj[None, :] >= l[:, None]
    scur = same & ((jj % 32)[None, :] <= l[:, None]) & ((jj % 32)[None, :] < 8)
    M[2, :, :128][sprev] = 0.0
    M[2, :, 128:][scur] = 0.0
    c["swam"] = M
    inv = ROPE_THETA ** (-np.arange(0, 16, 2, dtype=np.float32) / np.float32(16))
    R = np.zeros((NT + 1, 128, 32), np.float32)
    for t in range(NT + 1):
        pos = (128 * t + i) if t < NT else (PAST_LEN + (i % 32))
        ang = pos.astype(np.float32)[:, None] * inv[None, :].astype(np.float32)
        cs = np.cos(ang).astype(np.float32)
        sn = np.sin(ang).astype(np.float32)
        R[t, :, 0:8] = cs
        R[t, :, 8:16] = cs
        R[t, :, 16:24] = -sn
        R[t, :, 24:32] = sn
    c["rope"] = R
    return c


def build(NT, NL=4):
    L = NT * 128
    nc = bass.Bass("TRN2", target_bir_lowering=False)
    es = ExitStack()

    def din(name, shape):
        return nc.dram_tensor(name, list(shape), F32, kind="ExternalInput").ap()

    def dout(name, shape):
        return nc.dram_tensor(name, list(shape), F32, kind="ExternalOutput").ap()

    NE = (NL + 1) // 2
    NO = NL // 2
    xp = din("xp", [L, D]); ppd = din("pp", [NL, L, PLE])
    xs = din("xs", [NSEQ, 8, D]); psd = din("ps", [NL, NSEQ, 8, PLE])
    conv_s = din("conv_s", [NE, NSEQ, 3, 1536]); delta_s = din("delta_s", [NE, NSEQ, 4, 128, 128])
    wk_s = din("wk_s", [NE, NSEQ, 128, 128]); wv_s = din("wv_s", [NE, NSEQ, 128, 128])
    hg_s = din("hg_s", [max(NO, 1), NSEQ, 8, 128, 128])
    w_in_even = din("w_in_even", [NE, D, EVEN_IN]); conv_w = din("conv_w", [NE, 4, 1536])
    a_log = din("a_log", [NE, 4]); dt_bias = din("dt_bias", [NE, 4]); norm_a = din("norm_a", [NE, 128])
    sinks = din("sinks", [NE, 8]); w_out_even = din("w_out_even", [NE, D, D])
    w_in_odd = din("w_in_odd", [max(NO, 1), D, ODD_IN]); lb_raw = din("lb_raw", [2, D])
    norm_c = din("norm_c", [max(NO, 1), 128]); w_out_odd = din("w_out_odd", [max(NO, 1), D, D])
    ln_g = din("ln_g", [NL, D]); ln_b = din("ln_b", [NL, D])
    w_pproj = din("w_pproj", [NL, PLE, D]); w_pgate = din("w_pgate", [NL, D, D])
    c_ident = din("c_ident", [128, 128]); c_maskC = din("c_maskC", [128, 128]); c_maskS = din("c_maskS", [128, 128])
    c_bones = din("c_bones", [128, 128]); c_mnegC = din("c_mnegC", [128, 128]); c_ones = din("c_ones", [128, 128])
    c_ind = din("c_ind", [128, 4]); c_valid = din("c_valid", [128, 2]); c_swam = din("c_swam", [3, 128, 256])
    c_rope = din("c_rope", [NT + 1, 128, 32])

    yp = dout("yp", [L, D]); ys = dout("ys", [NSEQ, 8, D])
    conv_p = dout("conv_p", [NE, 3, 1536]); conv_so = dout("conv_so", [NE, NSEQ, 3, 1536])
    delta_p = dout("delta_p", [NE, 4, 128, 128]); delta_so = dout("delta_so", [NE, NSEQ, 4, 128, 128])
    wk_p = dout("wk_p", [NE, 128, 128]); wk_so = dout("wk_so", [NE, NSEQ, 128, 128])
    wv_p = dout("wv_p", [NE, 128, 128]); wv_so = dout("wv_so", [NE, NSEQ, 128, 128])
    hg_p = dout("hg_p", [max(NO, 1), 8, 128, 128]); hg_so = dout("hg_so", [max(NO, 1), NSEQ, 8, 128, 128])
    hP = [nc.dram_tensor("hP%d" % i, [L, D], F32, kind="Internal").ap() for i in range(2)]
    hS = [nc.dram_tensor("hS%d" % i, [NSEQ, 8, D], F32, kind="Internal").ap() for i in range(2)]
    BhP = [Buf("hP0d"), Buf("hP1d")]
    BhS = [Buf("hS0d"), Buf("hS1d")]

    with es:
        K = Kern(nc, es)
        bufs = {}

        def sb(name, shape, dt=F32):
            t = es.enter_context(nc.sbuf_tensor(name, list(shape), dt))
            bufs[name] = Buf(name)
            return t, bufs[name]

        V = lambda fn, r=(), w=(): K.op('v', fn, r, w)
        A = lambda fn, r=(), w=(): K.op('a', fn, r, w)
        PE = lambda fn, r=(), w=(): K.op('p', fn, r, w)
        G = lambda fn, r=(), w=(): K.op('g', fn, r, w)

        PT = es.enter_context(nc.psum_tensor("PT", [128, 8, 128], BF16)); BPT = Buf("PT")
        PA = es.enter_context(nc.psum_tensor("PA", [128, 7, 512], F32))
        BP = [None] + [Buf("PB%d" % b) for b in range(1, 8)]
        BPT.excl = True
        for b_ in BP[1:]:
            b_.excl = True

        def pb(b):
            return PA[:, b - 1, :]

        def pb4(b):
            return PA[:, b - 1, :].rearrange("p (a c) -> p a c", a=4)

        ident_f, Bidf = sb("ident_f", [128, 128]); ident_b, Bidb = sb("ident_b", [128, 128], BF16)
        maskC, BmC = sb("maskC", [128, 128]); maskS, BmS = sb("maskS", [128, 128])
        bones, Bbo = sb("bones", [128, 128]); mnegC, Bmn = sb("mnegC", [128, 128]); ones_f, Bon = sb("ones_f", [128, 128])
        ind, Bind = sb("ind", [128, 4]); valid, Bval = sb("valid", [128, 2])
        swam, Bswam = sb("swam", [128, 3, 256])
        for t_, b_, d_ in [(ident_f, Bidf, c_ident), (maskC, BmC, c_maskC), (maskS, BmS, c_maskS), (bones, Bbo, c_bones),
                           (mnegC, Bmn, c_mnegC), (ones_f, Bon, c_ones), (ind, Bind, c_ind), (valid, Bval, c_valid)]:
            K.dma('s', t_[:], d_, b_, w=[b_])
        K.dma('s', swam[:], c_swam.rearrange("m p k -> p m k"), Bswam, w=[Bswam])
        V(lambda e: e.tensor_copy(ident_b[:], ident_f[:]), r=[Bidf], w=[Bidb])

        wout, Bwout = sb("wout", [128, 8, D], BF16)
        wgate, Bwgate = sb("wgate", [128, 8, D], BF16)
        wpp, Bwpp = sb("wpp", [128, 2, D], BF16)
        WSTG = 1024
        stg_i = [0]
        gbc, Bgbc = sb("gbc", [128, D]); bbc, Bbbc = sb("bbc", [128, D])

        def load_w(dst, Bdst, src, nrows, ncols):
            for kc in range(nrows // 128):
                for c0 in range(0, ncols, WSTG):
                    cw = min(WSTG, ncols - c0)
                    st, Bst = stg[stg_i[0] % 3]
                    q = 's'
                    K.dma(q, st[:, 0:cw], src[kc * 128:(kc + 1) * 128, c0:c0 + cw], Bst, w=[Bst])
                    if stg_i[0] % 2 == 0:
                        V(lambda e: e.tensor_copy(dst[:, kc, c0:c0 + cw], st[:, 0:cw]), r=[Bst], w=[Bdst])
                    else:
                        A(lambda e: e.copy(dst[:, kc, c0:c0 + cw], st[:, 0:cw]), r=[Bst], w=[Bdst])
                    stg_i[0] += 1

        SL = []
        for i_ in range(2):
            SL.append(dict(xin=sb("xin%d" % i_, [128, D]), pin=sb("pin%d" % i_, [128, PLE]), xb=sb("xb%d" % i_, [128, D], BF16),
                           xT=sb("xT%d" % i_, [128, 8, 128], BF16), pb16=sb("pb16_%d" % i_, [128, PLE], BF16),
                           pT=sb("pT%d" % i_, [128, 2, 128], BF16)))
        zt, Bzt = sb("zt", [128, D])
        stg = [SL[0]["xin"], SL[1]["xin"], (zt, Bzt)]
        st6, Bst6 = sb("st6", [128, 2, 6]); mv, Bmv = sb("mv", [128, 2]); rstd, Brstd = sb("rstd", [128, 1])
        ocb, Bocb = sb("ocb", [128, D], BF16)
        Bd2d = Buf("d2d")
        for i_ in range(2):
            for t_, b_ in [SL[i_]["xin"], SL[i_]["pin"]]:
                V(lambda e: e.memset(t_[:], 0.0), w=[b_])

        def interleave(*gens, seq=False, pattern=None):
            gens = [g for g in gens if g is not None]
            if pattern is not None and not seq:
                alive = [True] * len(gens)
                for ch in pattern:
                    gi = int(ch)
                    if alive[gi]:
                        try:
                            next(gens[gi])
                        except StopIteration:
                            alive[gi] = False
                gens = [g for g, a in zip(gens, alive) if a]
            gens = [g for g in gens if g is not None]
            if seq:
                for g in gens:
                    for _ in g:
                        pass
                return
            while gens:
                for g in list(gens):
                    try:
                        next(g)
                    except StopIteration:
                        gens.remove(g)

        def transposes(n, src_fn, rbufs, dst, Bdst, dst_sl=None, eng='v'):
            for i in range(n):
                PE(lambda e: e.transpose(PT[:, i, :], src_fn(i), ident_b[:]), r=rbufs + [Bidb], w=[BPT])
            if eng == 'a':
                A(lambda e: e.copy(dst[:, 0:n, :], PT[:, 0:n, :]), r=[BPT], w=[Bdst])
            elif dst_sl is None:
                V(lambda e: e.tensor_copy(dst[:, 0:n, :], PT[:, 0:n, :]), r=[BPT], w=[Bdst])
            else:
                V(lambda e: e.tensor_copy(dst_sl, PT[:, 0:n, :]), r=[BPT], w=[Bdst])

        def load_tile(l, mode, idx, S):
            xin, Bxin = S["xin"]; pin, Bpin = S["pin"]; xb, Bxb = S["xb"]; xT, BxT = S["xT"]; pb16, Bpb16 = S["pb16"]; pT, BpT = S["pT"]
            if mode == 'p':
                src = xp if l == 0 else hP[(l - 1) % 2]
                rb = [] if l == 0 else [BhP[(l - 1) % 2]]
                K.dma('s', xin[:], src[idx * 128:(idx + 1) * 128, :], Bxin, r=rb, w=[Bxin])
                K.dma('s', pin[:], ppd[l, idx * 128:(idx + 1) * 128, :], Bpin, w=[Bpin])
            else:
                src = xs if l == 0 else hS[(l - 1) % 2]
                rb = [] if l == 0 else [BhS[(l - 1) % 2]]
                for q in range(4):
                    K.dma('s', xin[32 * q:32 * q + 8, :], src[4 * idx + q], Bxin, r=rb, w=[Bxin])
                    K.dma('s', pin[32 * q:32 * q + 8, :], psd[l, 4 * idx + q], Bpin, w=[Bpin])
            yield
            A(lambda e: e.copy(xb[:], xin[:]), r=[Bxin], w=[Bxb])
            transposes(8, lambda i: xb[:, i * 128:(i + 1) * 128], [Bxb], xT, BxT, eng='a')
            yield
            A(lambda e: e.copy(pb16[:], pin[:]), r=[Bpin], w=[Bpb16])
            transposes(2, lambda i: pb16[:, i * 128:(i + 1) * 128], [Bpb16], pT, BpT, eng='a')
            yield

        def out_proj_post(l, mode, idx, S):
            xin, Bxin = S["xin"]; pT, BpT = S["pT"]
            hnb, Bhnb = S["xb"]; hnT, BhnT = S["xT"]; oT, BoT = S["xT"]
            transposes(8, lambda i: ocb[:, i * 128:(i + 1) * 128], [Bocb], oT, BoT)
            yield
            for n in range(2):
                for kc in range(8):
                    PE(lambda e: e.matmul(pb(1 + n), oT[:, kc, :], wout[:, kc, n * 512:(n + 1) * 512], start=(kc == 0), stop=(kc == 7)),
                       r=[BoT, Bwout], w=[BP[1 + n]])
                V(lambda e: e.scalar_tensor_tensor(zt[:, n * 512:(n + 1) * 512], xin[:, n * 512:(n + 1) * 512], ALPHA, pb(1 + n), ALU.mult, ALU.add),
                  r=[Bxin, BP[1 + n]], w=[Bzt])
            yield
            for n in range(2):
                V(lambda e: e.bn_stats(st6[:, n, :], zt[:, n * 512:(n + 1) * 512]), r=[Bzt], w=[Bst6])
            V(lambda e: e.bn_aggr(mv[:], st6[:]), r=[Bst6], w=[Bmv])
            A(lambda e: e.activation(rstd[:], mv[:, 1:2], AF.Ln, bias=EPS), r=[Bmv], w=[Brstd])
            A(lambda e: e.activation(rstd[:], rstd[:], AF.Exp, scale=-0.5), r=[Brstd], w=[Brstd])
            V(lambda e: e.tensor_scalar(zt[:], zt[:], mv[:, 0:1], rstd[:, 0:1], ALU.subtract, ALU.mult), r=[Bzt, Bmv, Brstd], w=[Bzt])
            V(lambda e: e.tensor_tensor(zt[:], zt[:], gbc[:], ALU.mult), r=[Bzt, Bgbc], w=[Bzt])
            V(lambda e: e.tensor_tensor(zt[:], zt[:], bbc[:], ALU.add), r=[Bzt, Bbbc], w=[Bzt])
            A(lambda e: e.copy(hnb[:], zt[:]), r=[Bzt], w=[Bhnb])
            yield
            transposes(8, lambda i: hnb[:, i * 128:(i + 1) * 128], [Bhnb], hnT, BhnT)
            yield
            for n in range(2):
                for kc in range(8):
                    PE(lambda e: e.matmul(pb(3 + n), hnT[:, kc, :], wgate[:, kc, n * 512:(n + 1) * 512], start=(kc == 0), stop=(kc == 7)),
                       r=[BhnT, Bwgate], w=[BP[3 + n]])
                A(lambda e: e.activation(xin[:, n * 512:(n + 1) * 512], pb(3 + n), AF.Sigmoid), r=[BP[3 + n]], w=[Bxin])
            yield
            for n in range(2):
                for kc in range(2):
                    PE(lambda e: e.matmul(pb(5 + n), pT[:, kc, :], wpp[:, kc, n * 512:(n + 1) * 512], start=(kc == 0), stop=(kc == 1)),
                       r=[BpT, Bwpp], w=[BP[5 + n]])
                V(lambda e: e.tensor_tensor(xin[:, n * 512:(n + 1) * 512], xin[:, n * 512:(n + 1) * 512], pb(5 + n), ALU.mult),
                  r=[Bxin, BP[5 + n]], w=[Bxin])
            V(lambda e: e.tensor_tensor(xin[:], xin[:], zt[:], ALU.add), r=[Bxin, Bzt], w=[Bxin])
            last = (l == NL - 1)
            if mode == 'p':
                dst = yp if last else hP[l % 2]
                wb = [] if last else [BhP[l % 2]]
                K.dma('s', dst[idx * 128:(idx + 1) * 128, :], xin[:], Bxin, r=[Bxin], w=wb)
            else:
                dst = ys if last else hS[l % 2]
                wb = [] if last else [BhS[l % 2]]
                for q in range(4):
                    K.dma('s', dst[4 * idx + q], xin[32 * q:32 * q + 8, :], Bxin, r=[Bxin], w=wb)
            yield

        def make_even(sbl):
          win, Bwin = sbl("win", [128, 8, EVEN_IN], BF16)
          cv, Bcv = sbl("cv", [128, 12, 132], BF16)
          cvo, Bcvo = sbl("cvo", [128, 12, 4, 3]); cst, Bcst = sbl("cst", [128, 12, 4, 3])
          diagw, Bdiagw = sbl("diagw", [128, 12, 4, 128], BF16); cwt, Bcwt = sbl("cwt", [128, 12, 4])
          gna, Bgna = sbl("gna", [128, 512]); sgb, Bsgb = sbl("sgb", [128, 512])
          qkf, Bqkf = sbl("qkf", [128, 10, 64]); vf, Bvf = sbl("vf", [128, 128]); ab, Bab = sbl("ab", [128, 8])
          rt1, Brt1 = sbl("rt1", [128, 10, 16]); rt2, Brt2 = sbl("rt2", [128, 10, 16]); rope, Brope = sbl("rope", [128, 32])
          qkT, BqkT = sbl("qkT", [128, 8, 128])
          rn, Brn = sbl("rn", [128, 8, 128]); sqt, Bsqt = rn, Brn
          qkn, Bqkn = sbl("qkn", [128, 8, 128], BF16); vTb, BvTb = sbl("vTb", [128, 4, 128], BF16)
          vtok, Bvtok = sbl("vtok", [128, 4, 128], BF16); keg, Bkeg = sbl("keg", [128, 4, 128], BF16)
          kdec, Bkdec = sbl("kdec", [128, 4, 128], BF16)
          sc4 = {}
          for nm in ["zz", "gg", "beta", "nbeta", "Gs", "eG", "ekd", "osc", "tmp4", "ss4", "rs4", "mx4", "negm", "rsum", "es4", "den4"]:
              sc4[nm] = sbl("s4_" + nm, [128, 4])
          gq, Bgq = sbl("gq", [128, 4, 4]); eglb, Beglb = sbl("eglb", [128, 4, 4])
          negA, BnegA = sbl("negA", [128, 4]); dtb, Bdtb = sbl("dtb", [128, 4]); nabc, Bnabc = sbl("nabc", [128, 128])
          sinkbc, Bsinkbc = sbl("sinkbc", [128, 8]); rden, Brden = sbl("rden", [128, 8])
          gU, BgU = sbl("gU", [128, 4, 128]); ngU, BngU = sbl("ngU", [128, 4, 128])
          Ef, BEf = sbl("Ef", [128, 4, 128]); Es, BEs = sbl("Es", [128, 4, 128])
          big1, Bbig1 = sbl("big1", [128, 4, 256])
          Xm, BXm = big1[:, :, 0:128], Bbig1
          XTm, BXTm = big1[:, :, 128:256], Bbig1
          sm, Bsm = rn[:].rearrange("p a c -> p (a c)").rearrange("p (a c) -> p a c", a=4), Brn
          Pm, BPm = sbl("Pm", [128, 4, 128])
          Pb, BPb = sbl("Pb", [128, 4, 128], BF16); QKT, BQKT = sbl("QKT", [128, 4, 128], BF16)
          ub, Bub = sbl("ub", [128, 4, 128]); wT, BwT = sbl("wT", [128, 4, 128], BF16)
          vnew, Bvnew = sbl("vnew", [128, 4, 128], BF16)
          tq, Btq = gU, BgU
          o2, Bo2 = ngU, BngU
          oA, BoA = Ef, BEf
          tS, BtS = Pm, BPm
          SfAs = [sbl("SfA%d" % i, [128, 4, 128]) for i in range(2)]; SbAs = [sbl("SbA%d" % i, [128, 4, 128], BF16) for i in range(2)]
          qb16, Bqb16 = sbl("qb16", [128, 4, 2, 64], BF16)
          kcur = [sbl("kcur0", [128, 128], BF16)]
          kTs = [sbl("kTs%d" % i, [128, 128], BF16) for i in range(2)]
          v16 = [sbl("v16_%d" % i, [128, 128], BF16) for i in range(2)]
          qT, BqT = sbl("qT", [128, 4, 128], BF16)
          ztv = zt[:].rearrange("p (a c) -> p a c", a=8)
          kcf, Bkcf = ztv[:, 0:4, :], Bzt
          vcf, Bvcf = ztv[:, 4:8, :], Bzt
          kc16, Bkc16 = sbl("kc16", [128, 4, 128], BF16); vc16, Bvc16 = sbl("vc16", [128, 4, 128], BF16)
          kTc, BkTc = sbl("kTc", [128, 4, 128], BF16)
          pr, Bpr = sbl("pr", [128, 4, 256], BF16)
          prT, BprT = sbl("prT", [128, 8, 128], BF16); ob, Bob = sbl("ob", [128, 8, 64])
          for t_, b_ in [(v16[0][0], v16[0][1]), (v16[1][0], v16[1][1]), (kTs[0][0], kTs[0][1]), (kTs[1][0], kTs[1][1])]:
              V(lambda e: e.memset(t_[:], 0.0), w=[b_])

          def setup(j):
                load_w(win, Bwin, w_in_even[j], D, EVEN_IN)
                load_w(wout, Bwout, w_out_even[j], D, D)
                K.dma('s', negA[:], a_log[j:j + 1, :].partition_broadcast(128), BnegA, w=[BnegA])
                A(lambda e: e.activation(negA[:], negA[:], AF.Exp), r=[BnegA], w=[BnegA])
                V(lambda e: e.tensor_scalar_mul(negA[:], negA[:], -1.0), r=[BnegA], w=[BnegA])
                K.dma('s', dtb[:], dt_bias[j:j + 1, :].partition_broadcast(128), Bdtb, w=[Bdtb])
                K.dma('s', nabc[:], norm_a[j:j + 1, :].partition_broadcast(128), Bnabc, w=[Bnabc])
                K.dma('s', sinkbc[:], sinks[j:j + 1, :].partition_broadcast(128), Bsinkbc, w=[Bsinkbc])
                for jt in range(4):
                    K.dma('s', cwt[:, :, jt], conv_w[j, jt].rearrange("(m c) -> c m", c=128), Bcwt, w=[Bcwt], allow_slow_non_contiguous=True)
                for m in range(12):
                    for jt in range(4):
                        V(lambda e: e.tensor_scalar_mul(diagw[:, m, jt, :], ident_f[:], cwt[:, m, jt:jt + 1]), r=[Bidf, Bcwt], w=[Bdiagw])

          def even_tile(l, mode, idx, S, S2):
            j = l // 2
            vcol = 0 if mode == 'p' else 1
            lastp = (mode == 'p' and idx == NT - 1)
            xT, BxT = S["xT"]
            crow = gU[:].rearrange("p a c -> p (a c)")
            crow_o = ngU[:].rearrange("p a c -> p (a c)")

            def Pre():
                yield from load_tile(l, mode, idx, S)

            def Pg():
                K.dma('s', rope[:], c_rope[idx if mode == 'p' else NT], Brope, w=[Brope])
                if mode == 'p':
                    if idx == 0:
                        V(lambda e: e.memset(cv[:, :, 0:3], 0.0), w=[Bcv])
                    else:
                        V(lambda e: e.tensor_copy(cv[:, :, 0:3], cv[:, :, 128:131]), r=[Bcv], w=[Bcv])
                else:
                    for bk in range(3):
                        K.dma('s', crow[0:12, :], conv_s[j, 4 * idx:4 * idx + 4, :, bk * 512:(bk + 1) * 512].rearrange("q r c -> (q r) c"), BgU, w=[BgU])
                        for mm in range(4):
                            m = 4 * bk + mm
                            PE(lambda e: e.matmul(pb(7)[:, m * 12:(m + 1) * 12], crow[0:12, mm * 128:(mm + 1) * 128], ident_f[0:12, 0:12],
                                                  start=True, stop=True), r=[BgU, Bidf], w=[BP[7]])
                    V(lambda e: e.tensor_copy(cst[:].rearrange("p m q r -> p (m q r)"), pb(7)[:, 0:144]), r=[BP[7]], w=[Bcst])
                yield
                qlist = [0] if mode == 'p' else [0, 1, 2, 3]
                for bk in range(3):
                    for mm in range(4):
                        m = 4 * bk + mm
                        for kc in range(8):
                            PE(lambda e: e.matmul(pb4(1 + bk)[:, mm, :], win[:, kc, m * 128:(m + 1) * 128], xT[:, kc, :], start=(kc == 0), stop=(kc == 7)),
                               r=[Bwin, BxT], w=[BP[1 + bk]])
                    A(lambda e: e.copy(cv[:, 4 * bk:4 * bk + 4, 3:131], pb4(1 + bk)), r=[BP[1 + bk]], w=[Bcv])
                    if lastp or mode == 's':
                        V(lambda e: e.tensor_copy(ub[:], pb4(1 + bk)), r=[BP[1 + bk]], w=[Bub])
                        for mm in range(4):
                            for q in qlist:
                                t0_ = 125 if mode == 'p' else 32 * q + 5
                                PE(lambda e: e.matmul(pb(7)[32 * q:32 * q + 3, mm * 128:(mm + 1) * 128], ub[:, mm, t0_:t0_ + 3], ident_f[:],
                                                      start=True, stop=True, tile_position=(0, 32 * q)), r=[Bub, Bidf], w=[BP[7]])
                        V(lambda e: e.tensor_copy(crow_o, pb(7)), r=[BP[7]], w=[BngU])
                        for q in qlist:
                            dst_ = conv_p[j, :, bk * 512:(bk + 1) * 512] if mode == 'p' else conv_so[j, 4 * idx + q, :, bk * 512:(bk + 1) * 512]
                            K.dma('s', dst_, crow_o[32 * q:32 * q + 3, :], BngU, r=[BngU])
                    yield
                if mode == 's':
                    for q in range(4):
                        V(lambda e: e.tensor_copy(cv[:, :, 32 * q:32 * q + 3], cst[:, :, q, :]), r=[Bcst, Bcv], w=[Bcv])
                def tm(b, o0, c0, n):
                    for kc in range(8):
                        PE(lambda e: e.matmul(pb(b)[:, o0:o0 + n], xT[:, kc, :], win[:, kc, c0:c0 + n], start=(kc == 0), stop=(kc == 7)),
                           r=[Bwin, BxT], w=[BP[b]])
                tm(4, 0, 1536, 512)
                A(lambda e: e.activation(gna[:], pb(4), AF.Silu), r=[BP[4]], w=[Bgna])
                V(lambda e: e.tensor_tensor(gna[:].rearrange("p (a c) -> p a c", a=4), gna[:].rearrange("p (a c) -> p a c", a=4),
                                            nabc[:].unsqueeze(1).to_broadcast([128, 4, 128]), ALU.mult), r=[Bgna, Bnabc], w=[Bgna])
                yield
                tm(5, 0, 2056, 512)
                V(lambda e: e.tensor_copy(qkf[:, 0:8, :], pb(5).rearrange("p (a c) -> p a c", a=8)), r=[BP[5]], w=[Bqkf])
                yield
                tm(6, 0, 2568, 256)
                tm(6, 256, 2048, 8)
                V(lambda e: e.tensor_copy(qkf[:, 8:10, :], pb(6)[:, 0:128].rearrange("p (a c) -> p a c", a=2)), r=[BP[6]], w=[Bqkf])
                V(lambda e: e.tensor_copy(vf[:], pb(6)[:, 128:256]), r=[BP[6]], w=[Bvf])
                V(lambda e: e.tensor_copy(ab[:], pb(6)[:, 256:264]), r=[BP[6]], w=[Bab])
                yield
                tm(7, 0, 2824, 512)
                A(lambda e: e.activation(sgb[:], pb(7), AF.Silu), r=[BP[7]], w=[Bsgb])
                yield
                for bk in range(3):
                    for mm in range(4):
                        m = 4 * bk + mm
                        for jt in range(4):
                            PE(lambda e: e.matmul(pb4(1 + bk)[:, mm, :], diagw[:, m, jt, :], cv[:, m, jt:jt + 128], start=(jt == 0), stop=(jt == 3)),
                               r=[Bdiagw, Bcv], w=[BP[1 + bk]])
                    if bk < 2:
                        A(lambda e: e.activation(qkT[:, 4 * bk:4 * bk + 4, :], pb4(1 + bk), AF.Silu), r=[BP[1 + bk]], w=[BqkT])
                    else:
                        A(lambda e: e.activation(vTb[:], pb4(3), AF.Silu), r=[BP[3]], w=[BvTb])
                    yield

            def Dg():
                zz, Bzz = sc4["zz"]; gg, Bgg = sc4["gg"]; beta, Bbeta = sc4["beta"]; nbeta, Bnbeta = sc4["nbeta"]
                Gs, BGs = sc4["Gs"]; eG, BeG = sc4["eG"]; ekd, Bekd = sc4["ekd"]; osc, Bosc = sc4["osc"]; tmp4, Btmp4 = sc4["tmp4"]
                def Da():
                    V(lambda e: e.tensor_tensor(sqt[:], qkT[:], qkT[:], ALU.mult), r=[BqkT], w=[Bsqt])
                    for b in range(2):
                        PE(lambda e: e.matmul(pb(1 + b), ones_f[:], sqt[:, 4 * b:4 * b + 4, :].rearrange("p a c -> p (a c)"), start=True, stop=True),
                           r=[Bon, Bsqt], w=[BP[1 + b]])
                    for b in range(2):
                        A(lambda e: e.activation(rn[:, 4 * b:4 * b + 4, :], pb4(1 + b), AF.Ln, bias=EPS), r=[BP[1 + b]], w=[Brn])
                    A(lambda e: e.activation(rn[:], rn[:], AF.Exp, scale=-0.5), r=[Brn], w=[Brn])
                    V(lambda e: e.tensor_tensor(qkn[:], qkT[:], rn[:], ALU.mult), r=[BqkT, Brn], w=[Bqkn])
                    yield
                    transposes(4, lambda i: vTb[:, i, :], [BvTb], vtok, Bvtok)
                    yield

                def Db():
                    V(lambda e: e.tensor_tensor(zz[:], ab[:, 0:4], dtb[:], ALU.add), r=[Bab, Bdtb], w=[Bzz])
                    A(lambda e: e.activation(zz[:], zz[:], AF.Exp), r=[Bzz], w=[Bzz])
                    A(lambda e: e.activation(zz[:], zz[:], AF.Ln, bias=1.0), r=[Bzz], w=[Bzz])
                    V(lambda e: e.scalar_tensor_tensor(gg[:], zz[:], valid[:, vcol:vcol + 1], negA[:], ALU.mult, ALU.mult), r=[Bzz, Bval, BnegA], w=[Bgg])
                    A(lambda e: e.activation(beta[:], ab[:, 4:8], AF.Exp, scale=-1.0), r=[Bab], w=[Bbeta])
                    V(lambda e: e.tensor_scalar_add(beta[:], beta[:], 1.0), r=[Bbeta], w=[Bbeta])
                    V(lambda e: e.reciprocal(beta[:], beta[:]), r=[Bbeta], w=[Bbeta])
                    V(lambda e: e.tensor_scalar_mul(beta[:], beta[:], valid[:, vcol:vcol + 1]), r=[Bbeta, Bval], w=[Bbeta])
                    V(lambda e: e.tensor_scalar_mul(nbeta[:], beta[:], -1.0), r=[Bbeta], w=[Bnbeta])
                    V(lambda e: e.tensor_tensor(gq[:], gg[:].unsqueeze(1).to_broadcast([128, 4, 4]), ind[:].unsqueeze(2).to_broadcast([128, 4, 4]), ALU.mult),
                      r=[Bgg, Bind], w=[Bgq])
                    PE(lambda e: e.matmul(pb(6)[:, 0:4], maskC[:], gg[:], start=True, stop=True), r=[BmC, Bgg], w=[BP[6]])
                    PE(lambda e: e.matmul(pb(6)[:, 4:8], bones[:], gg[:], start=True, stop=True), r=[Bbo, Bgg], w=[BP[6]])
                    PE(lambda e: e.matmul(pb(6)[:, 8:24], ones_f[:], gq[:].rearrange("p a c -> p (a c)"), start=True, stop=True), r=[Bon, Bgq], w=[BP[6]])
                    V(lambda e: e.tensor_copy(Gs[:], pb(6)[:, 0:4]), r=[BP[6]], w=[BGs])
                    A(lambda e: e.activation(eG[:], pb(6)[:, 0:4], AF.Exp), r=[BP[6]], w=[BeG])
                    V(lambda e: e.tensor_tensor(tmp4[:], pb(6)[:, 4:8], Gs[:], ALU.subtract), r=[BP[6], BGs], w=[Btmp4])
                    A(lambda e: e.activation(ekd[:], tmp4[:], AF.Exp), r=[Btmp4], w=[Bekd])
                    A(lambda e: e.activation(eglb[:].rearrange("p a c -> p (a c)"), pb(6)[:, 8:24], AF.Exp), r=[BP[6]], w=[Beglb])
                    V(lambda e: e.tensor_scalar_mul(osc[:], eG[:], float(128 ** -0.5)), r=[BeG], w=[Bosc])
                    yield
                    V(lambda e: e.tensor_tensor(gU[:], maskC[:].unsqueeze(1).to_broadcast([128, 4, 128]), gg[:].unsqueeze(2).to_broadcast([128, 4, 128]), ALU.mult),
                      r=[BmC, Bgg], w=[BgU])
                    A(lambda e: e.mul(ngU[:], gU[:], -1.0), r=[BgU], w=[BngU])
                    for h in range(4):
                        PE(lambda e: e.matmul(pb4(7)[:, h, :], ones_f[:], gU[:, h, :], start=True, stop=False), r=[Bon, BgU], w=[BP[7]])
                        PE(lambda e: e.matmul(pb4(7)[:, h, :], ngU[:, h, :], ones_f[:], start=False, stop=False), r=[Bon, BngU], w=[BP[7]])
                        PE(lambda e: e.matmul(pb4(7)[:, h, :], ident_f[:], mnegC[:], start=False, stop=True), r=[Bidf, Bmn], w=[BP[7]])
                    A(lambda e: e.activation(Ef[:], pb4(7), AF.Exp), r=[BP[7]], w=[BEf])
                    V(lambda e: e.tensor_tensor(Es[:], Ef[:], maskS[:].unsqueeze(1).to_broadcast([128, 4, 128]), ALU.mult), r=[BEf, BmS], w=[BEs])
                    yield

                yield
                subs = [Da(), Db()]
                while subs:
                    for g_ in list(subs):
                        try:
                            next(g_)
                        except StopIteration:
                            subs.remove(g_)
                    yield
                for h in range(4):
                    PE(lambda e: e.transpose(PT[:, h, :], qkn[:, 4 + h, :], ident_b[:]), r=[Bqkn, Bidb], w=[BPT])
                V(lambda e: e.tensor_tensor(keg[:], PT[:, 0:4, :], eG[:].unsqueeze(2).to_broadcast([128, 4, 128]), ALU.mult), r=[BPT, BeG], w=[Bkeg])
                V(lambda e: e.tensor_tensor(kdec[:], PT[:, 0:4, :], ekd[:].unsqueeze(2).to_broadcast([128, 4, 128]), ALU.mult), r=[BPT, Bekd], w=[Bkdec])
                yield
                for h in range(4):
                    PE(lambda e: e.matmul(pb4(4)[:, h, :], qkn[:, 4 + h, :], qkn[:, 4 + h, :], start=True, stop=True), r=[Bqkn], w=[BP[4]])
                    PE(lambda e: e.matmul(pb4(5)[:, h, :], qkn[:, 4 + h, :], qkn[:, h, :], start=True, stop=True), r=[Bqkn], w=[BP[5]])
                V(lambda e: e.tensor_tensor(Xm[:], pb4(4), Es[:], ALU.mult), r=[BP[4], BEs], w=[BXm])
                V(lambda e: e.tensor_tensor(Xm[:], Xm[:], beta[:].unsqueeze(2).to_broadcast([128, 4, 128]), ALU.mult), r=[BXm, Bbeta], w=[BXm])
                V(lambda e: e.scalar_tensor_tensor(QKT[:], pb4(5), float(128 ** -0.5), Ef[:], ALU.mult, ALU.mult), r=[BP[5], BEf], w=[BQKT])
                yield
                for h in range(4):
                    PE(lambda e: e.transpose(pb4(4)[:, h, :], Xm[:, h, :], ident_f[:]), r=[BXm, Bidf], w=[BP[4]])
                V(lambda e: e.tensor_copy(XTm[:], pb4(4)), r=[BP[4]], w=[BXTm])
                V(lambda e: e.tensor_tensor(Pm[:], ident_f[:].unsqueeze(1).to_broadcast([128, 4, 128]), Xm[:], ALU.subtract), r=[Bidf, BXm], w=[BPm])
                for lev in range(4):
                    for h in range(4):
                        PE(lambda e: e.matmul(pb4(5)[:, h, :], XTm[:, h, :], Xm[:, h, :], start=True, stop=True), r=[BXm, BXTm], w=[BP[5]])
                        PE(lambda e: e.matmul(pb4(6)[:, h, :], Xm[:, h, :], XTm[:, h, :], start=True, stop=True), r=[BXm, BXTm], w=[BP[6]])
                    yield
                    V(lambda e: e.tensor_copy(Xm[:], pb4(5)), r=[BP[5]], w=[BXm])
                    A(lambda e: e.copy(XTm[:], pb4(6)), r=[BP[6]], w=[BXTm])
                    for h in range(4):
                        PE(lambda e: e.matmul(pb4(4)[:, h, :], XTm[:, h, :], Pm[:, h, :], start=True, stop=True), r=[BXTm, BPm], w=[BP[4]])
                    yield
                    V(lambda e: e.tensor_tensor(Pm[:], Pm[:], pb4(4), ALU.add), r=[BPm, BP[4]], w=[BPm])
                    yield
                A(lambda e: e.copy(Pb[:], Pm[:]), r=[BPm], w=[BPb])
                yield
                for h in range(4):
                    PE(lambda e: e.matmul(pb4(5)[:, h, :], Pb[:, h, :], vtok[:, h, :], start=True, stop=True), r=[BPb, Bvtok], w=[BP[5]])
                    PE(lambda e: e.matmul(pb4(6)[:, h, :], keg[:, h, :], Pb[:, h, :], start=True, stop=True), r=[BPb, Bkeg], w=[BP[6]])
                V(lambda e: e.tensor_tensor(ub[:], pb4(5), beta[:].unsqueeze(2).to_broadcast([128, 4, 128]), ALU.mult), r=[BP[5], Bbeta], w=[Bub])
                A(lambda e: e.copy(wT[:], pb4(6)), r=[BP[6]], w=[BwT])
                yield
                SfA, BSfA = SfAs[0]; SbA, BSbA = SbAs[0]
                if mode == 'p' and idx == 0:
                    V(lambda e: e.memset(SfA[:], 0.0), w=[BSfA])
                    V(lambda e: e.memset(SbA[:], 0.0), w=[BSbA])
                if mode == 's':
                    K.dma('s', SfAs[0][0][:], delta_s[j, 4 * idx].rearrange("h d e -> d h e"), SfAs[0][1], w=[SfAs[0][1]])
                for q in range(4):
                    qs = slice(32 * q, 32 * q + 32)
                    if mode == 's':
                        SfA, BSfA = SfAs[q % 2]; SbA, BSbA = SbAs[q % 2]
                        if q < 3:
                            K.dma('s', SfAs[(q + 1) % 2][0][:], delta_s[j, 4 * idx + q + 1].rearrange("h d e -> d h e"), SfAs[(q + 1) % 2][1],
                                  w=[SfAs[(q + 1) % 2][1]])
                        A(lambda e: e.copy(SbA[:], SfA[:]), r=[BSfA], w=[BSbA])
                    for h in range(4):
                        PE(lambda e: e.matmul(pb4(4)[qs, h, :], wT[:, h, qs], SbA[:, h, :], start=True, stop=True, tile_position=(0, 32 * q)),
                           r=[BwT, BSbA], w=[BP[4]])
                        PE(lambda e: e.matmul(pb4(7)[qs, h, :], qkn[:, h, qs], SbA[:, h, :], start=True, stop=True, tile_position=(0, 32 * q)),
                           r=[Bqkn, BSbA], w=[BP[7]])
                    V(lambda e: e.tensor_tensor(tS[:], SfA[:], eglb[:, q, :].unsqueeze(2).to_broadcast([128, 4, 128]), ALU.mult), r=[BSfA, Beglb], w=[BtS])
                    yield
                    V(lambda e: e.tensor_tensor(tq[qs], pb4(4)[qs], nbeta[qs].unsqueeze(2).to_broadcast([32, 4, 128]), ALU.mult), r=[BP[4], Bnbeta], w=[Btq])
                    V(lambda e: e.tensor_tensor(vnew[qs], tq[qs], ub[qs], ALU.add), r=[Btq, Bub], w=[Bvnew])
                    for h in range(4):
                        PE(lambda e: e.matmul(pb4(5)[:, h, :], kdec[qs, h, :], vnew[qs, h, :], start=True, stop=True, tile_position=(32 * q, 0)),
                           r=[Bkdec, Bvnew], w=[BP[5]])
                    yield
                    if mode == 'p':
                        V(lambda e: e.tensor_tensor(SbA[:], tS[:], pb4(5), ALU.add), r=[BtS, BP[5]], w=[BSbA])
                    V(lambda e: e.tensor_tensor(SfA[:], tS[:], pb4(5), ALU.add), r=[BtS, BP[5]], w=[BSfA])
                    if mode == 's':
                        K.dma('s', delta_so[j, 4 * idx + q].rearrange("h d e -> d h e"), SfA[:], BSfA, r=[BSfA])
                if lastp:
                    K.dma('s', delta_p[j].rearrange("h d e -> d h e"), SfA[:], BSfA, r=[BSfA])
                yield
                yield
                for h in range(4):
                    PE(lambda e: e.matmul(pb4(6)[:, h, :], QKT[:, h, :], vnew[:, h, :], start=True, stop=True), r=[BQKT, Bvnew], w=[BP[6]])
                V(lambda e: e.tensor_tensor(o2[:], pb4(7), osc[:].unsqueeze(2).to_broadcast([128, 4, 128]), ALU.mult), r=[BP[7], Bosc], w=[Bo2])
                V(lambda e: e.tensor_tensor(oA[:], o2[:], pb4(6), ALU.add), r=[Bo2, BP[6]], w=[BoA])
                yield
                ss4, Bss4 = sc4["ss4"]; rs4, Brs4 = sc4["rs4"]
                V(lambda e: e.tensor_tensor(o2[:], oA[:], oA[:], ALU.mult), r=[BoA], w=[Bo2])
                V(lambda e: e.tensor_reduce(ss4[:], o2[:], AX.X, ALU.add), r=[Bo2], w=[Bss4])
                A(lambda e: e.activation(rs4[:], ss4[:], AF.Ln, bias=EPS, scale=1.0 / 128), r=[Bss4], w=[Brs4])
                A(lambda e: e.activation(rs4[:], rs4[:], AF.Exp, scale=-0.5), r=[Brs4], w=[Brs4])
                V(lambda e: e.tensor_tensor(oA[:], oA[:], rs4[:].unsqueeze(2).to_broadcast([128, 4, 128]), ALU.mult), r=[BoA, Brs4], w=[BoA])
                V(lambda e: e.tensor_tensor(ocb[:, 0:512].rearrange("p (a c) -> p a c", a=4), oA[:], gna[:].rearrange("p (a c) -> p a c", a=4), ALU.mult),
                  r=[BoA, Bgna], w=[Bocb])


            def Sg():
                ridx = idx if mode == 'p' else NT
                V(lambda e: e.tensor_tensor(rt1[:], qkf[:, :, 0:16], rope[:, 0:16].unsqueeze(1).to_broadcast([128, 10, 16]), ALU.mult),
                  r=[Bqkf, Brope], w=[Brt1])
                V(lambda e: e.tensor_tensor(rt2[:, :, 0:8], qkf[:, :, 8:16], rope[:, 16:24].unsqueeze(1).to_broadcast([128, 10, 8]), ALU.mult),
                  r=[Bqkf, Brope], w=[Brt2])
                V(lambda e: e.tensor_tensor(rt2[:, :, 8:16], qkf[:, :, 0:8], rope[:, 24:32].unsqueeze(1).to_broadcast([128, 10, 8]), ALU.mult),
                  r=[Bqkf, Brope], w=[Brt2])
                V(lambda e: e.tensor_tensor(qkf[:, :, 0:16], rt1[:], rt2[:], ALU.add), r=[Brt1, Brt2], w=[Bqkf])
                yield
                cur = idx % 2 if mode == 'p' else 0
                prv = 1 - cur
                kc_t, Bkc_t = kcur[0]
                kT_cur, BkT_cur = kTs[cur]; kT_prv, BkT_prv = kTs[prv]
                v_cur, Bv_cur = v16[cur]; v_prv, Bv_prv = v16[prv]
                A(lambda e: e.copy(qb16[:], qkf[:, 0:8, :].rearrange("p (a c) d -> p c a d", a=2)), r=[Bqkf], w=[Bqb16])
                A(lambda e: e.copy(kc_t[:], qkf[:, 8:10, :].rearrange("p a c -> p (a c)")), r=[Bqkf], w=[Bkc_t])
                A(lambda e: e.copy(v_cur[:], vf[:]), r=[Bvf], w=[Bv_cur])
                if lastp:
                    K.dma('s', wk_p[j], qkf[:, 8:10, :].rearrange("p a c -> p (a c)"), Bqkf, r=[Bqkf])
                    K.dma('s', wv_p[j], vf[:], Bvf, r=[Bvf])
                if mode == 's':
                    for q in range(4):
                        sq_ = 4 * idx + q
                        K.dma('s', wk_so[j, sq_, 120:128, :], qkf[32 * q:32 * q + 8, 8:10, :].rearrange("p a c -> p (a c)"), Bqkf, r=[Bqkf])
                        K.dma('s', wv_so[j, sq_, 120:128, :], vf[32 * q:32 * q + 8, :], Bvf, r=[Bvf])
                        K.dma('s', wk_so[j, sq_, 0:120, :], wk_s[j, sq_, 8:128, :], Bd2d)
                        K.dma('s', wv_so[j, sq_, 0:120, :], wv_s[j, sq_, 8:128, :], Bd2d)
                        K.dma('s', kcf[:, q, :], wk_s[j, sq_], Bkcf, w=[Bkcf])
                        K.dma('s', vcf[:, q, :], wv_s[j, sq_], Bvcf, w=[Bvcf])
                    A(lambda e: e.copy(kc16[:], kcf[:]), r=[Bkcf], w=[Bkc16])
                    A(lambda e: e.copy(vc16[:], vcf[:]), r=[Bvcf], w=[Bvc16])
                    transposes(4, lambda i: kc16[:, i, :], [Bkc16], kTc, BkTc)
                yield
                transposes(4, lambda i: qb16[:, i, :, :].rearrange("p a d -> p (a d)"), [Bqb16], qT, BqT)
                PE(lambda e: e.transpose(PT[:, 0, :], kc_t[:], ident_b[:]), r=[Bkc_t, Bidb], w=[BPT])
                V(lambda e: e.tensor_copy(kT_cur[:], PT[:, 0, :]), r=[BPT], w=[BkT_cur])
                mi = 2 if mode == 's' else (1 if idx == 0 else 0)
                mx4, Bmx4 = sc4["mx4"]; negm, Bnegm = sc4["negm"]; rsum, Brsum = sc4["rsum"]; es4, Bes4 = sc4["es4"]; den4, Bden4 = sc4["den4"]
                for gi in range(2):
                    yield
                    ks = slice(64 * gi, 64 * gi + 64)
                    for c in range(4):
                        b = 1 + c // 2
                        o0 = (c % 2) * 256
                        if mode == 'p':
                            PE(lambda e: e.matmul(pb(b)[:, o0:o0 + 128], qT[ks, c, :], kT_prv[ks, :], start=True, stop=True, tile_position=(64 * gi, 0)),
                               r=[BqT, BkT_prv], w=[BP[b]])
                        else:
                            for q in range(4):
                                qs = slice(32 * q, 32 * q + 32)
                                PE(lambda e: e.matmul(pb(b)[qs, o0:o0 + 128], qT[ks, c, qs], kTc[ks, q, :], start=True, stop=True,
                                                      tile_position=(64 * gi, 32 * q)), r=[BqT, BkTc], w=[BP[b]])
                        PE(lambda e: e.matmul(pb(b)[:, o0 + 128:o0 + 256], qT[ks, c, :], kT_cur[ks, :], start=True, stop=True, tile_position=(64 * gi, 0)),
                           r=[BqT, BkT_cur], w=[BP[b]])
                    yield
                    for b2 in range(2):
                        V(lambda e: e.scalar_tensor_tensor(sm[:, 2 * b2:2 * b2 + 2, :], pb(1 + b2).rearrange("p (a c) -> p a c", a=2), 0.125,
                                                           swam[:, mi, :].unsqueeze(1).to_broadcast([128, 2, 256]), ALU.mult, ALU.add),
                          r=[BP[1 + b2], Bswam], w=[Bsm])
                    V(lambda e: e.tensor_reduce(mx4[:], sm[:], AX.X, ALU.max), r=[Bsm], w=[Bmx4])
                    V(lambda e: e.tensor_tensor(mx4[:], mx4[:], sinkbc[:, 4 * gi:4 * gi + 4], ALU.max), r=[Bmx4, Bsinkbc], w=[Bmx4])
                    V(lambda e: e.tensor_scalar_mul(negm[:], mx4[:], -1.0), r=[Bmx4], w=[Bnegm])
                    yield
                    for c in range(4):
                        A(lambda e: e.activation(pr[:, c, :], sm[:, c, :], AF.Exp, bias=negm[:, c:c + 1], accum_out=rsum[:, c:c + 1]),
                          r=[Bsm, Bnegm], w=[Bpr, Brsum])
                    V(lambda e: e.tensor_tensor(es4[:], sinkbc[:, 4 * gi:4 * gi + 4], mx4[:], ALU.subtract), r=[Bsinkbc, Bmx4], w=[Bes4])
                    A(lambda e: e.activation(es4[:], es4[:], AF.Exp), r=[Bes4], w=[Bes4])
                    V(lambda e: e.tensor_tensor(den4[:], rsum[:], es4[:], ALU.add), r=[Brsum, Bes4], w=[Bden4])
                    V(lambda e: e.reciprocal(rden[:, 4 * gi:4 * gi + 4], den4[:]), r=[Bden4], w=[Brden])
                    yield
                    for c in range(4):
                        PE(lambda e: e.transpose(PT[:, 2 * c, :], pr[:, c, 0:128], ident_b[:]), r=[Bpr, Bidb], w=[BPT])
                        PE(lambda e: e.transpose(PT[:, 2 * c + 1, :], pr[:, c, 128:256], ident_b[:]), r=[Bpr, Bidb], w=[BPT])
                    V(lambda e: e.tensor_copy(prT[:], PT[:]), r=[BPT], w=[BprT])
                    yield
                    for c in range(4):
                        h = 4 * gi + c
                        hs = slice(64 * h, 64 * h + 64)
                        if mode == 'p':
                            PE(lambda e: e.matmul(pb(3)[:, hs], prT[:, 2 * c, :], v_prv[:, ks], start=True, stop=False), r=[BprT, Bv_prv], w=[BP[3]])
                        else:
                            for q in range(4):
                                qs = slice(32 * q, 32 * q + 32)
                                PE(lambda e: e.matmul(pb(3)[qs, hs], prT[:, 2 * c, qs], vc16[:, q, ks], start=True, stop=False, tile_position=(0, 32 * q)),
                                   r=[BprT, Bvc16], w=[BP[3]])
                        PE(lambda e: e.matmul(pb(3)[:, hs], prT[:, 2 * c + 1, :], v_cur[:, ks], start=False, stop=True), r=[BprT, Bv_cur], w=[BP[3]])
                yield
                V(lambda e: e.tensor_tensor(ob[:], pb(3).rearrange("p (a c) -> p a c", a=8), rden[:].unsqueeze(2).to_broadcast([128, 8, 64]), ALU.mult),
                  r=[BP[3], Brden], w=[Bob])
                V(lambda e: e.tensor_tensor(ocb[:, 512:1024], ob[:].rearrange("p a c -> p (a c)"), sgb[:], ALU.mult), r=[Bob, Bsgb], w=[Bocb])

            def Eg():
                yield from out_proj_post(l, mode, idx, S)

            return Pre(), Pg(), [Dg(), Sg()], Eg()

          return setup, even_tile

        def make_odd(sbl):
          win, Bwin = sbl("win", [128, 8, ODD_IN], BF16)
          omlbc, Bomlbc = sbl("omlbc", [128, D]); ncbc, Bncbc = sbl("ncbc", [128, 128])
          kk, Bkk = sbl("kk", [128, D]); logf, Blogf = sbl("logf", [128, D])
          Gsb, BGsb = sbl("Gsb", [128, D])
          tmpD, BtmpD = sbl("tmpD", [128, D]); eGd, BeGd = sbl("eGd", [128, D])
          ktl, Bktl = sbl("ktl", [128, D], BF16); khat, Bkhat = sbl("khat", [128, D], BF16); qtl, Bqtl = sbl("qtl", [128, D], BF16)
          vtl, Bvtl = sbl("vtl", [128, D], BF16); gnc, Bgnc = sbl("gnc", [128, D])
          qTo, BqTo = sbl("qTo", [128, 8, 128], BF16); kTo, BkTo = sbl("kTo", [128, 8, 128], BF16)
          ATo, BATo = sbl("ATo", [128, 8, 128], BF16)
          eglo, Beglo = sbl("eglo", [128, 8, 4]); tSC, BtSC = sbl("tSC", [128, 8, 128])
          SfCs = [sbl("SfC%d" % i, [128, 8, 128]) for i in range(2)]; SbCs = [sbl("SbC%d" % i, [128, 8, 128], BF16) for i in range(2)]
          oC, BoC = sbl("oC", [128, 8, 128]); ss8, Bss8 = sbl("ss8", [128, 8]); rs8, Brs8 = sbl("rs8", [128, 8])

          def setup(j):
                load_w(win, Bwin, w_in_odd[j], D, ODD_IN)
                load_w(wout, Bwout, w_out_odd[j], D, D)
                K.dma('s', ncbc[:], norm_c[j:j + 1, :].partition_broadcast(128), Bncbc, w=[Bncbc])
                if j == 0:
                    V(lambda e: e.memset(omlbc[:], 1.0), w=[Bomlbc])
                else:
                    K.dma('s', tmpD[:], lb_raw[0:1, :].partition_broadcast(128), BtmpD, w=[BtmpD])
                    K.dma('s', eGd[:], lb_raw[1:2, :].partition_broadcast(128), BeGd, w=[BeGd])
                    V(lambda e: e.tensor_tensor(omlbc[:], tmpD[:], eGd[:], ALU.subtract), r=[BtmpD, BeGd], w=[Bomlbc])
                    A(lambda e: e.activation(omlbc[:], omlbc[:], AF.Sigmoid), r=[Bomlbc], w=[Bomlbc])

          def odd_tile(l, mode, idx, S, S2):
            j = l // 2
            vcol = 0 if mode == 'p' else 1
            lastp = (mode == 'p' and idx == NT - 1)
            xT, BxT = S["xT"]

            def Pre():
                yield from load_tile(l, mode, idx, S)

            def Pg():

                def proj(c0, b0):
                    for n in range(2):
                        for kc in range(8):
                            PE(lambda e: e.matmul(pb(b0 + n), xT[:, kc, :], win[:, kc, c0 + n * 512:c0 + (n + 1) * 512], start=(kc == 0), stop=(kc == 7)),
                               r=[Bwin, BxT], w=[BP[b0 + n]])

                H = lambda t_, n: t_[:, n * 512:(n + 1) * 512]
                yield
                proj(1024, 5)
                for n in range(2):
                    A(lambda e: e.activation(H(kk, n), pb(5 + n), AF.Sigmoid, scale=-1.0), r=[BP[5 + n]], w=[Bkk])
                V(lambda e: e.tensor_tensor(kk[:], kk[:], omlbc[:], ALU.mult), r=[Bkk, Bomlbc], w=[Bkk])
                A(lambda e: e.activation(logf[:], kk[:], AF.Ln, bias=1.0, scale=-1.0), r=[Bkk], w=[Blogf])
                if mode == 's':
                    V(lambda e: e.tensor_scalar_mul(logf[:], logf[:], valid[:, 1:2]), r=[Blogf, Bval], w=[Blogf])
                yield
                for n in range(2):
                    PE(lambda e: e.matmul(pb(1 + n), maskC[:], H(logf, n), start=True, stop=True), r=[BmC, Blogf], w=[BP[1 + n]])
                    PE(lambda e: e.matmul(pb(3 + n), bones[:], H(logf, n), start=True, stop=True), r=[Bbo, Blogf], w=[BP[3 + n]])
                for h in range(8):
                    PE(lambda e: e.matmul(pb(7)[:, 4 * h:4 * h + 4], logf[:, h * 128:(h + 1) * 128], ind[:], start=True, stop=True), r=[Blogf, Bind], w=[BP[7]])
                A(lambda e: e.activation(eglo[:].rearrange("p a c -> p (a c)"), pb(7)[:, 0:32], AF.Exp), r=[BP[7]], w=[Beglo])
                for n in range(2):
                    V(lambda e: e.tensor_copy(H(Gsb, n), pb(1 + n)), r=[BP[1 + n]], w=[BGsb])
                    A(lambda e: e.activation(H(tmpD, n), pb(1 + n), AF.Exp, scale=-1.0), r=[BP[1 + n]], w=[BtmpD])
                    A(lambda e: e.activation(H(eGd, n), pb(1 + n), AF.Exp), r=[BP[1 + n]], w=[BeGd])
                V(lambda e: e.tensor_tensor(ktl[:], kk[:], tmpD[:], ALU.mult), r=[Bkk, BtmpD], w=[Bktl])
                for n in range(2):
                    V(lambda e: e.tensor_tensor(H(tmpD, n), pb(3 + n), H(Gsb, n), ALU.subtract), r=[BP[3 + n], BGsb], w=[BtmpD])
                A(lambda e: e.activation(tmpD[:], tmpD[:], AF.Exp), r=[BtmpD], w=[BtmpD])
                V(lambda e: e.tensor_tensor(khat[:], kk[:], tmpD[:], ALU.mult), r=[Bkk, BtmpD], w=[Bkhat])
                yield
                proj(2048, 5)
                for n in range(2):
                    if mode == 's':
                        V(lambda e: e.tensor_scalar_mul(H(vtl, n), pb(5 + n), valid[:, 1:2]), r=[BP[5 + n], Bval], w=[Bvtl])
                    else:
                        V(lambda e: e.tensor_copy(H(vtl, n), pb(5 + n)), r=[BP[5 + n]], w=[Bvtl])
                yield
                proj(0, 1)
                for n in range(2):
                    A(lambda e: e.activation(H(tmpD, n), pb(1 + n), AF.Silu), r=[BP[1 + n]], w=[BtmpD])
                V(lambda e: e.scalar_tensor_tensor(qtl[:], tmpD[:], float(128 ** -0.5), eGd[:], ALU.mult, ALU.mult), r=[BtmpD, BeGd], w=[Bqtl])
                yield
                proj(3072, 3)
                for n in range(2):
                    A(lambda e: e.activation(H(gnc, n), pb(3 + n), AF.Silu), r=[BP[3 + n]], w=[Bgnc])
                V(lambda e: e.tensor_tensor(gnc[:].rearrange("p (a c) -> p a c", a=8), gnc[:].rearrange("p (a c) -> p a c", a=8),
                                            ncbc[:].unsqueeze(1).to_broadcast([128, 8, 128]), ALU.mult), r=[Bgnc, Bncbc], w=[Bgnc])
                yield
                transposes(8, lambda i: qtl[:, i * 128:(i + 1) * 128], [Bqtl], qTo, BqTo)
                yield
                transposes(8, lambda i: ktl[:, i * 128:(i + 1) * 128], [Bktl], kTo, BkTo)
                yield
                for h in range(8):
                    b = 5 + h // 4
                    PE(lambda e: e.matmul(pb4(b)[:, h % 4, :], kTo[:, h, :], qTo[:, h, :], start=True, stop=True), r=[BkTo, BqTo], w=[BP[b]])
                for n in range(2):
                    V(lambda e: e.tensor_tensor(ATo[:, 4 * n:4 * n + 4, :], pb4(5 + n), maskC[:].unsqueeze(1).to_broadcast([128, 4, 128]), ALU.mult),
                      r=[BP[5 + n], BmC], w=[BATo])

            def Cg():
                SfC, BSfC = SfCs[0]; SbC, BSbC = SbCs[0]
                if mode == 'p' and idx == 0:
                    V(lambda e: e.memset(SfC[:], 0.0), w=[BSfC])
                    V(lambda e: e.memset(SbC[:], 0.0), w=[BSbC])
                if mode == 's':
                    K.dma('s', SfCs[0][0][:], hg_s[j, 4 * idx].rearrange("h d e -> d h e"), SfCs[0][1], w=[SfCs[0][1]])
                for q in range(4):
                    qs = slice(32 * q, 32 * q + 32)
                    if mode == 's':
                        SfC, BSfC = SfCs[q % 2]; SbC, BSbC = SbCs[q % 2]
                        if q < 3:
                            K.dma('s', SfCs[(q + 1) % 2][0][:], hg_s[j, 4 * idx + q + 1].rearrange("h d e -> d h e"), SfCs[(q + 1) % 2][1],
                                  w=[SfCs[(q + 1) % 2][1]])
                        A(lambda e: e.copy(SbC[:], SfC[:]), r=[BSfC], w=[BSbC])
                    for h in range(8):
                        b = 1 + h // 4
                        PE(lambda e: e.matmul(pb4(b)[qs, h % 4, :], qTo[:, h, qs], SbC[:, h, :], start=True, stop=True, tile_position=(0, 32 * q)),
                           r=[BqTo, BSbC], w=[BP[b]])
                    for h in range(8):
                        b = 3 + h // 4
                        PE(lambda e: e.matmul(pb4(b)[:, h % 4, :], khat[qs, h * 128:(h + 1) * 128], vtl[qs, h * 128:(h + 1) * 128], start=True, stop=True,
                                              tile_position=(32 * q, 0)), r=[Bkhat, Bvtl], w=[BP[b]])
                    V(lambda e: e.tensor_tensor(tSC[:], SfC[:], eglo[:, :, q:q + 1].to_broadcast([128, 8, 128]), ALU.mult), r=[BSfC, Beglo], w=[BtSC])
                    yield
                    if mode == 'p':
                        for n in range(2):
                            V(lambda e: e.tensor_tensor(SbC[:, 4 * n:4 * n + 4, :], tSC[:, 4 * n:4 * n + 4, :], pb4(3 + n), ALU.add), r=[BtSC, BP[3 + n]], w=[BSbC])
                    for n in range(2):
                        V(lambda e: e.tensor_tensor(SfC[:, 4 * n:4 * n + 4, :], tSC[:, 4 * n:4 * n + 4, :], pb4(3 + n), ALU.add), r=[BtSC, BP[3 + n]], w=[BSfC])
                    if mode == 's':
                        K.dma('s', hg_so[j, 4 * idx + q].rearrange("h d e -> d h e"), SfC[:], BSfC, r=[BSfC])
                if lastp:
                    K.dma('s', hg_p[j].rearrange("h d e -> d h e"), SfC[:], BSfC, r=[BSfC])
                yield
                for n in range(2):
                    A(lambda e: e.copy(oC[:, 4 * n:4 * n + 4, :], pb4(1 + n)), r=[BP[1 + n]], w=[BoC])
                for h in range(8):
                    b = 5 + h // 4
                    PE(lambda e: e.matmul(pb4(b)[:, h % 4, :], ATo[:, h, :], vtl[:, h * 128:(h + 1) * 128], start=True, stop=True), r=[BATo, Bvtl], w=[BP[b]])
                for n in range(2):
                    V(lambda e: e.tensor_tensor(oC[:, 4 * n:4 * n + 4, :], oC[:, 4 * n:4 * n + 4, :], pb4(5 + n), ALU.add), r=[BoC, BP[5 + n]], w=[BoC])
                yield
                t3 = tmpD[:].rearrange("p (a c) -> p a c", a=8)
                V(lambda e: e.tensor_tensor(t3, oC[:], oC[:], ALU.mult), r=[BoC], w=[BtmpD])
                V(lambda e: e.tensor_reduce(ss8[:], t3, AX.X, ALU.add), r=[BtmpD], w=[Bss8])
                A(lambda e: e.activation(rs8[:], ss8[:], AF.Ln, bias=EPS, scale=1.0 / 128), r=[Bss8], w=[Brs8])
                A(lambda e: e.activation(rs8[:], rs8[:], AF.Exp, scale=-0.5), r=[Brs8], w=[Brs8])
                V(lambda e: e.tensor_tensor(oC[:], oC[:], rs8[:].unsqueeze(2).to_broadcast([128, 8, 128]), ALU.mult), r=[BoC, Brs8], w=[BoC])
                V(lambda e: e.tensor_tensor(ocb[:].rearrange("p (a c) -> p a c", a=8), oC[:], gnc[:].rearrange("p (a c) -> p a c", a=8), ALU.mult),
                  r=[BoC, Bgnc], w=[Bocb])

            def Eg():
                yield from out_proj_post(l, mode, idx, S)

            return Pre(), Pg(), [Cg()], Eg()

          return setup, odd_tile

        for l in range(NL):
            j = l // 2
            with ExitStack() as les:
                def sbl(name, shape, dt=F32, _les=les, _l=l):
                    t = _les.enter_context(nc.sbuf_tensor("%s_L%d" % (name, _l), list(shape), dt))
                    return t, Buf("%s_L%d" % (name, _l))
                K.dma('s', gbc[:], ln_g[l:l + 1, :].partition_broadcast(128), Bgbc, w=[Bgbc])
                K.dma('s', bbc[:], ln_b[l:l + 1, :].partition_broadcast(128), Bbbc, w=[Bbbc])
                load_w(wgate, Bwgate, w_pgate[l], D, D)
                load_w(wpp, Bwpp, w_pproj[l], PLE, D)
                setup, tile_fn = (make_even if l % 2 == 0 else make_odd)(sbl)
                setup(j)
                tiles = [('p', t) for t in range(NT)] + [('s', st) for st in range(NSEQ // 4)]
                parts = [tile_fn(l, mode_, idx_, SL[ti % 2], SL[(ti + 1) % 2]) for ti, (mode_, idx_) in enumerate(tiles)]
                prefetch = (l % 2 == 1)
                if prefetch:
                    interleave(parts[0][0])
                prevE = None

                def chain2(a_, b_):
                    yield from a_
                    yield from b_

                def take(gen_, n_):
                    for _ in range(n_):
                        try:
                            next(gen_)
                        except StopIteration:
                            return
                        yield
                early = [False] * len(tiles)
                for ti in range(len(tiles)):
                    Pre_, Pg, chains, Eg = parts[ti]
                    if prefetch or early[ti]:
                        interleave(prevE, Pg, seq=(SEQUENTIAL in (1, 2)))
                    else:
                        interleave(prevE, chain2(Pre_, Pg), seq=(SEQUENTIAL in (1, 2)))
                    if len(chains) == 2:
                        D_, S_ = chains
                        aliveD = aliveS = True
                        while aliveS:
                            for _ in range(2):
                                if aliveD:
                                    try:
                                        next(D_)
                                    except StopIteration:
                                        aliveD = False
                            try:
                                next(S_)
                            except StopIteration:
                                aliveS = False
                        X_ = None
                        if ti + 1 < len(tiles) and tiles[ti + 1][0] == 'p' and tiles[ti + 1][1] < NT - 1 and not SEQUENTIAL:
                            X_ = chain2(parts[ti + 1][0], take(parts[ti + 1][1], 4))
                            early[ti + 1] = True
                        interleave(D_ if aliveD else None, X_)
                    else:
                        nxt = None
                        if prefetch and ti + 1 < len(tiles):
                            nxt = chain2(parts[ti + 1][0], take(parts[ti + 1][1], 2)) if not SEQUENTIAL else parts[ti + 1][0]
                        interleave(chains[0], nxt, seq=(SEQUENTIAL in (1, 3)))
                    prevE = Eg
                interleave(prevE)
                K.barrier()
        K.finish_all('s')
    return nc


_CACHE = {}


def run_cores(per_core, NT, NL=4):
    key = (NT, NL)
    if key not in _CACHE:
        _CACHE[key] = build(NT, NL)
    nc = _CACHE[key]
    res = run_bass_kernel_spmd(nc, per_core, core_ids=list(range(len(per_core))))
    return res.results


def kernel(x_prompt, x_sample, state_conv_a, state_delta_a, cache_win_k, cache_win_v, state_hgrn_c,
           p_prompt, p_sample, w_in_even, conv_w_a, a_log, dt_bias, norm_a, sinks_b, w_out_even,
           w_in_odd, lb_raw, norm_c, w_out_odd, ln_g, ln_b, w_ple_proj, w_ple_gate):
    f = lambda a: np.ascontiguousarray(np.asarray(a, dtype=np.float32))
    x_prompt = f(x_prompt); x_sample = f(x_sample)
    NB, L = x_prompt.shape[:2]
    NT = L // 128
    NL = ln_g.shape[0]
    NE = (NL + 1) // 2
    consts = make_consts(NT)
    shared = {
        "w_in_even": f(w_in_even), "conv_w": f(conv_w_a), "a_log": f(a_log), "dt_bias": f(dt_bias), "norm_a": f(norm_a),
        "sinks": f(sinks_b), "w_out_even": f(w_out_even), "w_in_odd": f(w_in_odd), "lb_raw": f(lb_raw), "norm_c": f(norm_c),
        "w_out_odd": f(w_out_odd), "ln_g": f(ln_g), "ln_b": f(ln_b), "w_pproj": f(w_ple_proj), "w_pgate": f(w_ple_gate),
        "c_ident": consts["ident"], "c_maskC": consts["maskC"], "c_maskS": consts["maskS"], "c_bones": consts["bones"],
        "c_mnegC": consts["mnegC"], "c_ones": consts["ones"], "c_ind": consts["ind"], "c_valid": consts["valid"],
        "c_swam": consts["swam"], "c_rope": consts["rope"],
    }
    p_prompt = f(p_prompt); p_sample = f(p_sample)
    state_conv_a = f(state_conv_a); state_delta_a = f(state_delta_a)
    cache_win_k = f(cache_win_k); cache_win_v = f(cache_win_v); state_hgrn_c = f(state_hgrn_c)
    ncores = 8
    per_core = []
    for c in range(ncores):
        sq = c % NB
        s0, s1 = c * NSEQ, (c + 1) * NSEQ
        m = dict(shared)
        m["xp"] = f(x_prompt[sq]); m["pp"] = f(p_prompt[:, sq])
        m["xs"] = f(x_sample[s0:s1]); m["ps"] = f(p_sample[:, s0:s1])
        m["conv_s"] = f(state_conv_a[:, s0:s1]); m["delta_s"] = f(state_delta_a[:, s0:s1])
        m["wk_s"] = f(cache_win_k[:, s0:s1].reshape(NE, NSEQ, 128, 128)); m["wv_s"] = f(cache_win_v[:, s0:s1].reshape(NE, NSEQ, 128, 128))
        m["hg_s"] = f(state_hgrn_c[:, s0:s1])
        per_core.append(m)
    res = run_cores(per_core, NT, NL)
    cat = lambda k, ax: np.concatenate([res[c][k] for c in range(ncores)], axis=ax)
    stk = lambda k: np.stack([res[c][k] for c in range(NB)], axis=1)
    y_p = np.stack([res[c]["yp"] for c in range(NB)], axis=0)
    y_s = cat("ys", 0)
    NS = ncores * NSEQ
    outs = (y_p, y_s, stk("conv_p"), cat("conv_so", 1), stk("delta_p"), cat("delta_so", 1),
            stk("wk_p").reshape(NE, NB, 128, 2, 64), cat("wk_so", 1).reshape(NE, NS, 128, 2, 64),
            stk("wv_p").reshape(NE, NB, 128, 2, 64), cat("wv_so", 1).reshape(NE, NS, 128, 2, 64),
            stk("hg_p"), cat("hg_so", 1))
    return tuple(np.ascontiguousarray(o.astype(np.float32)) for o in outs)
```
